# Optimizing a Trainium2 kernel written in Bass

```python
import jax, jax.numpy as jnp
from jax import lax
import numpy as np

D_MODEL = 1024
BATCH = 8
SEQ = 4096
DEPTH = 2

CTX_LEN = 256
GRID_W = 64
EPS = 1e-6
N_MOD = 9
D_FF = 2816

FNET_GROUPS = 4
FNET_GROUP_DIM = 128
FNET_DIM = FNET_GROUPS * FNET_GROUP_DIM
NA_HEADS = 8
NA_HEAD_DIM = 64
NA_DIM = NA_HEADS * NA_HEAD_DIM
NA_KH = 8
NA_KW = 16
AB_IN_DIM = FNET_DIM + 3 * NA_DIM
AB_OUT_DIM = FNET_DIM + NA_DIM
MLA_HEADS = 8
MLA_NOPE = 128
MLA_ROPE = 64
MLA_V = 128
MLA_Q_RANK = 384
MLA_KV_RANK = 128
MLA_IN_DIM = MLA_Q_RANK + MLA_KV_RANK + MLA_ROPE
ROPE_BASE = 10000.0
Q_BLOCK = 128

N_EVEN = (DEPTH + 1) // 2
N_ODD = DEPTH // 2

kernel_name = "hybrid_fnet_natten_mla_macaron_dit"


def rmsnorm(x, g):
    xf = x.astype(jnp.float32)
    y = xf * lax.rsqrt(jnp.mean(xf * xf, axis=-1, keepdims=True) + EPS)
    return (y * g.astype(jnp.float32)).astype(x.dtype)


def swiglu(h, w_gate, w_up, w_down):
    return (jax.nn.silu(h @ w_gate) * (h @ w_up)) @ w_down


def half_ffn(x, shift, scale, gate, g, w_gate, w_up, w_down):
    h = rmsnorm(x, g) * (1 + scale) + shift
    return x + 0.5 * gate * swiglu(h, w_gate, w_up, w_down)


def axial_rope_tables(n_tok, rot_dim):
    pos = jnp.arange(n_tok)
    row = (pos // GRID_W).astype(jnp.float32)
    col = (pos % GRID_W).astype(jnp.float32)
    half = rot_dim // 2
    inv = ROPE_BASE ** (-jnp.arange(0, half, 2, dtype=jnp.float32) / half)
    ang_r = row[:, None] * inv[None]
    ang_c = col[:, None] * inv[None]
    ang = jnp.concatenate([ang_r, ang_r, ang_c, ang_c], axis=-1)
    return jnp.cos(ang), jnp.sin(ang)


def apply_axial_rope(x, cos, sin):
    xf = x.astype(jnp.float32)
    a, b, cc, d = jnp.split(xf, 4, axis=-1)
    rot = jnp.concatenate([-b, a, -d, cc], axis=-1)
    return (xf * cos + rot * sin).astype(x.dtype)


def fourier_mix(u):
    b, n, _ = u.shape
    ug = u.astype(jnp.float32).reshape(b, n, FNET_GROUPS, FNET_GROUP_DIM)
    f = jnp.fft.fft2(ug, axes=(1, 3), norm="ortho").real
    return f.reshape(b, n, FNET_DIM).astype(u.dtype)


def dense_attention(q, k, v):
    s = jnp.einsum('bqhd,bkhd->bhqk', q, k, preferred_element_type=jnp.float32) * (q.shape[-1] ** -0.5)
    p = jax.nn.softmax(s, axis=-1).astype(v.dtype)
    return jnp.einsum('bhqk,bkhd->bqhd', p, v)


def neighbourhood_attention(q, k, v, k_ctx, v_ctx, rpb):
    b, s, h, dh = q.shape
    rows = s // GRID_W
    kh = min(NA_KH, rows)
    kw = NA_KW
    scale = dh ** -0.5
    qg = q.reshape(b, rows, GRID_W, h, dh)
    kg = k.reshape(b, rows, GRID_W, h, dh)
    vg = v.reshape(b, rows, GRID_W, h, dh)
    cols = jnp.arange(GRID_W)
    col_start = jnp.clip(cols - kw // 2, 0, GRID_W - kw)
    col_idx = col_start[:, None] + jnp.arange(kw)[None, :]
    col_off = col_idx - cols[:, None] + (NA_KW - 1)

    def one_row(r):
        r0 = jnp.clip(r - kh // 2, 0, rows - kh)
        row_off = r0 + jnp.arange(kh) - r + (NA_KH - 1)
        k_band = lax.dynamic_slice_in_dim(kg, r0, kh, axis=1)
        v_band = lax.dynamic_slice_in_dim(vg, r0, kh, axis=1)
        k_win = k_band[:, :, col_idx]
        v_win = v_band[:, :, col_idx]
        qr = lax.dynamic_index_in_dim(qg, r, axis=1, keepdims=False)
        bias = rpb[:, row_off[:, None, None], col_off[None, :, :]]
        bias = jnp.transpose(bias, (0, 2, 1, 3)).reshape(h, GRID_W, kh * kw).astype(jnp.float32)
        s_win = jnp.einsum('bqhd,biqjhd->bhqij', qr, k_win, preferred_element_type=jnp.float32)
        s_win = s_win.reshape(b, h, GRID_W, kh * kw) * scale + bias[None]
        s_ctx = jnp.einsum('bqhd,blhd->bhql', qr, k_ctx, preferred_element_type=jnp.float32) * scale
        p = jax.nn.softmax(jnp.concatenate([s_win, s_ctx], axis=-1), axis=-1).astype(v.dtype)
        p_win = p[..., :kh * kw].reshape(b, h, GRID_W, kh, kw)
        p_ctx = p[..., kh * kw:]
        return (jnp.einsum('bhqij,biqjhd->bqhd', p_win, v_win)
                + jnp.einsum('bhql,blhd->bqhd', p_ctx, v_ctx))

    o = lax.map(one_row, jnp.arange(rows))
    return jnp.transpose(o, (1, 0, 2, 3, 4)).reshape(b, s, h * dh)


def fourier_na_mixer(h_lat, h_ctx, w_in, rpb, w_out, need_ctx_out):
    def split(z):
        u = z[..., :FNET_DIM]
        qkv = z[..., FNET_DIM:].reshape(z.shape[0], z.shape[1], 3, NA_HEADS, NA_HEAD_DIM)
        return u, qkv[:, :, 0], qkv[:, :, 1], qkv[:, :, 2]

    u_l, q_l, k_l, v_l = split(h_lat @ w_in)
    u_c, q_c, k_c, v_c = split(h_ctx @ w_in)
    a_l = fourier_mix(u_l)
    b_l = neighbourhood_attention(q_l, k_l, v_l, k_c, v_c, rpb)
    y_lat = jnp.concatenate([a_l, b_l], axis=-1) @ w_out
    if not need_ctx_out:
        return y_lat, None
    a_c = fourier_mix(u_c)
    b_c = dense_attention(q_c, k_c, v_c).reshape(h_ctx.shape[0], h_ctx.shape[1], NA_DIM)
    y_ctx = jnp.concatenate([a_c, b_c], axis=-1) @ w_out
    return y_lat, y_ctx


def mla_block_attention(q_nope, q_rope, k_nope, k_rope, v):
    b, n, h, _ = q_nope.shape
    nb = n // Q_BLOCK
    scale = (MLA_NOPE + MLA_ROPE) ** -0.5
    qn = q_nope.reshape(b, nb, Q_BLOCK, h, MLA_NOPE).swapaxes(0, 1)
    qr = q_rope.reshape(b, nb, Q_BLOCK, h, MLA_ROPE).swapaxes(0, 1)

    def one_block(args):
        qn_b, qr_b = args
        s = (jnp.einsum('bqhd,bkhd->bhqk', qn_b, k_nope, preferred_element_type=jnp.float32)
             + jnp.einsum('bqhr,bkr->bhqk', qr_b, k_rope, preferred_element_type=jnp.float32)) * scale
        p = jax.nn.softmax(s, axis=-1).astype(v.dtype)
        return jnp.einsum('bhqk,bkhd->bqhd', p, v)

    o = lax.map(one_block, (qn, qr))
    return o.swapaxes(0, 1).reshape(b, n, h * MLA_V)


def mla_mixer(h_lat, h_ctx, w_in, g_q, g_kv, w_uq, w_uk, w_uv, w_o, cos, sin, need_ctx_out):
    def project(hh):
        bb, nn = hh.shape[0], hh.shape[1]
        z = hh @ w_in
        c_q = rmsnorm(z[..., :MLA_Q_RANK], g_q)
        c_kv = rmsnorm(z[..., MLA_Q_RANK:MLA_Q_RANK + MLA_KV_RANK], g_kv)
        k_rope = z[..., MLA_Q_RANK + MLA_KV_RANK:]
        q = (c_q @ w_uq).reshape(bb, nn, MLA_HEADS, MLA_NOPE + MLA_ROPE)
        k_nope = (c_kv @ w_uk).reshape(bb, nn, MLA_HEADS, MLA_NOPE)
        v = (c_kv @ w_uv).reshape(bb, nn, MLA_HEADS, MLA_V)
        return q[..., :MLA_NOPE], q[..., MLA_NOPE:], k_nope, k_rope, v

    qn_l, qr_l, kn_l, kr_l, v_l = project(h_lat)
    qn_c, qr_c, kn_c, kr_c, v_c = project(h_ctx)
    qr_l = apply_axial_rope(qr_l, cos[:, None, :], sin[:, None, :])
    kr_l = apply_axial_rope(kr_l, cos, sin)
    kn_all = jnp.concatenate([kn_l, kn_c], axis=1)
    kr_all = jnp.concatenate([kr_l, kr_c], axis=1)
    v_all = jnp.concatenate([v_l, v_c], axis=1)
    y_lat = mla_block_attention(qn_l, qr_l, kn_all, kr_all, v_all) @ w_o
    if not need_ctx_out:
        return y_lat, None
    y_ctx = mla_block_attention(qn_c, qr_c, kn_c, kr_c, v_c) @ w_o
    return y_lat, y_ctx


def setup_inputs(seed: int = 0) -> dict:
    key = jax.random.key(seed)
    ks = jax.random.split(key, 24)
    f32 = jnp.float32

    def nrm(k, shape, std):
        return jax.random.normal(k, shape, f32) * std

    return {
        "x": nrm(ks[0], (BATCH, SEQ, D_MODEL), 1.0),
        "c": nrm(ks[1], (BATCH, D_MODEL), 1.0),
        "ctx": nrm(ks[2], (BATCH, CTX_LEN, D_MODEL), 1.0),
        "c_ctx": nrm(ks[3], (D_MODEL,), 1.0),
        "ada_w": nrm(ks[4], (DEPTH, D_MODEL, N_MOD * D_MODEL), 0.5 * D_MODEL ** -0.5),
        "ada_b": nrm(ks[5], (DEPTH, N_MOD * D_MODEL), 0.02),
        "norm_g": 1.0 + nrm(ks[6], (DEPTH, 3, D_MODEL), 0.02),
        "ffn_w_gate": nrm(ks[7], (DEPTH, 2, D_MODEL, D_FF), D_MODEL ** -0.5),
        "ffn_w_up": nrm(ks[8], (DEPTH, 2, D_MODEL, D_FF), D_MODEL ** -0.5),
        "ffn_w_down": nrm(ks[9], (DEPTH, 2, D_FF, D_MODEL), D_FF ** -0.5),
        "ab_w_in": nrm(ks[10], (N_EVEN, D_MODEL, AB_IN_DIM), D_MODEL ** -0.5),
        "ab_rpb": nrm(ks[11], (N_EVEN, NA_HEADS, 2 * NA_KH - 1, 2 * NA_KW - 1), 0.1),
        "ab_w_out": nrm(ks[12], (N_EVEN, AB_OUT_DIM, D_MODEL), AB_OUT_DIM ** -0.5),
        "mla_w_in": nrm(ks[13], (N_ODD, D_MODEL, MLA_IN_DIM), D_MODEL ** -0.5),
        "mla_g_q": 1.0 + nrm(ks[14], (N_ODD, MLA_Q_RANK), 0.02),
        "mla_g_kv": 1.0 + nrm(ks[15], (N_ODD, MLA_KV_RANK), 0.02),
        "mla_w_uq": nrm(ks[16], (N_ODD, MLA_Q_RANK, MLA_HEADS * (MLA_NOPE + MLA_ROPE)), MLA_Q_RANK ** -0.5),
        "mla_w_uk": nrm(ks[17], (N_ODD, MLA_KV_RANK, MLA_HEADS * MLA_NOPE), MLA_KV_RANK ** -0.5),
        "mla_w_uv": nrm(ks[18], (N_ODD, MLA_KV_RANK, MLA_HEADS * MLA_V), MLA_KV_RANK ** -0.5),
        "mla_w_o": nrm(ks[19], (N_ODD, MLA_HEADS * MLA_V, D_MODEL), (MLA_HEADS * MLA_V) ** -0.5),
        "final_g": 1.0 + nrm(ks[20], (D_MODEL,), 0.02),
    }


def reference(x, c, ctx, c_ctx, ada_w, ada_b, norm_g, ffn_w_gate, ffn_w_up, ffn_w_down,
              ab_w_in, ab_rpb, ab_w_out, mla_w_in, mla_g_q, mla_g_kv, mla_w_uq, mla_w_uk,
              mla_w_uv, mla_w_o, final_g):
    n_lat = x.shape[1]
    cos, sin = axial_rope_tables(n_lat, MLA_ROPE)
    sc = jax.nn.silu(c)
    scc = jax.nn.silu(c_ctx)
    x_l, x_c = x, ctx
    for layer in range(DEPTH):
        last = layer == DEPTH - 1
        mod_l = (sc @ ada_w[layer] + ada_b[layer]).reshape(-1, N_MOD, D_MODEL)[:, :, None, :]
        mod_c = (scc @ ada_w[layer] + ada_b[layer]).reshape(N_MOD, D_MODEL)

        x_l = half_ffn(x_l, mod_l[:, 0], mod_l[:, 1], mod_l[:, 2], norm_g[layer, 0],
                       ffn_w_gate[layer, 0], ffn_w_up[layer, 0], ffn_w_down[layer, 0])
        x_c = half_ffn(x_c, mod_c[0], mod_c[1], mod_c[2], norm_g[layer, 0],
                       ffn_w_gate[layer, 0], ffn_w_up[layer, 0], ffn_w_down[layer, 0])

        h_l = rmsnorm(x_l, norm_g[layer, 1]) * (1 + mod_l[:, 4]) + mod_l[:, 3]
        h_c = rmsnorm(x_c, norm_g[layer, 1]) * (1 + mod_c[4]) + mod_c[3]
        i = layer // 2
        if layer % 2 == 0:
            y_l, y_c = fourier_na_mixer(h_l, h_c, ab_w_in[i], ab_rpb[i], ab_w_out[i], not last)
        else:
            y_l, y_c = mla_mixer(h_l, h_c, mla_w_in[i], mla_g_q[i], mla_g_kv[i], mla_w_uq[i],
                                 mla_w_uk[i], mla_w_uv[i], mla_w_o[i], cos, sin, not last)
        x_l = x_l + mod_l[:, 5] * y_l

        x_l = half_ffn(x_l, mod_l[:, 6], mod_l[:, 7], mod_l[:, 8], norm_g[layer, 2],
                       ffn_w_gate[layer, 1], ffn_w_up[layer, 1], ffn_w_down[layer, 1])
        if not last:
            x_c = x_c + mod_c[5] * y_c
            x_c = half_ffn(x_c, mod_c[6], mod_c[7], mod_c[8], norm_g[layer, 2],
                           ffn_w_gate[layer, 1], ffn_w_up[layer, 1], ffn_w_down[layer, 1])
    return rmsnorm(x_l, final_g)
```

```python
import contextlib
import numpy as np
import concourse.bass as bass
import concourse.mybir as mybir
from concourse.bass_utils import run_bass_kernel_spmd

F32 = mybir.dt.float32
BF16 = mybir.dt.bfloat16
AF = mybir.ActivationFunctionType
ALU = mybir.AluOpType

D = 1024
NL = 4096
NCX = 256
NT = NL + NCX
DFF = 2816
NF = DFF // 128
EPS = 1e-6
SEM_CHUNK = 30000


class Sched:
    ENG = ("pe", "act", "dve", "pool", "sp")

    def __init__(self, nc):
        self.nc = nc
        self.ops = {e: [] for e in self.ENG}
        self.res = {}
        self.dma_cnt = {}
        self.seen = {e: {} for e in self.ENG}

    def _collect(self, eng, reads, writes):
        need = {}

        def add(src, val, raw):
            if src[0] == "e" and src[1] == eng:
                if eng in ("pe", "sp"):
                    return
            if need.get(src, 0) < val:
                need[src] = val

        for k in reads:
            st = self.res.get(k)
            if st and st[0] is not None:
                add(st[0][0], st[0][1], True)
        for k in writes:
            st = self.res.get(k)
            if st:
                if st[0] is not None:
                    add(st[0][0], st[0][1], False)
                for src, val in st[1].items():
                    add(src, val, False)
        return self._filter(eng, need)

    def _filter(self, eng, need):
        waits = []
        seen = self.seen[eng]
        for src, val in need.items():
            if seen.get(src, 0) >= val:
                continue
            seen[src] = val
            waits.append((src, val))
            if src[0] == "e":
                self.ops[src[1]][val - 1]["signal"] = True
        return waits

    def _update(self, ref, reads, writes):
        src, val = ref
        for k in writes:
            self.res[k] = [ref, {}]
        for k in reads:
            st = self.res.setdefault(k, [None, {}])
            if st[1].get(src, 0) < val:
                st[1][src] = val

    def op(self, eng, fn, reads=(), writes=()):
        waits = self._collect(eng, reads, writes)
        lst = self.ops[eng]
        lst.append(dict(fn=fn, waits=waits, signal=False, dma=None))
        self._update((("e", eng), len(lst)), reads, writes)

    def dma(self, eng, fn, sem, reads=(), writes=()):
        waits = self._collect(eng, reads, writes)
        sem = f"{eng}_{sem}"
        cnt = self.dma_cnt.get(sem, 0) + 1
        assert cnt * 16 < SEM_CHUNK
        self.dma_cnt[sem] = cnt
        self.ops[eng].append(dict(fn=fn, waits=waits, signal=False, dma=sem))
        self._update((("d", sem), cnt), reads, writes)

    def barrier(self):
        last = {}
        for f in self.ENG:
            idx = 0
            for i, o in enumerate(self.ops[f]):
                if o["dma"] is None and o["fn"] is not None:
                    idx = i + 1
            last[f] = idx
        need = {("e", f): last[f] for f in self.ENG if f != "sp" and last[f] > 0}
        for k, c in self.dma_cnt.items():
            need[("d", k)] = c
        waits = self._filter("sp", need)
        self.ops["sp"].append(dict(fn=lambda e: e.nop(), waits=waits, signal=False, dma=None))
        idx = len(self.ops["sp"])
        for e in self.ENG:
            if e != "sp":
                w = self._filter(e, {("e", "sp"): idx})
                self.ops[e].append(dict(fn=None, waits=w, signal=False, dma=None))
        self.res = {}

    def emit(self):
        nc = self.nc
        with contextlib.ExitStack() as es:
            esem = {}
            sigpos = {}
            for e in self.ENG:
                c = 0
                pos = []
                for o in self.ops[e]:
                    if o["signal"]:
                        c += 1
                    pos.append(c)
                sigpos[e] = pos
                nsem = max(1, (c + SEM_CHUNK - 1) // SEM_CHUNK)
                esem[e] = [es.enter_context(nc.semaphore(f"s_{e}_{i}")) for i in range(nsem)]
            dsem = {k: es.enter_context(nc.semaphore(f"d_{k}")) for k in self.dma_cnt}
            block = es.enter_context(nc.Block())

            def run(e, engobj):
                cnt = 0
                for o in self.ops[e]:
                    for src, val in o["waits"]:
                        if src[0] == "e":
                            c = sigpos[src[1]][val - 1]
                            assert c > 0
                            engobj.wait_ge(esem[src[1]][(c - 1) // SEM_CHUNK], (c - 1) % SEM_CHUNK + 1)
                        else:
                            engobj.wait_ge(dsem[src[1]], 16 * val)
                    if o["fn"] is None:
                        continue
                    ins = o["fn"](engobj)
                    if o["dma"] is not None:
                        ins.then_inc(dsem[o["dma"]], 16)
                    elif o["signal"]:
                        cnt += 1
                        ins.then_inc(esem[e][(cnt - 1) // SEM_CHUNK], 1)

            @block.tensor
            def _(eng):
                run("pe", eng)

            @block.scalar
            def _(eng):
                run("act", eng)

            @block.vector
            def _(eng):
                run("dve", eng)

            @block.gpsimd
            def _(eng):
                run("pool", eng)

            @block.sync
            def _(eng):
                run("sp", eng)


class Arena:
    def __init__(self, nc, base, size):
        self.nc, self.base, self.size, self.off, self.n = nc, base, size, 0, 0

    def reset(self, off=0):
        self.off = off

    def t(self, name, shape, dt):
        nb = int(np.prod(shape[1:])) * (4 if dt == F32 else 2)
        nb = (nb + 63) // 64 * 64
        assert self.off + nb <= self.size, (name, self.off, nb, self.size)
        self.n += 1
        h = self.nc.alloc_sbuf_tensor_at(f"{name}_{self.n}", list(shape), dt, offset=self.base + self.off)
        self.off += nb
        return h


class K:
    pass


def build(only=None, dbg=(), small=False, xin=()):
    nc = bass.Bass("TRN2", target_bir_lowering=False)
    k = K()
    k.nc = nc

    def din(name, shape, dt=F32):
        return nc.dram_tensor(name, list(shape), dt, kind="ExternalInput").ap()

    def dscr(name, shape, dt=BF16):
        kind = "ExternalOutput" if name in dbg else "Internal"
        return nc.dram_tensor(name, list(shape), dt, kind=kind).ap()

    k.x0 = din("x0", [D, NT])
    k.sc = din("sc", [128, 8, 2])
    k.ada_w = din("ada_w", [2, D, 9 * D])
    k.adb = din("adb", [128, 2, 9, 16])
    k.ng = din("ng", [128, 2, 3, 16])
    k.fg = din("fg", [128, 8])
    k.w_gate = din("ffn_w_gate", [2, 2, D, DFF])
    k.w_up = din("ffn_w_up", [2, 2, D, DFF])
    k.w_down = din("ffn_w_down", [2, 2, DFF, D])
    k.out = nc.dram_tensor("outT", [D, NL], F32, kind="ExternalOutput").ap()
    k.ab_w_in = din("ab_w_in", [D, 2048])
    k.ab_w_out = din("ab_w_out", [D, D])
    k.mla_w_in = din("mla_w_in", [D, 640])
    k.mla_w_uq = din("mla_w_uq", [384, 2048])
    k.mla_w_ukT = din("mla_w_ukT", [8, 128, 128])
    k.mla_w_uv = din("mla_w_uv", [128, 1024])
    k.mla_w_o = din("mla_w_o", [D, D])
    k.gq = din("gq", [128, 3])
    k.gkv = din("gkv", [128, 1])
    k.ident_d = din("ident", [128, 128])
    k.kvec = din("kvec", [128, 4096])
    k.tcol = din("tcol", [128, 32])
    k.csl = din("csl", [128, 256])
    k.csc = din("csc", [128, 256])
    k.ropec = din("ropec", [64, NL])
    k.ropes = din("ropes", [64, NL])
    k.nabias = din("nabias", [3, 8, 8, 128, 512])

    k.X = {}
    for n_ in ("X1", "X2", "X3", "X4", "X5", "X6"):
        if n_ in xin:
            k.X[n_] = din(n_, [D, NT])
        else:
            k.X[n_] = dscr(n_, [D, NT], F32)
    k.wgu = {(l, s): dscr(f"wgu{l}{s}", [NF, 128, 2, 8, 128]) for l in range(2) for s in range(2)}
    k.wd = {(l, s): dscr(f"wd{l}{s}", [8, 128, NF, 128]) for l in range(2) for s in range(2)}
    k.WIN0 = dscr("WIN0", [16, 128, 8, 128])
    k.WV0 = dscr("WV0", [128, 8, 512])
    k.WOUT0 = dscr("WOUT0", [8, 128, 8, 128])
    k.QT0 = dscr("QT0", [4, 128, NT])
    k.KT0 = dscr("KT0", [4, 128, NT])
    k.VT0 = dscr("VT0", [34, 128, 8, 128])
    k.UCS = dscr("UCS", [34, 128, 4, 256])
    k.CAT = dscr("CAT", [8, 128, NT])
    k.NAB = dscr("NAB", [3, 8, 128, 8, 512])
    k.WIN1 = dscr("WIN1", [5, 128, 8, 128])
    k.WUQ = dscr("WUQ", [16, 128, 3, 128])
    k.WUKT = dscr("WUKT", [128, 8, 128])
    k.WUV = dscr("WUV", [128, 1024])
    k.WO = dscr("WO", [8, 128, 8, 128])
    k.QP = dscr("QP", [8, 128, NL])
    k.QR = dscr("QR", [4, 128, NL])
    k.CKVT = dscr("CKVT", [128, NT])
    k.KRT = dscr("KRT", [128, NT])
    k.CKVTOK = dscr("CKVTOK", [34, 128, 128])

    arena_h = nc.alloc_sbuf_tensor("arena", [128, 204800], mybir.dt.uint8)
    base = nc.lookup_mloc(arena_h).addr
    A = Arena(nc, base, 204800)
    k.A = A
    k.ps = nc.alloc_psum_tensor("ps", [128, 8, 512], F32)
    S = Sched(nc)
    k.S = S

    k.ident = A.t("ident", [128, 128], F32)
    k.onesD = A.t("onesD", [128, 128], BF16)
    k.ones = A.t("ones", [128, 128], BF16)
    k.identb = A.t("identb", [128, 128], BF16)
    k.junk = A.t("junk", [128, 512], BF16)
    k.mod = A.t("mod", [128, 2, 9, 16], F32)
    k.Am = A.t("Am", [128, 2, 3, 16], F32)
    k.Gh = A.t("Gh", [128, 2, 3, 16], F32)
    k.fgt = A.t("fgt", [128, 8], F32)
    k.ngt = A.t("ngt", [128, 2, 3, 16], F32)
    k.adbt = A.t("adbt", [128, 2, 9, 16], F32)
    k.static_end = 8192
    assert A.off <= k.static_end

    lat_groups = [(g * 1024, 1024, 0) for g in range(4)]
    na_q = [(i * 512, 512, "lat") for i in range(8)] + [(NL, NCX, "ctx")]
    kts = list(range(8))
    mla_q = [i * 512 for i in range(8)]
    if small:
        lat_groups = [(0, 1024, 0)]
        na_q = [(0, 512, "lat"), (NL, NCX, "ctx")]
        kts = [0]
        mla_q = [0]
    if small == 2:
        lat_groups = [(g * 1024, 1024, 0) for g in range(4)]
    all_groups = lat_groups + [(NL, NCX, 1)]

    def casts_mixer0():
        cast_lin(k, k.ab_w_in, k.WIN0, D, 2048)
        DMA(S, "pool", k.WV0, k.ab_w_in[:, 1536:2048].rearrange("(k p) n -> p k n", p=128), "cast")
        cast_lin(k, k.ab_w_out, k.WOUT0, D, D)
        for pat in range(3):
            for h in range(8):
                DMA(S, "pool", k.NAB[pat, h], k.nabias[pat, h].rearrange("j p q -> p j q"), "cast")

    def casts_l1():
        cast_lin(k, k.mla_w_in, k.WIN1, D, 640)
        cast_lin(k, k.mla_w_uq, k.WUQ, 384, 2048)
        DMA(S, "pool", k.WUKT, k.mla_w_ukT.rearrange("h d r -> d h r"), "cast")
        DMA(S, "pool", k.WUV, k.mla_w_uv, "cast")
        cast_lin(k, k.mla_w_o, k.WO, D, D)

    def all_casts():
        casts_mixer0()
        casts_l1()
        for l_, s_ in ((0, 1), (1, 0), (1, 1)):
            cast_ffn(k, l_, s_)

    full = only is None
    plan = [
        ("setup", lambda: phase_setup(k, casts_mixer0 if full else all_casts)),
        ("ffn00", lambda: ffn_phase(k, 0, 0, k.x0, k.X["X1"], all_groups,
                                    (lambda: (cast_ffn(k, 0, 1), cast_ffn(k, 1, 0))) if full else None)),
        ("proj0", lambda: proj0_phase(k, all_groups, (lambda: (casts_l1(), cast_ffn(k, 1, 1))) if full else None)),
        ("na", lambda: na_phase(k, na_q)),
        ("fnet", lambda: fnet_phase(k, kts)),
        ("wout", lambda: lin_res_phase(k, k.X["X1"], k.X["X2"], k.CAT, k.WOUT0, 0, all_groups)),
        ("ffn01", lambda: ffn_phase(k, 0, 1, k.X["X2"], k.X["X3"], all_groups, None)),
        ("ffn10", lambda: ffn_phase(k, 1, 0, k.X["X3"], k.X["X4"], all_groups, None)),
        ("proj1", lambda: proj1_phase(k, all_groups)),
        ("mla", lambda: mla_phase(k, mla_q)),
        ("ffn11", lambda: ffn_phase(k, 1, 1, k.X["X5"], k.X["X6"], lat_groups, None)),
        ("final", lambda: final_phase(k, k.X["X6"], lat_groups)),
    ]
    for name, fn in plan:
        if only is not None and name not in only:
            continue
        fn()
        S.barrier()
    global LAST_S
    LAST_S = S
    S.emit()
    return nc


def MM(S, out, lhsT, rhs, start, stop, reads, writes):
    S.op("pe", lambda e: e.matmul(out, lhsT=lhsT, rhs=rhs, start=start, stop=stop), reads, writes)


def ACTF(S, out, in_, func, reads, writes, scale=1.0, bias=0.0):
    S.op("act", lambda e: e.activation(out=out, in_=in_, func=func, scale=scale, bias=bias), reads, writes)


def TT(S, out, in0, in1, op, reads, writes, eng="dve"):
    S.op(eng, lambda e: e.tensor_tensor(out=out, in0=in0, in1=in1, op=op), reads, writes)


def STT(S, out, in0, scalar, in1, op0, op1, reads, writes, eng="dve"):
    S.op(eng, lambda e: e.scalar_tensor_tensor(out=out, in0=in0, scalar=scalar, in1=in1, op0=op0, op1=op1), reads, writes)


def TS(S, out, in0, s1, op0, reads, writes, s2=None, op1=None, eng="dve"):
    if op1 is None:
        S.op(eng, lambda e: e.tensor_scalar(out=out, in0=in0, scalar1=s1, scalar2=None, op0=op0), reads, writes)
    else:
        S.op(eng, lambda e: e.tensor_scalar(out=out, in0=in0, scalar1=s1, scalar2=s2, op0=op0, op1=op1), reads, writes)


def CP(S, out, in_, reads, writes, eng="dve"):
    S.op(eng, lambda e: e.tensor_copy(out=out, in_=in_), reads, writes)


def RCP(S, out, in_, reads, writes):
    S.op("dve", lambda e: e.reciprocal(out=out, in_=in_), reads, writes)


def MSET(S, ap, val, writes, eng="dve"):
    S.op(eng, lambda e: e.memset(ap, val), (), writes)


def DMA(S, eng, out, in_, sem, reads=(), writes=()):
    S.dma(eng, lambda e: e.dma_start(out=out, in_=in_), sem, reads, writes)


def phase_setup(k, extra_casts=None):
    nc, S, A, ps = k.nc, k.S, k.A, k.ps
    A.reset(k.static_end)
    sc_f = A.t("sc_f", [128, 8, 2], F32)
    sc_b = A.t("sc_b", [128, 8, 2], BF16)
    adw = [A.t(f"adw{i}", [128, 8, 1024], BF16) for i in range(2)]
    cast_ffn(k, 0, 0)
    if extra_casts:
        extra_casts()
    DMA(S, "sp", k.ident[:], k.ident_d, "m_id", writes=["ident"])
    DMA(S, "sp", sc_f[:], k.sc, "m_sc", writes=["sc_f"])
    DMA(S, "sp", k.adbt[:], k.adb, "m_adb", writes=["adbt"])
    DMA(S, "sp", k.ngt[:], k.ng, "m_ng", writes=["ngt"])
    DMA(S, "sp", k.fgt[:], k.fg, "m_fg", writes=["fgt"])
    MSET(S, k.onesD[:], 1.0 / D, ["onesD"])
    MSET(S, k.ones[:], 1.0, ["ones"])
    MSET(S, k.junk[:], 1.0, ["junk"])
    CP(S, k.identb[:], k.ident[:], ["ident"], ["identb"])
    ACTF(S, sc_b[:], sc_f[:], AF.Silu, ["sc_f"], ["sc_b"])
    n = 0
    for l in range(2):
        for i in range(9):
            b = n % 2
            DMA(S, "pool", adw[b][:], k.ada_w[l, :, i * 1024:(i + 1) * 1024].rearrange("(k p) n -> p k n", p=128),
                f"adw{b}", writes=[("adw", b)])
            bank = n % 2
            for j in range(8):
                for kk in range(8):
                    MM(S, ps[:, bank, j * 2:j * 2 + 2], adw[b][:, kk, j * 128:(j + 1) * 128], sc_b[:, kk, :],
                       kk == 0, kk == 7, [("adw", b), "sc_b"], [("ps", bank)])
            TT(S, k.mod[:, l, i, :], ps[:, bank, 0:16], k.adbt[:, l, i, :], ALU.add, [("ps", bank), "adbt"], ["mod"])
            n += 1
    for l in range(2):
        for i in range(3):
            STT(S, k.Am[:, l, i, :], k.mod[:, l, 3 * i + 1, :], 1.0, k.ngt[:, l, i, :], ALU.add, ALU.mult,
                ["mod", "ngt"], ["Am"])
            TS(S, k.Gh[:, l, i, :], k.mod[:, l, 3 * i + 2, :], (1.0 if i == 1 else 0.5), ALU.mult, ["mod"], ["Gh"])


def cast_ffn(k, l, s):
    S = k.S
    for f in range(NF):
        for gi, W in enumerate((k.w_gate, k.w_up)):
            DMA(S, "pool", k.wgu[(l, s)][f, :, gi],
                W[l, s, :, f * 128:(f + 1) * 128].rearrange("(k p) m -> p k m", p=128), "cast")
    for dj in range(8):
        DMA(S, "pool", k.wd[(l, s)][dj],
            k.w_down[l, s, :, dj * 128:(dj + 1) * 128].rearrange("(k p) m -> p k m", p=128), "cast")


def norm_stats(k, xb_j, n, sq, rs, tag):
    S, ps = k.S, k.ps
    nsub = (n + 511) // 512
    w = min(n, 512)
    for j in range(8):
        q = j % 2
        ap, key = xb_j(j)
        ACTF(S, sq[q][:, 0:n], ap, AF.Square, [key], [("sq", q)])
        for sub in range(nsub):
            MM(S, ps[:, 6 + sub, 0:w], k.onesD[:], sq[q][:, sub * 512:sub * 512 + w], j == 0, j == 7,
               [("sq", q), "onesD"], [("ps", 6 + sub)])
    ACTF(S, rs[:, 0:n], ps[:, 6:6 + nsub, 0:w].rearrange("p a n -> p (a n)"), AF.Sqrt,
         [("ps", 6 + i) for i in range(nsub)], [tag], bias=EPS)
    RCP(S, rs[:, 0:n], rs[:, 0:n], [tag], [tag])


def ffn_phase(k, l, s, Xsrc, Xdst, groups, pre_casts=None):
    nc, S, A, ps = k.nc, k.S, k.A, k.ps
    A.reset(k.static_end)
    xb = [A.t(f"xb{i}", [128, 8, 1024], F32) for i in range(2)]
    hb = A.t("hb", [128, 8, 1024], BF16)
    act = A.t("act", [128, NF, 1024], BF16)
    sq = [A.t(f"sq{i}", [128, 1024], BF16) for i in range(2)]
    rs = [A.t(f"rs{i}", [128, 1024], F32) for i in range(2)]
    tmp = [A.t(f"tmp{i}", [128, 1024], F32) for i in range(2)]
    sg = [A.t(f"sg{i}", [128, 512], F32) for i in range(2)]
    gu = [A.t(f"gu{i}", [128, 2, 8, 128], BF16) for i in range(3)]
    wdt = [A.t(f"wd{i}", [128, NF, 128], BF16) for i in range(3)]
    isub = 0 if s == 0 else 2
    wgu_scr, wd_scr = k.wgu[(l, s)], k.wd[(l, s)]
    ng = len(groups)

    events = []
    for gi in range(ng):
        events += [("gu", f) for f in range(NF)] + [("wd", dj) for dj in range(8)]
    cnt = {"gu": 0, "wd": 0}
    slot_of = []
    for kind, idx in events:
        slot_of.append(cnt[kind] % 3)
        cnt[kind] += 1
    issued = [0]

    def issue_upto(i):
        while issued[0] <= min(i, len(events) - 1):
            ev = issued[0]
            kind, idx = events[ev]
            sl = slot_of[ev]
            if kind == "gu":
                DMA(S, "sp", gu[sl][:], wgu_scr[idx], f"gu{sl}", writes=[("gu", sl)])
            else:
                DMA(S, "sp", wdt[sl][:], wd_scr[idx], f"wd{sl}", writes=[("wd", sl)])
            issued[0] += 1

    def load_x(gi):
        t0, n, st = groups[gi]
        b = gi % 2
        DMA(S, "pool", xb[b][:, :, 0:n], Xsrc[:, t0:t0 + n].rearrange("(j p) t -> p j t", p=128),
            f"xb{b}", writes=[("xb", b, j) for j in range(8)])

    def stats(gi):
        t0, n, st = groups[gi]
        b = gi % 2
        norm_stats(k, lambda j: (xb[b][:, j, 0:n], ("xb", b, j)), n, sq, rs[b], ("rs", b))

    def hpiece(gi, j):
        t0, n, st = groups[gi]
        b = gi % 2
        q = j % 2
        c = j * 2 + st
        TT(S, tmp[q][:, 0:n], xb[b][:, j, 0:n], rs[b][:, 0:n], ALU.mult, [("xb", b, j), ("rs", b)], [("tmp", q)])
        ACTF(S, hb[:, j, 0:n], tmp[q][:, 0:n], AF.Identity, [("tmp", q), "Am", "mod"], [("hb", j)],
             scale=k.Am[:, l, isub, c:c + 1], bias=k.mod[:, l, 3 * isub, c:c + 1])

    def do_group(gi):
        t0, n, st = groups[gi]
        b = gi % 2
        nsub = (n + 511) // 512
        w = min(n, 512)
        for f in range(NF):
            ev = gi * 30 + f
            issue_upto(ev + 2)
            sl = slot_of[ev]
            for sub in range(nsub):
                it = f * nsub + sub
                bg, bu = 2 * (it % 3), 2 * (it % 3) + 1
                for half, bank in ((0, bg), (1, bu)):
                    for kk in range(8):
                        MM(S, ps[:, bank, 0:w], gu[sl][:, half, kk, :], hb[:, kk, sub * 512:sub * 512 + w],
                           kk == 0, kk == 7, [("gu", sl), ("hb", kk)], [("ps", bank)])
                q = it % 2
                ACTF(S, sg[q][:, 0:w], ps[:, bg, 0:w], AF.Silu, [("ps", bg)], [("sg", q)])
                TT(S, act[:, f, sub * 512:sub * 512 + w], sg[q][:, 0:w], ps[:, bu, 0:w], ALU.mult,
                   [("sg", q), ("ps", bu)], [("act", f)])
        if gi + 1 < ng:
            stats(gi + 1)
        for dj in range(8):
            ev = gi * 30 + NF + dj
            issue_upto(ev + 2)
            sl = slot_of[ev]
            c = dj * 2 + st
            for sub in range(nsub):
                by = (dj * nsub + sub) % 6
                for kf in range(NF):
                    MM(S, ps[:, by, 0:w], wdt[sl][:, kf, :], act[:, kf, sub * 512:sub * 512 + w],
                       kf == 0, kf == NF - 1, [("wd", sl), ("act", kf)], [("ps", by)])
                xs = xb[b][:, dj, sub * 512:sub * 512 + w]
                STT(S, xs, ps[:, by, 0:w], k.Gh[:, l, isub, c:c + 1], xs, ALU.mult, ALU.add,
                    [("ps", by), ("xb", b, dj), "Gh"], [("xb", b, dj)])
            if gi + 1 < ng:
                hpiece(gi + 1, dj)
        DMA(S, "sp", Xdst[:, t0:t0 + n].rearrange("(j p) t -> p j t", p=128), xb[b][:, :, 0:n],
            f"xs{b}", reads=[("xb", b, j) for j in range(8)])
        if gi + 2 < ng:
            load_x(gi + 2)

    load_x(0)
    if ng > 1:
        load_x(1)
    if pre_casts:
        pre_casts()
    issue_upto(1)
    stats(0)
    for j in range(8):
        hpiece(0, j)
    for gi in range(ng):
        do_group(gi)


def final_phase(k, Xsrc, groups):
    S, A = k.S, k.A
    A.reset(k.static_end)
    xb = [A.t(f"xb{i}", [128, 8, 1024], F32) for i in range(2)]
    sq = [A.t(f"sq{i}", [128, 1024], BF16) for i in range(2)]
    rs = [A.t(f"rs{i}", [128, 1024], F32) for i in range(2)]

    def grp(gi):
        t0, n, st = groups[gi]
        b = gi % 2
        DMA(S, "sp", xb[b][:, :, 0:n], Xsrc[:, t0:t0 + n].rearrange("(j p) t -> p j t", p=128),
            f"xb{b}", writes=[("xb", b, j) for j in range(8)])
        norm_stats(k, lambda j: (xb[b][:, j, 0:n], ("xb", b, j)), n, sq, rs[b], ("rs", b))
        for j in range(8):
            STT(S, xb[b][:, j, 0:n], xb[b][:, j, 0:n], k.fgt[:, j:j + 1], rs[b][:, 0:n], ALU.mult, ALU.mult,
                [("xb", b, j), ("rs", b), "fgt"], [("xb", b, j)])
        DMA(S, "sp", k.out[:, t0:t0 + n].rearrange("(j p) t -> p j t", p=128), xb[b][:, :, 0:n],
            f"xs{b}", reads=[("xb", b, j) for j in range(8)])

    for gi in range(len(groups)):
        grp(gi)


def cast_lin(k, src, dst, K_, N_):
    for j in range(N_ // 128):
        DMA(k.S, "pool", dst[j], src[:, j * 128:(j + 1) * 128].rearrange("(k p) m -> p k m", p=128), "cast")


class Banks:
    def __init__(self, lst):
        self.lst, self.i = lst, 0

    def next(self):
        b = self.lst[self.i % len(self.lst)]
        self.i += 1
        return b


def load_norm(k, Xsrc, t0, n, st, l, isub, xb, sq, rs, tmp, hb, tag, xtag=None):
    S = k.S
    xt = tag if xtag is None else xtag
    DMA(S, "sp", xb[:, :, 0:n], Xsrc[:, t0:t0 + n].rearrange("(j p) t -> p j t", p=128), f"xb{xt}",
        writes=[("xb", xt, j) for j in range(8)])
    norm_stats(k, lambda j: (xb[:, j, 0:n], ("xb", xt, j)), n, sq, rs, ("rs", 0))
    for j in range(8):
        q = j % 2
        c = j * 2 + st
        TT(S, tmp[q][:, 0:n], xb[:, j, 0:n], rs[:, 0:n], ALU.mult, [("xb", xt, j), ("rs", 0)], [("tmp", q)])
        ACTF(S, hb[:, j, 0:n], tmp[q][:, 0:n], AF.Identity, [("tmp", q)], [("hb", tag, j)],
             scale=k.Am[:, l, isub, c:c + 1], bias=k.mod[:, l, 3 * isub, c:c + 1])


def copy_alt(k, out, in_, reads, writes):
    k.cpi = getattr(k, "cpi", 0) + 1
    if k.cpi % 2:
        ACTF(k.S, out, in_, AF.Identity, reads, writes)
    else:
        CP(k.S, out, in_, reads, writes)


def proj0_phase(k, groups, pre_casts=None):
    S, A, ps = k.S, k.A, k.ps
    A.reset(k.static_end)
    xb = [A.t(f"xb{i}", [128, 8, 1024], F32) for i in range(2)]
    hbs = [A.t(f"hb{i}", [128, 8, 1024], BF16) for i in range(2)]
    win = A.t("win", [128, 12, 8, 128], BF16)
    wv = A.t("wv", [128, 8, 512], BF16)
    sq = [A.t(f"sq{i}", [128, 1024], BF16) for i in range(2)]
    rs = A.t("rs", [128, 1024], F32)
    tmp = [A.t(f"tmp{i}", [128, 1024], F32) for i in range(2)]
    uT = A.t("uT", [128, 4, 1024], BF16)
    qk = [A.t(f"qk{i}", [128, 1024], BF16) for i in range(2)]
    vt = [A.t(f"vt{i}", [128, 8, 128], BF16) for i in range(2)]
    uc = [A.t(f"uc{i}", [128, 4, 256], BF16) for i in range(2)]
    csf = A.t("csf", [128, 2, 256], F32)
    csb = A.t("csb", [128, 2, 256], BF16)
    DMA(S, "sp", win[:], k.WIN0[0:12].rearrange("j p k m -> p j k m"), "win", writes=["win"])
    DMA(S, "sp", wv[:], k.WV0, "wv", writes=["wv"])
    DMA(S, "sp", csf[:, 0, :], k.csl, "csl", writes=["csf0"])
    DMA(S, "sp", csf[:, 1, :], k.csc, "csc", writes=["csf1"])
    CP(S, csb[:], csf[:], ["csf0", "csf1"], ["csb"])
    for i_ in range(2):
        MSET(S, vt[i_][:, :, 64:128], 1.0, [("vt", i_)])
    if pre_casts:
        pre_casts()
    bk = Banks([0, 1, 2, 3, 4, 5])
    n_i = 0
    def ln(gi):
        t0, n, st = groups[gi]
        load_norm(k, k.X["X1"], t0, n, st, 0, 1, xb[gi % 2], sq, rs, tmp, hbs[gi % 2], gi % 2)

    ln(0)
    for gi, (t0, n, st) in enumerate(groups):
        b = gi % 2
        hb = hbs[b]
        if gi + 1 < len(groups):
            ln(gi + 1)
        nsub = (n + 511) // 512
        w = min(n, 512)
        for j in range(12):
            q = n_i % 2
            n_i += 1
            for sub in range(nsub):
                bank = bk.next()
                for kk in range(8):
                    MM(S, ps[:, bank, 0:w], win[:, j, kk, :], hb[:, kk, sub * 512:sub * 512 + w], kk == 0, kk == 7,
                       ["win", ("hb", b, kk)], [("ps", bank)])
                if j < 4:
                    copy_alt(k, uT[:, j, sub * 512:sub * 512 + w], ps[:, bank, 0:w], [("ps", bank)], [("uT", j)])
                elif j < 8:
                    ACTF(S, qk[q][:, sub * 512:sub * 512 + w], ps[:, bank, 0:w], AF.Identity, [("ps", bank)], [("qk", q)], scale=0.125)
                else:
                    copy_alt(k, qk[q][:, sub * 512:sub * 512 + w], ps[:, bank, 0:w], [("ps", bank)], [("qk", q)])
            if j >= 4:
                dst = k.QT0[j - 4] if j < 8 else k.KT0[j - 8]
                DMA(S, "sp", dst[:, t0:t0 + n], qk[q][:, 0:n], f"qk{q}", reads=[("qk", q)])
        for tc in range(n // 128):
            gtc = t0 // 128 + tc
            q = tc % 2
            bank = bk.next()
            for kk in range(8):
                MM(S, ps[:, bank, :], hb[:, kk, tc * 128:(tc + 1) * 128], wv[:, kk, :], kk == 0, kk == 7,
                   ["wv", ("hb", b, kk)], [("ps", bank)])
            copy_alt(k, vt[q][:, :, 0:64], ps[:, bank, :].rearrange("p (h d) -> p h d", h=8), [("ps", bank)], [("vt", q)])
            DMA(S, "sp", k.VT0[gtc], vt[q][:], f"vt{q}", reads=[("vt", q)])
            for half in range(2):
                bank = bk.next()
                for gg in range(2):
                    g = half * 2 + gg
                    MM(S, ps[:, bank, gg * 256:(gg + 1) * 256], uT[:, g, tc * 128:(tc + 1) * 128], csb[:, st, :],
                       True, True, [("uT", g), "csb"], [("ps", bank)])
                copy_alt(k, uc[q][:, half * 2:half * 2 + 2, :], ps[:, bank, :].rearrange("p (a n) -> p a n", a=2),
                         [("ps", bank)], [("uc", q, half)])
            DMA(S, "sp", k.UCS[gtc], uc[q][:], f"uc{q}", reads=[("uc", q, 0), ("uc", q, 1)])


def attention(k, st_, nq, qparts, kparts_fn, v_fn, dv, chunks, scale, bias_fn, dmode="pe", ndummy=0, mask_fn=None, nsb=3):
    S, ps = k.S, k.ps
    st_["n"] = st_.get("n", 0) + 1
    cnum = st_["n"]
    BO = nsb + cnum % 2
    BD = nsb + 2 + cnum % 2
    pt, p2, rec, on = st_["pt"], st_["p2"], st_["rec"], st_["on"]
    nch = len(chunks)
    LA = nsb - 1
    o = cnum % 2
    mo = 128 if dmode == "merged" else dv

    def qk(jj):
        bs = st_.setdefault("sb", 0) % nsb
        st_["sb"] += 1
        kp = kparts_fn(chunks[jj])
        bias = bias_fn(jj) if bias_fn else None
        nparts = len(qparts) + (1 if bias is not None else 0)
        for pi, ((qa, qkey), (ka, kkey)) in enumerate(zip(qparts, kp)):
            MM(S, ps[:, bs, 0:nq], ka, qa, pi == 0, pi == nparts - 1, [qkey, kkey], [("ps", bs)])
        if bias is not None:
            bap, bkey = bias
            MM(S, ps[:, bs, 0:nq], k.identb[:], bap, False, True, ["identb", bkey], [("ps", bs)])
        return bs

    banks = [qk(jj) for jj in range(min(LA, nch))]
    dq = []
    npairs = (nch + 1) // 2
    dcount = [0]

    def flush_d():
        while dq:
            ap, key = dq.pop(0)
            MM(S, ps[0:dv, BD, 0:nq], k.ones[:, 0:dv], ap, dcount[0] == 0, dcount[0] == npairs - 1, ["ones", key], [("ps", BD)])
            dcount[0] += 1

    prev = None
    for jj in range(nch):
        bs = banks[jj]
        if jj + LA < nch:
            banks.append(qk(jj + LA))
        if dmode == "pair":
            flush_d()
        q = st_.setdefault("pi", 0) % len(pt)
        st_["pi"] += 1
        ACTF(S, pt[q][:, 0:nq], ps[:, bs, 0:nq], AF.Exp, [("ps", bs)], [("pt", q)], scale=scale)
        mk = mask_fn(jj) if mask_fn else None
        if mk is not None:
            map_, mkey = mk
            TT(S, pt[q][:, 0:nq], pt[q][:, 0:nq], map_, ALU.mult, [("pt", q), mkey], [("pt", q)])
        va, vkey = v_fn(chunks[jj])
        for _ in range(ndummy):
            MM(S, ps[:, 7, :], k.ones[:], k.junk[:], True, True, ["ones", "junk"], [("ps", 7)])
        if dmode == "pe":
            MM(S, ps[0:dv, BD, 0:nq], k.ones[:, 0:dv], pt[q][:, 0:nq], jj == 0, jj == nch - 1, ["ones", ("pt", q)], [("ps", BD)])
        MM(S, ps[0:mo, BO, 0:nq], va, pt[q][:, 0:nq], jj == 0, jj == nch - 1, [vkey, ("pt", q)], [("ps", BO)])
        if dmode == "pair":
            if jj % 2 == 1:
                q2 = st_.setdefault("p2i", 0) % 2
                st_["p2i"] += 1
                TT(S, p2[q2][:, 0:nq], pt[prev][:, 0:nq], pt[q][:, 0:nq], ALU.add, [("pt", prev), ("pt", q)], [("p2", q2)])
                dq.append((p2[q2][:, 0:nq], ("p2", q2)))
            elif jj == nch - 1:
                dq.append((pt[q][:, 0:nq], ("pt", q)))
        prev = q
        if jj == 1 and st_.get("pending"):
            fn = st_.pop("pending")
            fn()
    if dmode in ("pair", "pe"):
        flush_d()
        RCP(S, rec[o][0:dv, 0:nq], ps[0:dv, BD, 0:nq], [("ps", BD)], [("rec", o)])
        TT(S, on[o][0:dv, 0:nq], ps[0:dv, BO, 0:nq], rec[o][0:dv, 0:nq], ALU.mult, [("ps", BO), ("rec", o)], [("on", o)])
    else:
        RCP(S, rec[o][64:128, 0:nq], ps[64:128, BO, 0:nq], [("ps", BO)], [("rec", o)])
        TT(S, on[o][0:dv, 0:nq], ps[0:dv, BO, 0:nq], rec[o][64:128, 0:nq], ALU.mult, [("ps", BO), ("rec", o)], [("on", o)])
    return on[o][0:dv, 0:nq], ("on", o)


def attn_bufs(A, nqmax):
    return dict(pt=[A.t(f"pt{i}", [128, nqmax], BF16) for i in range(6)],
                p2=[A.t(f"p2{i}", [128, nqmax], BF16) for i in range(2)],
                rec=[A.t(f"rec{i}", [128, nqmax], F32) for i in range(2)],
                on=[A.t(f"on{i}", [128, nqmax], BF16) for i in range(2)])


def na_phase(k, qtiles):
    S, A, ps = k.S, k.A, k.ps
    A.reset(k.static_end)
    kT = A.t("kT", [128, 4, NT], BF16)
    qT = A.t("qT", [128, 4, NT], BF16)
    vtk = A.t("vtk", [128, 34, 8, 128], BF16)
    bias = [A.t(f"bias{i}", [128, 8, 512], BF16) for i in range(2)]
    st_ = attn_bufs(A, 512)
    DMA(S, "sp", kT[:], k.KT0.rearrange("j p t -> p j t"), "kT", writes=["kT"])
    DMA(S, "sp", qT[:], k.QT0.rearrange("j p t -> p j t"), "qT", writes=["qT"])
    DMA(S, "sp", vtk[:], k.VT0.rearrange("c p h d -> p c h d"), "vtk", writes=["vtk"])
    nb = 0
    for h in range(8):
        hp, ho = h // 2, (h % 2) * 64
        for pat in (1, 0, 2, None):
            tiles = []
            for (q0, nq, kind) in qtiles:
                if kind == "lat":
                    i = q0 // 512
                    p_ = 0 if i == 0 else (2 if i == 7 else 1)
                    if p_ == pat:
                        c0 = int(np.clip(8 * i - 4, 0, 48)) // 2
                        tiles.append((q0, nq, list(range(c0, c0 + 8)) + [32, 33]))
                elif pat is None:
                    tiles.append((q0, nq, [32, 33]))
            if not tiles:
                continue
            bfn = None
            if pat is not None:
                bb = nb % 2
                nb += 1
                DMA(S, "sp", bias[bb][:], k.NAB[pat, h], f"bias{bb}", writes=[("bias", bb)])
                ACTF(S, bias[bb][:].rearrange("p j q -> p (j q)"), bias[bb][:].rearrange("p j q -> p (j q)"), AF.Exp,
                     [("bias", bb)], [("bias", bb)])
                bfn = (lambda jj, bb=bb: (bias[bb][:, jj, :], ("bias", bb)) if jj < 8 else None)
            for (q0, nq, chunks) in tiles:
                on_ap, on_key = attention(
                    k, st_, nq, [(qT[ho:ho + 64, hp, q0:q0 + nq], "qT")],
                    lambda kc, hp=hp, ho=ho: [(kT[ho:ho + 64, hp, kc * 128:(kc + 1) * 128], "kT")],
                    lambda kc, h=h: (vtk[:, kc, h, :], "vtk"), 64, chunks, 1.0, None, dmode="merged", mask_fn=bfn, nsb=4)
                DMA(S, "sp", k.CAT[4 + hp, ho:ho + 64, q0:q0 + nq], on_ap, f"on{on_key[1]}", reads=[on_key])


def fnet_phase(k, kts=range(8), do_ctx=True):
    S, A, ps = k.S, k.A, k.ps
    A.reset(k.static_end)
    ucs = A.t("ucs", [128, 34, 4, 256], BF16)
    kvec = A.t("kvec", [128, 4096], F32)
    tcol = A.t("tcol", [128, 32], F32)
    gc = A.t("gc", [128, 32, 512], BF16)
    gs = A.t("gs", [128, 32, 512], BF16)
    mS = [A.t(f"mS{i}", [128, 512], F32) for i in range(2)]
    mC = [A.t(f"mC{i}", [128, 512], F32) for i in range(2)]
    yv = [A.t(f"yv{i}", [128, 512], F32) for i in range(2)]
    nS = [A.t(f"nS{i}", [128, 512], F32) for i in range(2)]
    nC = [A.t(f"nC{i}", [128, 512], F32) for i in range(2)]
    tcol2 = A.t("tcol2", [128, 2], F32)
    at = [A.t(f"at{i}", [128, 512], BF16) for i in range(2)]
    DMA(S, "sp", ucs[:], k.UCS.rearrange("c p g n -> p c g n"), "ucs", writes=["ucs"])
    DMA(S, "sp", kvec[:], k.kvec, "kvec", writes=["kvec"])
    DMA(S, "sp", tcol[:], k.tcol, "tcol", writes=["tcol"])
    TS(S, tcol2[:], tcol[:, 0:2], 16.0, ALU.mult, ["tcol"], ["tcol"])
    bk = Banks([0, 1, 2, 3])
    ai = 0

    MAGIC = 12582912.0

    def tables(ntc, kslice, w, tcl):
        for tc0_ in range(0, ntc, 2):
            pr = [(tc0_ + i, i) for i in range(2) if tc0_ + i < ntc]
            for tc, q in pr:
                ACTF(S, yv[q][:, 0:w], kslice, AF.Identity, ["kvec", "tcol"], [("yv", q)], scale=tcl[:, tc:tc + 1])
            for tc, q in pr:
                TS(S, nS[q][:, 0:w], yv[q][:, 0:w], MAGIC, ALU.add, [("yv", q)], [("nS", q)])
            for tc, q in pr:
                TS(S, nC[q][:, 0:w], yv[q][:, 0:w], 0.25, ALU.add, [("yv", q)], [("nC", q)], s2=MAGIC, op1=ALU.add)
            for tc, q in pr:
                STT(S, mS[q][:, 0:w], nS[q][:, 0:w], MAGIC, yv[q][:, 0:w], ALU.subtract, ALU.subtract, [("nS", q), ("yv", q)], [("mS", q)])
            for tc, q in pr:
                STT(S, mC[q][:, 0:w], nC[q][:, 0:w], MAGIC, yv[q][:, 0:w], ALU.subtract, ALU.subtract, [("nC", q), ("yv", q)], [("mC", q)])
            for tc, q in pr:
                ACTF(S, gs[:, tc, 0:w], mS[q][:, 0:w], AF.Sin, [("mS", q)], [("gs", tc)], scale=6.28318)
                ACTF(S, gc[:, tc, 0:w], mC[q][:, 0:w], AF.Sin, [("mC", q)], [("gc", tc)], scale=6.28318, bias=-1.570796)

    def mix(ntc, tc0, w, dst_fn):
        nonlocal ai
        for g in range(4):
            bank = bk.next()
            for tc in range(ntc):
                MM(S, ps[:, bank, 0:w], ucs[:, tc0 + tc, g, 0:128], gc[:, tc, 0:w], tc == 0, False, ["ucs", ("gc", tc)], [("ps", bank)])
                MM(S, ps[:, bank, 0:w], ucs[:, tc0 + tc, g, 128:256], gs[:, tc, 0:w], False, tc == ntc - 1, ["ucs", ("gs", tc)], [("ps", bank)])
            q = ai % 2
            ai += 1
            copy_alt(k, at[q][:, 0:w], ps[:, bank, 0:w], [("ps", bank)], [("at", q)])
            DMA(S, "sp", dst_fn(g), at[q][:, 0:w], f"at{q}", reads=[("at", q)])

    for kt in kts:
        tables(32, kvec[:, kt * 512:(kt + 1) * 512], 512, tcol)
        mix(32, 0, 512, lambda g, kt=kt: k.CAT[g, :, kt * 512:(kt + 1) * 512])
    if do_ctx:
        tables(2, kvec[:, 0:256], 256, tcol2)
        mix(2, 32, 256, lambda g: k.CAT[g, :, NL:NT])


def lin_res_phase(k, Xsrc, Xdst, ACTsrc, Wscr, l, groups):
    S, A, ps = k.S, k.A, k.ps
    A.reset(k.static_end)
    xb = [A.t(f"xb{i}", [128, 8, 1024], F32) for i in range(2)]
    cb = [A.t(f"cb{i}", [128, 8, 1024], BF16) for i in range(2)]
    wt = A.t("wt", [128, 8, 8, 128], BF16)
    DMA(S, "sp", wt[:], Wscr.rearrange("j p k m -> p j k m"), "wt", writes=["wt"])
    bk = Banks([0, 1, 2, 3, 4, 5])
    for gi, (t0, n, st) in enumerate(groups):
        b = gi % 2
        DMA(S, "sp", xb[b][:, :, 0:n], Xsrc[:, t0:t0 + n].rearrange("(j p) t -> p j t", p=128), f"xb{b}",
            writes=[("xb", b, j) for j in range(8)])
        DMA(S, "sp", cb[b][:, :, 0:n], ACTsrc[:, :, t0:t0 + n].rearrange("j p t -> p j t"), f"cb{b}", writes=[("cb", b)])
        nsub = (n + 511) // 512
        w = min(n, 512)
        for j in range(8):
            c = j * 2 + st
            for sub in range(nsub):
                bank = bk.next()
                for kk in range(8):
                    MM(S, ps[:, bank, 0:w], wt[:, j, kk, :], cb[b][:, kk, sub * 512:sub * 512 + w], kk == 0, kk == 7,
                       ["wt", ("cb", b)], [("ps", bank)])
                xs = xb[b][:, j, sub * 512:sub * 512 + w]
                STT(S, xs, ps[:, bank, 0:w], k.Gh[:, l, 1, c:c + 1], xs, ALU.mult, ALU.add, [("ps", bank), ("xb", b, j)], [("xb", b, j)])
        DMA(S, "sp", Xdst[:, t0:t0 + n].rearrange("(j p) t -> p j t", p=128), xb[b][:, :, 0:n], f"xs{b}",
            reads=[("xb", b, j) for j in range(8)])


def proj1_phase(k, groups):
    S, A, ps = k.S, k.A, k.ps
    A.reset(k.static_end)
    xb1 = A.t("xb", [128, 8, 1024], F32)
    xb = [xb1, xb1]
    hbs = [A.t(f"hb{i}", [128, 8, 1024], BF16) for i in range(2)]
    sq = [A.t(f"sq{i}", [128, 1024], BF16) for i in range(2)]
    rs = A.t("rs", [128, 1024], F32)
    tmp = [A.t(f"tmp{i}", [128, 1024], F32) for i in range(2)]
    win = A.t("win", [128, 5, 8, 128], BF16)
    wuq = A.t("wuq", [128, 16, 3, 128], BF16)
    wuk = A.t("wuk", [128, 8, 128], BF16)
    zq = A.t("zq", [128, 3, 1024], F32)
    zkv = A.t("zkv", [128, 1024], F32)
    kr = A.t("kr", [64, 2, 1024], F32)
    cqn = A.t("cqn", [128, 3, 1024], BF16)
    ckvn = A.t("ckvn", [128, 1024], F32)
    ckvb = A.t("ckvb", [128, 1024], BF16)
    krb = A.t("krb", [64, 1024], BF16)
    rq = A.t("rq", [128, 1024], F32)
    rc = A.t("rc", [64, 1024], F32)
    rsn = A.t("rsn", [64, 1024], F32)
    qn = [A.t(f"qn{i}", [128, 1024], BF16) for i in range(2)]
    qp = [A.t(f"qp{i}", [128, 1024], BF16) for i in range(2)]
    qr = [A.t(f"qr{i}", [64, 1024], BF16) for i in range(2)]
    ctok = [A.t(f"ctok{i}", [128, 128], BF16) for i in range(2)]
    onesQ = A.t("onesQ", [128, 128], BF16)
    onesK = A.t("onesK", [128, 128], BF16)
    gq = A.t("gq", [128, 3], F32)
    gkv = A.t("gkv", [128, 1], F32)
    t1 = [A.t(f"t1{i}", [64, 1024], F32) for i in range(2)]
    DMA(S, "sp", win[:], k.WIN1.rearrange("j p k m -> p j k m"), "win", writes=["win"])
    DMA(S, "sp", wuq[:], k.WUQ.rearrange("j p k m -> p j k m"), "wuq", writes=["wuq"])
    DMA(S, "sp", wuk[:], k.WUKT, "wuk", writes=["wuk"])
    DMA(S, "sp", gq[:], k.gq, "gq", writes=["gq"])
    DMA(S, "sp", gkv[:], k.gkv, "gkv", writes=["gkv"])
    MSET(S, onesQ[:], 1.0 / 384, ["onesQ"])
    MSET(S, onesK[:], 1.0 / 128, ["onesK"])
    bk = Banks([0, 1, 2, 3, 4, 5])
    ti = 0
    def ln(gi):
        t0, n, st = groups[gi]
        load_norm(k, k.X["X4"], t0, n, st, 1, 1, xb[gi % 2], sq, rs, tmp, hbs[gi % 2], gi % 2, xtag=0)

    ln(0)
    for gi, (t0, n, st) in enumerate(groups):
        b = gi % 2
        hb = hbs[b]
        if gi + 1 < len(groups):
            ln(gi + 1)
        nsub = (n + 511) // 512
        w = min(n, 512)
        if st == 0:
            DMA(S, "sp", rc[:, 0:n], k.ropec[:, t0:t0 + n], "rc", writes=["rc"])
            DMA(S, "sp", rsn[:, 0:n], k.ropes[:, t0:t0 + n], "rsn", writes=["rsn"])

        def lin5(j, m0, m1, dst, dkey):
            for sub in range(nsub):
                bank = bk.next()
                for kk in range(8):
                    MM(S, ps[0:m1 - m0, bank, 0:w], win[:, j, kk, m0:m1], hb[:, kk, sub * 512:sub * 512 + w], kk == 0, kk == 7,
                       ["win", ("hb", b, kk)], [("ps", bank)])
                copy_alt(k, dst[:, sub * 512:sub * 512 + w], ps[0:m1 - m0, bank, 0:w], [("ps", bank)], [dkey])

        for j in range(3):
            lin5(j, 0, 128, zq[:, j, :], ("zq", j))
        lin5(3, 0, 128, zkv[:, :], "zkv")
        lin5(4, 0, 64, kr[:, 0, :], ("kr", 0))
        lin5(4, 64, 128, kr[:, 1, :], ("kr", 1))
        for j in range(3):
            q = j % 2
            ACTF(S, sq[q][:, 0:n], zq[:, j, 0:n], AF.Square, [("zq", j)], [("sq", q)])
            for sub in range(nsub):
                MM(S, ps[:, 6 + sub, 0:w], onesQ[:], sq[q][:, sub * 512:sub * 512 + w], j == 0, j == 2, [("sq", q), "onesQ"], [("ps", 6 + sub)])
        ACTF(S, rq[:, 0:n], ps[:, 6:6 + nsub, 0:w].rearrange("p a n -> p (a n)"), AF.Sqrt, [("ps", 6 + i) for i in range(nsub)], ["rq"], bias=EPS)
        RCP(S, rq[:, 0:n], rq[:, 0:n], ["rq"], ["rq"])
        for j in range(3):
            STT(S, cqn[:, j, 0:n], zq[:, j, 0:n], gq[:, j:j + 1], rq[:, 0:n], ALU.mult, ALU.mult, [("zq", j), "gq", "rq"], [("cqn", j)])
        ACTF(S, sq[0][:, 0:n], zkv[:, 0:n], AF.Square, ["zkv"], [("sq", 0)])
        for sub in range(nsub):
            MM(S, ps[:, 6 + sub, 0:w], onesK[:], sq[0][:, sub * 512:sub * 512 + w], True, True, [("sq", 0), "onesK"], [("ps", 6 + sub)])
        ACTF(S, rq[:, 0:n], ps[:, 6:6 + nsub, 0:w].rearrange("p a n -> p (a n)"), AF.Sqrt, [("ps", 6 + i) for i in range(nsub)], ["rq"], bias=EPS)
        RCP(S, rq[:, 0:n], rq[:, 0:n], ["rq"], ["rq"])
        STT(S, ckvn[:, 0:n], zkv[:, 0:n], gkv[:, 0:1], rq[:, 0:n], ALU.mult, ALU.mult, ["zkv", "gkv", "rq"], ["ckvn"])
        CP(S, ckvb[:, 0:n], ckvn[:, 0:n], ["ckvn"], ["ckvb"])
        DMA(S, "sp", k.CKVT[:, t0:t0 + n], ckvb[:, 0:n], "ckvb", reads=["ckvb"])
        for tc in range(n // 128):
            bank = bk.next()
            q = tc % 2
            S.op("pe", (lambda tc=tc, bank=bank: (lambda e: e.transpose(ps[:, bank, 0:128], ckvn[:, tc * 128:(tc + 1) * 128], k.ident[:])))(),
                 ["ckvn", "ident"], [("ps", bank)])
            copy_alt(k, ctok[q][:], ps[:, bank, 0:128], [("ps", bank)], [("ctok", q)])
            DMA(S, "sp", k.CKVTOK[t0 // 128 + tc], ctok[q][:], f"ctok{q}", reads=[("ctok", q)])
        if st == 0:
            TT(S, t1[0][:, 0:n], kr[:, 0, 0:n], rc[:, 0:n], ALU.mult, [("kr", 0), "rc"], [("t1", 0)])
            TT(S, t1[1][:, 0:n], kr[:, 1, 0:n], rsn[:, 0:n], ALU.mult, [("kr", 1), "rsn"], [("t1", 1)])
            TT(S, krb[:, 0:n], t1[0][:, 0:n], t1[1][:, 0:n], ALU.add, [("t1", 0), ("t1", 1)], ["krb"])
        else:
            CP(S, krb[:, 0:n], kr[:, 0, 0:n], [("kr", 0)], ["krb"])
        DMA(S, "sp", k.KRT[0:64, t0:t0 + n], krb[:, 0:n], "krb", reads=["krb"])
        DMA(S, "sp", k.KRT[64:128, t0:t0 + n], krb[:, 0:n], "krb2", reads=["krb"])
        if st != 0:
            continue
        for h in range(8):
            q = h % 2
            for sub in range(nsub):
                bank = bk.next()
                for kk in range(3):
                    MM(S, ps[:, bank, 0:w], wuq[:, h, kk, :], cqn[:, kk, sub * 512:sub * 512 + w], kk == 0, kk == 2,
                       ["wuq", ("cqn", kk)], [("ps", bank)])
                copy_alt(k, qn[q][:, sub * 512:sub * 512 + w], ps[:, bank, 0:w], [("ps", bank)], [("qn", q)])
            for sub in range(nsub):
                bank = bk.next()
                MM(S, ps[:, bank, 0:w], wuk[:, h, :], qn[q][:, sub * 512:sub * 512 + w], True, True, ["wuk", ("qn", q)], [("ps", bank)])
                copy_alt(k, qp[q][:, sub * 512:sub * 512 + w], ps[:, bank, 0:w], [("ps", bank)], [("qp", q)])
            DMA(S, "sp", k.QP[h, :, t0:t0 + n], qp[q][:, 0:n], f"qp{q}", reads=[("qp", q)])
            hp, ho = h // 2, (h % 2) * 64
            for sub in range(nsub):
                ba, bb_ = bk.next(), bk.next()
                for part, bank in ((0, ba), (1, bb_)):
                    for kk in range(3):
                        MM(S, ps[0:64, bank, 0:w], wuq[:, 8 + 4 * part + hp, kk, ho:ho + 64], cqn[:, kk, sub * 512:sub * 512 + w],
                           kk == 0, kk == 2, ["wuq", ("cqn", kk)], [("ps", bank)])
                sl = slice(sub * 512, sub * 512 + w)
                TT(S, t1[0][:, sl], ps[0:64, ba, 0:w], rc[:, sl], ALU.mult, [("ps", ba), "rc"], [("t1", 0)])
                TT(S, t1[1][:, sl], ps[0:64, bb_, 0:w], rsn[:, sl], ALU.mult, [("ps", bb_), "rsn"], [("t1", 1)])
                TT(S, qr[q][:, sl], t1[0][:, sl], t1[1][:, sl], ALU.add, [("t1", 0), ("t1", 1)], [("qr", q)])
            DMA(S, "sp", k.QR[hp, ho:ho + 64, t0:t0 + n], qr[q][:, 0:n], f"qr{q}", reads=[("qr", q)])


def mla_phase(k, qtiles):
    S, A, ps = k.S, k.A, k.ps
    A.reset(k.static_end)
    ckvT = A.t("ckvT", [128, NT], BF16)
    krT = A.t("krT", [128, NT], BF16)
    ctok = A.t("ctok", [128, 34, 128], BF16)
    wuv = A.t("wuv", [128, 1024], BF16)
    wo = A.t("wo", [128, 8, 8, 128], BF16)
    qp = [A.t(f"qp{i}", [128, 8, 512], BF16) for i in range(2)]
    qr = [A.t(f"qr{i}", [128, 4, 512], BF16) for i in range(2)]
    xb = [A.t(f"xb{i}", [128, 8, 512], F32) for i in range(2)]
    oT = A.t("oT", [128, 8, 512], BF16)
    st_ = attn_bufs(A, 512)
    DMA(S, "sp", ckvT[:], k.CKVT, "ckvT", writes=["ckvT"])
    DMA(S, "sp", krT[:], k.KRT, "krT", writes=["krT"])
    DMA(S, "sp", ctok[:], k.CKVTOK.rearrange("c p r -> p c r"), "ctok", writes=["ctok"])
    DMA(S, "sp", wuv[:], k.WUV, "wuv", writes=["wuv"])
    DMA(S, "sp", wo[:], k.WO.rearrange("j p k m -> p j k m"), "wo", writes=["wo"])
    chunks = list(range(34))
    scale = float(192 ** -0.5)
    for qi, q0 in enumerate(qtiles):
        b = qi % 2
        DMA(S, "sp", qp[b][:], k.QP[:, :, q0:q0 + 512].rearrange("h p t -> p h t"), f"qp{b}", writes=[("qp", b)])
        DMA(S, "sp", qr[b][:], k.QR[:, :, q0:q0 + 512].rearrange("h p t -> p h t"), f"qr{b}", writes=[("qr", b)])
        DMA(S, "sp", xb[b][:], k.X["X4"][:, q0:q0 + 512].rearrange("(j p) t -> p j t", p=128), f"xb{b}",
            writes=[("xb", b, j) for j in range(8)])
        def tail(h, on_ap, on_key):
            MM(S, ps[:, 7, :], wuv[:, h * 128:(h + 1) * 128], on_ap, True, True, ["wuv", on_key], [("ps", 7)])
            copy_alt(k, oT[:, h, :], ps[:, 7, :], [("ps", 7)], [("oT", h)])

        for h in range(8):
            hp, ho = h // 2, (h % 2) * 64
            on_ap, on_key = attention(
                k, st_, 512, [(qp[b][:, h, :], ("qp", b)), (qr[b][ho:ho + 64, hp, :], ("qr", b))],
                lambda kc, ho=ho: [(ckvT[:, kc * 128:(kc + 1) * 128], "ckvT"), (krT[ho:ho + 64, kc * 128:(kc + 1) * 128], "krT")],
                lambda kc: (ctok[:, kc, :], "ctok"), 128, chunks, scale, None)
            if h < 7:
                st_["pending"] = (lambda h=h, a=on_ap, kk_=on_key: tail(h, a, kk_))
            else:
                tail(h, on_ap, on_key)
        for j in range(8):
            bank = 7
            for kk in range(8):
                MM(S, ps[:, bank, :], wo[:, j, kk, :], oT[:, kk, :], kk == 0, kk == 7, ["wo", ("oT", kk)], [("ps", bank)])
            STT(S, xb[b][:, j, :], ps[:, bank, :], k.Gh[:, 1, 1, 2 * j:2 * j + 1], xb[b][:, j, :], ALU.mult, ALU.add,
                [("ps", bank), ("xb", b, j)], [("xb", b, j)])
        DMA(S, "sp", k.X["X5"][:, q0:q0 + 512].rearrange("(j p) t -> p j t", p=128), xb[b][:], f"xs{b}",
            reads=[("xb", b, j) for j in range(8)])


def fm(v):
    v = np.asarray(v, np.float32)
    lead = v.shape[:-1]
    return np.ascontiguousarray(np.moveaxis(v.reshape(lead + (8, 128)), -1, 0))


_CONST = {}


def consts():
    if _CONST:
        return _CONST
    c = {}
    c["ident"] = np.eye(128, dtype=np.float32)
    c["kvec"] = np.ascontiguousarray(np.broadcast_to(np.arange(4096, dtype=np.float32) / np.float32(4096.0), (128, 4096)))
    c["tcol"] = (np.arange(32)[None, :] * 128 + np.arange(128)[:, None]).astype(np.float32)
    cc = np.arange(128, dtype=np.float64)
    ang = 2 * np.pi * np.outer(cc, cc) / 128.0
    cs = np.concatenate([-np.cos(ang), np.sin(ang)], 1)
    c["csl"] = (cs / np.sqrt(4096.0 * 128.0)).astype(np.float32)
    c["csc"] = (cs / np.sqrt(256.0 * 128.0)).astype(np.float32)
    pos = np.arange(NL)
    row = (pos // 64).astype(np.float32)
    col = (pos % 64).astype(np.float32)
    inv = (np.float32(10000.0) ** (-np.arange(0, 32, 2, dtype=np.float32) / np.float32(32))).astype(np.float32)
    ang_r = row[:, None] * inv[None]
    ang_c = col[:, None] * inv[None]
    ang = np.concatenate([ang_r, ang_r, ang_c, ang_c], -1)
    sign = np.ones(64, np.float32)
    sign[0:16] = -1
    sign[32:48] = -1
    c["ropec"] = np.ascontiguousarray(np.cos(ang).T.astype(np.float32))
    c["ropes"] = np.ascontiguousarray((np.sin(ang) * sign[None]).T.astype(np.float32))
    _CONST.update(c)
    return c


ROPE_PERM = np.concatenate([np.arange(16, 32), np.arange(0, 16), np.arange(48, 64), np.arange(32, 48)])


def na_bias(rpb):
    out = np.full((3, 8, 8, 128, 512), -30000.0, np.float32)
    p = np.arange(128)
    qq = np.arange(512)
    for pat, i in enumerate((0, 1, 7)):
        krow0 = int(np.clip(8 * i - 4, 0, 48))
        qrow = 8 * i + qq // 64
        qcol = qq % 64
        r0 = np.clip(qrow - 4, 0, 56)
        cs = np.clip(qcol - 8, 0, 48)
        for jj in range(8):
            krow = krow0 + 2 * jj + p // 64
            kcol = p % 64
            valid = ((krow[:, None] >= r0[None]) & (krow[:, None] < r0[None] + 8)
                     & (kcol[:, None] >= cs[None]) & (kcol[:, None] < cs[None] + 16))
            ri = np.clip(krow[:, None] - qrow[None] + 7, 0, 14)
            ci = np.clip(kcol[:, None] - qcol[None] + 15, 0, 30)
            for h in range(8):
                g = rpb[h][ri, ci]
                out[pat, h, jj] = np.where(valid, g, np.float32(-30000.0))
    return out


def prep_inputs(inp, b):
    x0 = np.concatenate([inp["x"][b].T, inp["ctx"][b].T], axis=1)
    sc = np.stack([inp["c"][b], inp["c_ctx"]], -1).reshape(8, 128, 2).transpose(1, 0, 2)
    adb = fm(inp["ada_b"].reshape(2, 9, D))
    adb = np.repeat(adb[..., None], 2, -1).reshape(128, 2, 9, 16)
    ng = fm(inp["norm_g"])
    ng = np.repeat(ng[..., None], 2, -1).reshape(128, 2, 3, 16)
    w_in = inp["mla_w_in"][0]
    w_in_ext = np.concatenate([w_in, w_in[:, 512:576][:, ROPE_PERM]], 1)
    wq = inp["mla_w_uq"][0].reshape(384, 8, 192)
    wq_ext = np.concatenate([wq[:, :, :128].reshape(384, 1024), wq[:, :, 128:].reshape(384, 512),
                             wq[:, :, 128:][:, :, ROPE_PERM].reshape(384, 512)], 1)
    wukT = inp["mla_w_uk"][0].reshape(128, 8, 128).transpose(1, 2, 0)
    m = dict(
        x0=np.ascontiguousarray(x0, np.float32), sc=np.ascontiguousarray(sc, np.float32),
        ada_w=inp["ada_w"], adb=np.ascontiguousarray(adb), ng=np.ascontiguousarray(ng),
        fg=fm(inp["final_g"]),
        ffn_w_gate=inp["ffn_w_gate"], ffn_w_up=inp["ffn_w_up"], ffn_w_down=inp["ffn_w_down"],
        ab_w_in=inp["ab_w_in"][0], ab_w_out=inp["ab_w_out"][0],
        mla_w_in=np.ascontiguousarray(w_in_ext), mla_w_uq=np.ascontiguousarray(wq_ext),
        mla_w_ukT=np.ascontiguousarray(wukT), mla_w_uv=inp["mla_w_uv"][0], mla_w_o=inp["mla_w_o"][0],
        gq=np.ascontiguousarray(inp["mla_g_q"][0].reshape(3, 128).T), gkv=np.ascontiguousarray(inp["mla_g_kv"][0].reshape(1, 128).T),
    )
    m.update(consts())
    return m


def kernel(**inputs):
    inp = {k_: np.asarray(v) for k_, v in inputs.items()}
    nc = build()
    nab = na_bias(inp["ab_rpb"][0])
    in_maps = [prep_inputs(inp, b) for b in range(8)]
    for m in in_maps:
        m["nabias"] = nab
    res = run_bass_kernel_spmd(nc, in_maps, core_ids=list(range(8)))
    out = np.stack([np.ascontiguousarray(r["outT"].T) for r in res.results], 0)
    return out.astype(np.float32)
```

```python
import contextlib
import numpy as np
import concourse.bass as bass
import concourse.mybir as mybir
from concourse.bass_utils import run_bass_kernel_spmd

F32 = mybir.dt.float32
BF16 = mybir.dt.bfloat16
AF = mybir.ActivationFunctionType
ALU = mybir.AluOpType

D = 1024
NL = 4096
NCX = 256
NT = NL + NCX
DFF = 2816
NF = DFF // 128
EPS = 1e-6
SEM_CHUNK = 30000


class Sched:
    ENG = ("pe", "act", "dve", "pool", "sp")

    def __init__(self, nc):
        self.nc = nc
        self.ops = {e: [] for e in self.ENG}
        self.res = {}
        self.dma_cnt = {}
        self.seen = {e: {} for e in self.ENG}

    def _collect(self, eng, reads, writes):
        need = {}

        def add(src, val, raw):
            if src[0] == "e" and src[1] == eng:
                if eng in ("pe", "sp"):
                    return
            if need.get(src, 0) < val:
                need[src] = val

        for k in reads:
            st = self.res.get(k)
            if st and st[0] is not None:
                add(st[0][0], st[0][1], True)
        for k in writes:
            st = self.res.get(k)
            if st:
                if st[0] is not None:
                    add(st[0][0], st[0][1], False)
                for src, val in st[1].items():
                    add(src, val, False)
        return self._filter(eng, need)

    def _filter(self, eng, need):
        waits = []
        seen = self.seen[eng]
        for src, val in need.items():
            if seen.get(src, 0) >= val:
                continue
            seen[src] = val
            waits.append((src, val))
            if src[0] == "e":
                self.ops[src[1]][val - 1]["signal"] = True
        return waits

    def _update(self, ref, reads, writes):
        src, val = ref
        for k in writes:
            self.res[k] = [ref, {}]
        for k in reads:
            st = self.res.setdefault(k, [None, {}])
            if st[1].get(src, 0) < val:
                st[1][src] = val

    def op(self, eng, fn, reads=(), writes=()):
        waits = self._collect(eng, reads, writes)
        lst = self.ops[eng]
        lst.append(dict(fn=fn, waits=waits, signal=False, dma=None))
        self._update((("e", eng), len(lst)), reads, writes)

    def dma(self, eng, fn, sem, reads=(), writes=()):
        waits = self._collect(eng, reads, writes)
        sem = f"{eng}_{sem}"
        cnt = self.dma_cnt.get(sem, 0) + 1
        assert cnt * 16 < SEM_CHUNK
        self.dma_cnt[sem] = cnt
        self.ops[eng].append(dict(fn=fn, waits=waits, signal=False, dma=sem))
        self._update((("d", sem), cnt), reads, writes)

    def barrier(self):
        last = {}
        for f in self.ENG:
            idx = 0
            for i, o in enumerate(self.ops[f]):
                if o["dma"] is None and o["fn"] is not None:
                    idx = i + 1
            last[f] = idx
        need = {("e", f): last[f] for f in self.ENG if f != "sp" and last[f] > 0}
        for k, c in self.dma_cnt.items():
            need[("d", k)] = c
        waits = self._filter("sp", need)
        self.ops["sp"].append(dict(fn=lambda e: e.nop(), waits=waits, signal=False, dma=None))
        idx = len(self.ops["sp"])
        for e in self.ENG:
            if e != "sp":
                w = self._filter(e, {("e", "sp"): idx})
                self.ops[e].append(dict(fn=None, waits=w, signal=False, dma=None))
        self.res = {}

    def emit(self):
        nc = self.nc
        with contextlib.ExitStack() as es:
            esem = {}
            sigpos = {}
            for e in self.ENG:
                c = 0
                pos = []
                for o in self.ops[e]:
                    if o["signal"]:
                        c += 1
                    pos.append(c)
                sigpos[e] = pos
                nsem = max(1, (c + SEM_CHUNK - 1) // SEM_CHUNK)
                esem[e] = [es.enter_context(nc.semaphore(f"s_{e}_{i}")) for i in range(nsem)]
            dsem = {k: es.enter_context(nc.semaphore(f"d_{k}")) for k in self.dma_cnt}
            block = es.enter_context(nc.Block())

            def run(e, engobj):
                cnt = 0
                for o in self.ops[e]:
                    for src, val in o["waits"]:
                        if src[0] == "e":
                            c = sigpos[src[1]][val - 1]
                            assert c > 0
                            engobj.wait_ge(esem[src[1]][(c - 1) // SEM_CHUNK], (c - 1) % SEM_CHUNK + 1)
                        else:
                            engobj.wait_ge(dsem[src[1]], 16 * val)
                    if o["fn"] is None:
                        continue
                    ins = o["fn"](engobj)
                    if o["dma"] is not None:
                        ins.then_inc(dsem[o["dma"]], 16)
                    elif o["signal"]:
                        cnt += 1
                        ins.then_inc(esem[e][(cnt - 1) // SEM_CHUNK], 1)

            @block.tensor
            def _(eng):
                run("pe", eng)

            @block.scalar
            def _(eng):
                run("act", eng)

            @block.vector
            def _(eng):
                run("dve", eng)

            @block.gpsimd
            def _(eng):
                run("pool", eng)

            @block.sync
            def _(eng):
                run("sp", eng)


class Arena:
    def __init__(self, nc, base, size):
        self.nc, self.base, self.size, self.off, self.n = nc, base, size, 0, 0

    def reset(self, off=0):
        self.off = off

    def t(self, name, shape, dt):
        nb = int(np.prod(shape[1:])) * (4 if dt == F32 else 2)
        nb = (nb + 63) // 64 * 64
        assert self.off + nb <= self.size, (name, self.off, nb, self.size)
        self.n += 1
        h = self.nc.alloc_sbuf_tensor_at(f"{name}_{self.n}", list(shape), dt, offset=self.base + self.off)
        self.off += nb
        return h


class K:
    pass


def build(only=None, dbg=(), small=False, xin=()):
    nc = bass.Bass("TRN2", target_bir_lowering=False)
    k = K()
    k.nc = nc

    def din(name, shape, dt=F32):
        return nc.dram_tensor(name, list(shape), dt, kind="ExternalInput").ap()

    def dscr(name, shape, dt=BF16):
        kind = "ExternalOutput" if name in dbg else "Internal"
        return nc.dram_tensor(name, list(shape), dt, kind=kind).ap()

    k.x0 = din("x0", [D, NT])
    k.sc = din("sc", [128, 8, 2])
    k.ada_w = din("ada_w", [2, D, 9 * D])
    k.adb = din("adb", [128, 2, 9, 16])
    k.ng = din("ng", [128, 2, 3, 16])
    k.fg = din("fg", [128, 8])
    k.w_gate = din("ffn_w_gate", [2, 2, D, DFF])
    k.w_up = din("ffn_w_up", [2, 2, D, DFF])
    k.w_down = din("ffn_w_down", [2, 2, DFF, D])
    k.out = nc.dram_tensor("outT", [D, NL], F32, kind="ExternalOutput").ap()
    k.ab_w_in = din("ab_w_in", [D, 2048])
    k.ab_w_out = din("ab_w_out", [D, D])
    k.mla_w_in = din("mla_w_in", [D, 640])
    k.mla_w_uq = din("mla_w_uq", [384, 2048])
    k.mla_w_ukT = din("mla_w_ukT", [8, 128, 128])
    k.mla_w_uv = din("mla_w_uv", [128, 1024])
    k.mla_w_o = din("mla_w_o", [D, D])
    k.gq = din("gq", [128, 3])
    k.gkv = din("gkv", [128, 1])
    k.ident_d = din("ident", [128, 128])
    k.kvec = din("kvec", [128, 4096])
    k.tcol = din("tcol", [128, 32])
    k.csl = din("csl", [128, 256])
    k.csc = din("csc", [128, 256])
    k.ropec = din("ropec", [64, NL])
    k.ropes = din("ropes", [64, NL])
    k.nabias = din("nabias", [3, 8, 8, 128, 512])

    k.X = {}
    for n_ in ("X1", "X2", "X3", "X4", "X5", "X6"):
        if n_ in xin:
            k.X[n_] = din(n_, [D, NT])
        else:
            k.X[n_] = dscr(n_, [D, NT], F32)
    k.wgu = {(l, s): dscr(f"wgu{l}{s}", [NF, 128, 2, 8, 128]) for l in range(2) for s in range(2)}
    k.wd = {(l, s): dscr(f"wd{l}{s}", [8, 128, NF, 128]) for l in range(2) for s in range(2)}
    k.WIN0 = dscr("WIN0", [16, 128, 8, 128])
    k.WV0 = dscr("WV0", [128, 8, 512])
    k.WOUT0 = dscr("WOUT0", [8, 128, 8, 128])
    k.QT0 = dscr("QT0", [4, 128, NT])
    k.KT0 = dscr("KT0", [4, 128, NT])
    k.VT0 = dscr("VT0", [34, 128, 8, 128])
    k.UCS = dscr("UCS", [34, 128, 4, 256])
    k.CAT = dscr("CAT", [8, 128, NT])
    k.NAB = dscr("NAB", [3, 8, 128, 8, 512])
    k.WIN1 = dscr("WIN1", [5, 128, 8, 128])
    k.WUQ = dscr("WUQ", [16, 128, 3, 128])
    k.WUKT = dscr("WUKT", [128, 8, 128])
    k.WUV = dscr("WUV", [128, 1024])
    k.WO = dscr("WO", [8, 128, 8, 128])
    k.QP = dscr("QP", [8, 128, NL])
    k.QR = dscr("QR", [4, 128, NL])
    k.CKVT = dscr("CKVT", [128, NT])
    k.KRT = dscr("KRT", [128, NT])
    k.CKVTOK = dscr("CKVTOK", [34, 128, 128])

    arena_h = nc.alloc_sbuf_tensor("arena", [128, 204800], mybir.dt.uint8)
    base = nc.lookup_mloc(arena_h).addr
    A = Arena(nc, base, 204800)
    k.A = A
    k.ps = nc.alloc_psum_tensor("ps", [128, 8, 512], F32)
    S = Sched(nc)
    k.S = S

    k.ident = A.t("ident", [128, 128], F32)
    k.onesD = A.t("onesD", [128, 128], BF16)
    k.ones = A.t("ones", [128, 128], BF16)
    k.identb = A.t("identb", [128, 128], BF16)
    k.junk = A.t("junk", [128, 512], BF16)
    k.mod = A.t("mod", [128, 2, 9, 16], F32)
    k.Am = A.t("Am", [128, 2, 3, 16], F32)
    k.Gh = A.t("Gh", [128, 2, 3, 16], F32)
    k.fgt = A.t("fgt", [128, 8], F32)
    k.ngt = A.t("ngt", [128, 2, 3, 16], F32)
    k.adbt = A.t("adbt", [128, 2, 9, 16], F32)
    k.static_end = 8192
    assert A.off <= k.static_end

    lat_groups = [(g * 1024, 1024, 0) for g in range(4)]
    na_q = [(i * 512, 512, "lat") for i in range(8)] + [(NL, NCX, "ctx")]
    kts = list(range(8))
    mla_q = [i * 512 for i in range(8)]
    if small:
        lat_groups = [(0, 1024, 0)]
        na_q = [(0, 512, "lat"), (NL, NCX, "ctx")]
        kts = [0]
        mla_q = [0]
    if small == 2:
        lat_groups = [(g * 1024, 1024, 0) for g in range(4)]
    all_groups = lat_groups + [(NL, NCX, 1)]

    def casts_mixer0():
        cast_lin(k, k.ab_w_in, k.WIN0, D, 2048)
        DMA(S, "pool", k.WV0, k.ab_w_in[:, 1536:2048].rearrange("(k p) n -> p k n", p=128), "cast")
        cast_lin(k, k.ab_w_out, k.WOUT0, D, D)
        for pat in range(3):
            for h in range(8):
                DMA(S, "pool", k.NAB[pat, h], k.nabias[pat, h].rearrange("j p q -> p j q"), "cast")

    def casts_l1():
        cast_lin(k, k.mla_w_in, k.WIN1, D, 640)
        cast_lin(k, k.mla_w_uq, k.WUQ, 384, 2048)
        DMA(S, "pool", k.WUKT, k.mla_w_ukT.rearrange("h d r -> d h r"), "cast")
        DMA(S, "pool", k.WUV, k.mla_w_uv, "cast")
        cast_lin(k, k.mla_w_o, k.WO, D, D)

    def all_casts():
        casts_mixer0()
        casts_l1()
        for l_, s_ in ((0, 1), (1, 0), (1, 1)):
            cast_ffn(k, l_, s_)

    full = only is None
    plan = [
        ("setup", lambda: phase_setup(k, casts_mixer0 if full else all_casts)),
        ("ffn00", lambda: ffn_phase(k, 0, 0, k.x0, k.X["X1"], all_groups,
                                    (lambda: (cast_ffn(k, 0, 1), cast_ffn(k, 1, 0))) if full else None)),
        ("proj0", lambda: proj0_phase(k, all_groups, (lambda: (casts_l1(), cast_ffn(k, 1, 1))) if full else None)),
        ("na", lambda: na_phase(k, na_q)),
        ("fnet", lambda: fnet_phase(k, kts)),
        ("wout", lambda: lin_res_phase(k, k.X["X1"], k.X["X2"], k.CAT, k.WOUT0, 0, all_groups)),
        ("ffn01", lambda: ffn_phase(k, 0, 1, k.X["X2"], k.X["X3"], all_groups, None)),
        ("ffn10", lambda: ffn_phase(k, 1, 0, k.X["X3"], k.X["X4"], all_groups, None)),
        ("proj1", lambda: proj1_phase(k, all_groups)),
        ("mla", lambda: mla_phase(k, mla_q)),
        ("ffn11", lambda: ffn_phase(k, 1, 1, k.X["X5"], k.X["X6"], lat_groups, None)),
        ("final", lambda: final_phase(k, k.X["X6"], lat_groups)),
    ]
    for name, fn in plan:
        if only is not None and name not in only:
            continue
        fn()
        S.barrier()
    global LAST_S
    LAST_S = S
    S.emit()
    return nc


def MM(S, out, lhsT, rhs, start, stop, reads, writes):
    S.op("pe", lambda e: e.matmul(out, lhsT=lhsT, rhs=rhs, start=start, stop=stop), reads, writes)


def ACTF(S, out, in_, func, reads, writes, scale=1.0, bias=0.0):
    S.op("act", lambda e: e.activation(out=out, in_=in_, func=func, scale=scale, bias=bias), reads, writes)


def TT(S, out, in0, in1, op, reads, writes, eng="dve"):
    S.op(eng, lambda e: e.tensor_tensor(out=out, in0=in0, in1=in1, op=op), reads, writes)


def STT(S, out, in0, scalar, in1, op0, op1, reads, writes, eng="dve"):
    S.op(eng, lambda e: e.scalar_tensor_tensor(out=out, in0=in0, scalar=scalar, in1=in1, op0=op0, op1=op1), reads, writes)


def TS(S, out, in0, s1, op0, reads, writes, s2=None, op1=None, eng="dve"):
    if op1 is None:
        S.op(eng, lambda e: e.tensor_scalar(out=out, in0=in0, scalar1=s1, scalar2=None, op0=op0), reads, writes)
    else:
        S.op(eng, lambda e: e.tensor_scalar(out=out, in0=in0, scalar1=s1, scalar2=s2, op0=op0, op1=op1), reads, writes)


def CP(S, out, in_, reads, writes, eng="dve"):
    S.op(eng, lambda e: e.tensor_copy(out=out, in_=in_), reads, writes)


def RCP(S, out, in_, reads, writes):
    S.op("dve", lambda e: e.reciprocal(out=out, in_=in_), reads, writes)


def MSET(S, ap, val, writes, eng="dve"):
    S.op(eng, lambda e: e.memset(ap, val), (), writes)


def DMA(S, eng, out, in_, sem, reads=(), writes=()):
    S.dma(eng, lambda e: e.dma_start(out=out, in_=in_), sem, reads, writes)


def phase_setup(k, extra_casts=None):
    nc, S, A, ps = k.nc, k.S, k.A, k.ps
    A.reset(k.static_end)
    sc_f = A.t("sc_f", [128, 8, 2], F32)
    sc_b = A.t("sc_b", [128, 8, 2], BF16)
    adw = [A.t(f"adw{i}", [128, 8, 1024], BF16) for i in range(2)]
    cast_ffn(k, 0, 0)
    if extra_casts:
        extra_casts()
    DMA(S, "sp", k.ident[:], k.ident_d, "m_id", writes=["ident"])
    DMA(S, "sp", sc_f[:], k.sc, "m_sc", writes=["sc_f"])
    DMA(S, "sp", k.adbt[:], k.adb, "m_adb", writes=["adbt"])
    DMA(S, "sp", k.ngt[:], k.ng, "m_ng", writes=["ngt"])
    DMA(S, "sp", k.fgt[:], k.fg, "m_fg", writes=["fgt"])
    MSET(S, k.onesD[:], 1.0 / D, ["onesD"])
    MSET(S, k.ones[:], 1.0, ["ones"])
    MSET(S, k.junk[:], 1.0, ["junk"])
    CP(S, k.identb[:], k.ident[:], ["ident"], ["identb"])
    ACTF(S, sc_b[:], sc_f[:], AF.Silu, ["sc_f"], ["sc_b"])
    n = 0
    for l in range(2):
        for i in range(9):
            b = n % 2
            DMA(S, "pool", adw[b][:], k.ada_w[l, :, i * 1024:(i + 1) * 1024].rearrange("(k p) n -> p k n", p=128),
                f"adw{b}", writes=[("adw", b)])
            bank = n % 2
            for j in range(8):
                for kk in range(8):
                    MM(S, ps[:, bank, j * 2:j * 2 + 2], adw[b][:, kk, j * 128:(j + 1) * 128], sc_b[:, kk, :],
                       kk == 0, kk == 7, [("adw", b), "sc_b"], [("ps", bank)])
            TT(S, k.mod[:, l, i, :], ps[:, bank, 0:16], k.adbt[:, l, i, :], ALU.add, [("ps", bank), "adbt"], ["mod"])
            n += 1
    for l in range(2):
        for i in range(3):
            STT(S, k.Am[:, l, i, :], k.mod[:, l, 3 * i + 1, :], 1.0, k.ngt[:, l, i, :], ALU.add, ALU.mult,
                ["mod", "ngt"], ["Am"])
            TS(S, k.Gh[:, l, i, :], k.mod[:, l, 3 * i + 2, :], (1.0 if i == 1 else 0.5), ALU.mult, ["mod"], ["Gh"])


def cast_ffn(k, l, s):
    S = k.S
    for f in range(NF):
        for gi, W in enumerate((k.w_gate, k.w_up)):
            DMA(S, "pool", k.wgu[(l, s)][f, :, gi],
                W[l, s, :, f * 128:(f + 1) * 128].rearrange("(k p) m -> p k m", p=128), "cast")
    for dj in range(8):
        DMA(S, "pool", k.wd[(l, s)][dj],
            k.w_down[l, s, :, dj * 128:(dj + 1) * 128].rearrange("(k p) m -> p k m", p=128), "cast")


def norm_stats(k, xb_j, n, sq, rs, tag):
    S, ps = k.S, k.ps
    nsub = (n + 511) // 512
    w = min(n, 512)
    for j in range(8):
        q = j % 2
        ap, key = xb_j(j)
        ACTF(S, sq[q][:, 0:n], ap, AF.Square, [key], [("sq", q)])
        for sub in range(nsub):
            MM(S, ps[:, 6 + sub, 0:w], k.onesD[:], sq[q][:, sub * 512:sub * 512 + w], j == 0, j == 7,
               [("sq", q), "onesD"], [("ps", 6 + sub)])
    ACTF(S, rs[:, 0:n], ps[:, 6:6 + nsub, 0:w].rearrange("p a n -> p (a n)"), AF.Sqrt,
         [("ps", 6 + i) for i in range(nsub)], [tag], bias=EPS)
    RCP(S, rs[:, 0:n], rs[:, 0:n], [tag], [tag])


def ffn_phase(k, l, s, Xsrc, Xdst, groups, pre_casts=None):
    nc, S, A, ps = k.nc, k.S, k.A, k.ps
    A.reset(k.static_end)
    xb = [A.t(f"xb{i}", [128, 8, 1024], F32) for i in range(2)]
    hb = A.t("hb", [128, 8, 1024], BF16)
    act = A.t("act", [128, NF, 1024], BF16)
    sq = [A.t(f"sq{i}", [128, 1024], BF16) for i in range(2)]
    rs = [A.t(f"rs{i}", [128, 1024], F32) for i in range(2)]
    tmp = [A.t(f"tmp{i}", [128, 1024], F32) for i in range(2)]
    sg = [A.t(f"sg{i}", [128, 512], F32) for i in range(2)]
    gu = [A.t(f"gu{i}", [128, 2, 8, 128], BF16) for i in range(3)]
    wdt = [A.t(f"wd{i}", [128, NF, 128], BF16) for i in range(3)]
    isub = 0 if s == 0 else 2
    wgu_scr, wd_scr = k.wgu[(l, s)], k.wd[(l, s)]
    ng = len(groups)

    events = []
    for gi in range(ng):
        events += [("gu", f) for f in range(NF)] + [("wd", dj) for dj in range(8)]
    cnt = {"gu": 0, "wd": 0}
    slot_of = []
    for kind, idx in events:
        slot_of.append(cnt[kind] % 3)
        cnt[kind] += 1
    issued = [0]

    def issue_upto(i):
        while issued[0] <= min(i, len(events) - 1):
            ev = issued[0]
            kind, idx = events[ev]
            sl = slot_of[ev]
            if kind == "gu":
                DMA(S, "sp", gu[sl][:], wgu_scr[idx], f"gu{sl}", writes=[("gu", sl)])
            else:
                DMA(S, "sp", wdt[sl][:], wd_scr[idx], f"wd{sl}", writes=[("wd", sl)])
            issued[0] += 1

    def load_x(gi):
        t0, n, st = groups[gi]
        b = gi % 2
        DMA(S, "pool", xb[b][:, :, 0:n], Xsrc[:, t0:t0 + n].rearrange("(j p) t -> p j t", p=128),
            f"xb{b}", writes=[("xb", b, j) for j in range(8)])

    def stats(gi):
        t0, n, st = groups[gi]
        b = gi % 2
        norm_stats(k, lambda j: (xb[b][:, j, 0:n], ("xb", b, j)), n, sq, rs[b], ("rs", b))

    def hpiece(gi, j):
        t0, n, st = groups[gi]
        b = gi % 2
        q = j % 2
        c = j * 2 + st
        TT(S, tmp[q][:, 0:n], xb[b][:, j, 0:n], rs[b][:, 0:n], ALU.mult, [("xb", b, j), ("rs", b)], [("tmp", q)])
        ACTF(S, hb[:, j, 0:n], tmp[q][:, 0:n], AF.Identity, [("tmp", q), "Am", "mod"], [("hb", j)],
             scale=k.Am[:, l, isub, c:c + 1], bias=k.mod[:, l, 3 * isub, c:c + 1])

    def do_group(gi):
        t0, n, st = groups[gi]
        b = gi % 2
        nsub = (n + 511) // 512
        w = min(n, 512)
        for f in range(NF):
            ev = gi * 30 + f
            issue_upto(ev + 2)
            sl = slot_of[ev]
            for sub in range(nsub):
                it = f * nsub + sub
                bg, bu = 2 * (it % 3), 2 * (it % 3) + 1
                for half, bank in ((0, bg), (1, bu)):
                    for kk in range(8):
                        MM(S, ps[:, bank, 0:w], gu[sl][:, half, kk, :], hb[:, kk, sub * 512:sub * 512 + w],
                           kk == 0, kk == 7, [("gu", sl), ("hb", kk)], [("ps", bank)])
                q = it % 2
                ACTF(S, sg[q][:, 0:w], ps[:, bg, 0:w], AF.Silu, [("ps", bg)], [("sg", q)])
                TT(S, act[:, f, sub * 512:sub * 512 + w], sg[q][:, 0:w], ps[:, bu, 0:w], ALU.mult,
                   [("sg", q), ("ps", bu)], [("act", f)])
        if gi + 1 < ng:
            stats(gi + 1)
        for dj in range(8):
            ev = gi * 30 + NF + dj
            issue_upto(ev + 2)
            sl = slot_of[ev]
            c = dj * 2 + st
            for sub in range(nsub):
                by = (dj * nsub + sub) % 6
                for kf in range(NF):
                    MM(S, ps[:, by, 0:w], wdt[sl][:, kf, :], act[:, kf, sub * 512:sub * 512 + w],
                       kf == 0, kf == NF - 1, [("wd", sl), ("act", kf)], [("ps", by)])
                xs = xb[b][:, dj, sub * 512:sub * 512 + w]
                STT(S, xs, ps[:, by, 0:w], k.Gh[:, l, isub, c:c + 1], xs, ALU.mult, ALU.add,
                    [("ps", by), ("xb", b, dj), "Gh"], [("xb", b, dj)])
            if gi + 1 < ng:
                hpiece(gi + 1, dj)
        DMA(S, "sp", Xdst[:, t0:t0 + n].rearrange("(j p) t -> p j t", p=128), xb[b][:, :, 0:n],
            f"xs{b}", reads=[("xb", b, j) for j in range(8)])
        if gi + 2 < ng:
            load_x(gi + 2)

    load_x(0)
    if ng > 1:
        load_x(1)
    if pre_casts:
        pre_casts()
    issue_upto(1)
    stats(0)
    for j in range(8):
        hpiece(0, j)
    for gi in range(ng):
        do_group(gi)


def final_phase(k, Xsrc, groups):
    S, A = k.S, k.A
    A.reset(k.static_end)
    xb = [A.t(f"xb{i}", [128, 8, 1024], F32) for i in range(2)]
    sq = [A.t(f"sq{i}", [128, 1024], BF16) for i in range(2)]
    rs = [A.t(f"rs{i}", [128, 1024], F32) for i in range(2)]

    def grp(gi):
        t0, n, st = groups[gi]
        b = gi % 2
        DMA(S, "sp", xb[b][:, :, 0:n], Xsrc[:, t0:t0 + n].rearrange("(j p) t -> p j t", p=128),
            f"xb{b}", writes=[("xb", b, j) for j in range(8)])
        norm_stats(k, lambda j: (xb[b][:, j, 0:n], ("xb", b, j)), n, sq, rs[b], ("rs", b))
        for j in range(8):
            STT(S, xb[b][:, j, 0:n], xb[b][:, j, 0:n], k.fgt[:, j:j + 1], rs[b][:, 0:n], ALU.mult, ALU.mult,
                [("xb", b, j), ("rs", b), "fgt"], [("xb", b, j)])
        DMA(S, "sp", k.out[:, t0:t0 + n].rearrange("(j p) t -> p j t", p=128), xb[b][:, :, 0:n],
            f"xs{b}", reads=[("xb", b, j) for j in range(8)])

    for gi in range(len(groups)):
        grp(gi)


def cast_lin(k, src, dst, K_, N_):
    for j in range(N_ // 128):
        DMA(k.S, "pool", dst[j], src[:, j * 128:(j + 1) * 128].rearrange("(k p) m -> p k m", p=128), "cast")


class Banks:
    def __init__(self, lst):
        self.lst, self.i = lst, 0

    def next(self):
        b = self.lst[self.i % len(self.lst)]
        self.i += 1
        return b


def load_norm(k, Xsrc, t0, n, st, l, isub, xb, sq, rs, tmp, hb, tag, xtag=None):
    S = k.S
    xt = tag if xtag is None else xtag
    DMA(S, "sp", xb[:, :, 0:n], Xsrc[:, t0:t0 + n].rearrange("(j p) t -> p j t", p=128), f"xb{xt}",
        writes=[("xb", xt, j) for j in range(8)])
    norm_stats(k, lambda j: (xb[:, j, 0:n], ("xb", xt, j)), n, sq, rs, ("rs", 0))
    for j in range(8):
        q = j % 2
        c = j * 2 + st
        TT(S, tmp[q][:, 0:n], xb[:, j, 0:n], rs[:, 0:n], ALU.mult, [("xb", xt, j), ("rs", 0)], [("tmp", q)])
        ACTF(S, hb[:, j, 0:n], tmp[q][:, 0:n], AF.Identity, [("tmp", q)], [("hb", tag, j)],
             scale=k.Am[:, l, isub, c:c + 1], bias=k.mod[:, l, 3 * isub, c:c + 1])


def copy_alt(k, out, in_, reads, writes):
    k.cpi = getattr(k, "cpi", 0) + 1
    if k.cpi % 2:
        ACTF(k.S, out, in_, AF.Identity, reads, writes)
    else:
        CP(k.S, out, in_, reads, writes)


def proj0_phase(k, groups, pre_casts=None):
    S, A, ps = k.S, k.A, k.ps
    A.reset(k.static_end)
    xb = [A.t(f"xb{i}", [128, 8, 1024], F32) for i in range(2)]
    hbs = [A.t(f"hb{i}", [128, 8, 1024], BF16) for i in range(2)]
    win = A.t("win", [128, 12, 8, 128], BF16)
    wv = A.t("wv", [128, 8, 512], BF16)
    sq = [A.t(f"sq{i}", [128, 1024], BF16) for i in range(2)]
    rs = A.t("rs", [128, 1024], F32)
    tmp = [A.t(f"tmp{i}", [128, 1024], F32) for i in range(2)]
    uT = A.t("uT", [128, 4, 1024], BF16)
    qk = [A.t(f"qk{i}", [128, 1024], BF16) for i in range(2)]
    vt = [A.t(f"vt{i}", [128, 8, 128], BF16) for i in range(2)]
    uc = [A.t(f"uc{i}", [128, 4, 256], BF16) for i in range(2)]
    csf = A.t("csf", [128, 2, 256], F32)
    csb = A.t("csb", [128, 2, 256], BF16)
    DMA(S, "sp", win[:], k.WIN0[0:12].rearrange("j p k m -> p j k m"), "win", writes=["win"])
    DMA(S, "sp", wv[:], k.WV0, "wv", writes=["wv"])
    DMA(S, "sp", csf[:, 0, :], k.csl, "csl", writes=["csf0"])
    DMA(S, "sp", csf[:, 1, :], k.csc, "csc", writes=["csf1"])
    CP(S, csb[:], csf[:], ["csf0", "csf1"], ["csb"])
    for i_ in range(2):
        MSET(S, vt[i_][:, :, 64:128], 1.0, [("vt", i_)])
    if pre_casts:
        pre_casts()
    bk = Banks([0, 1, 2, 3, 4, 5])
    n_i = 0
    def ln(gi):
        t0, n, st = groups[gi]
        load_norm(k, k.X["X1"], t0, n, st, 0, 1, xb[gi % 2], sq, rs, tmp, hbs[gi % 2], gi % 2)

    ln(0)
    for gi, (t0, n, st) in enumerate(groups):
        b = gi % 2
        hb = hbs[b]
        if gi + 1 < len(groups):
            ln(gi + 1)
        nsub = (n + 511) // 512
        w = min(n, 512)
        for j in range(12):
            q = n_i % 2
            n_i += 1
            for sub in range(nsub):
                bank = bk.next()
                for kk in range(8):
                    MM(S, ps[:, bank, 0:w], win[:, j, kk, :], hb[:, kk, sub * 512:sub * 512 + w], kk == 0, kk == 7,
                       ["win", ("hb", b, kk)], [("ps", bank)])
                if j < 4:
                    copy_alt(k, uT[:, j, sub * 512:sub * 512 + w], ps[:, bank, 0:w], [("ps", bank)], [("uT", j)])
                elif j < 8:
                    ACTF(S, qk[q][:, sub * 512:sub * 512 + w], ps[:, bank, 0:w], AF.Identity, [("ps", bank)], [("qk", q)], scale=0.125)
                else:
                    copy_alt(k, qk[q][:, sub * 512:sub * 512 + w], ps[:, bank, 0:w], [("ps", bank)], [("qk", q)])
            if j >= 4:
                dst = k.QT0[j - 4] if j < 8 else k.KT0[j - 8]
                DMA(S, "sp", dst[:, t0:t0 + n], qk[q][:, 0:n], f"qk{q}", reads=[("qk", q)])
        for tc in range(n // 128):
            gtc = t0 // 128 + tc
            q = tc % 2
            bank = bk.next()
            for kk in range(8):
                MM(S, ps[:, bank, :], hb[:, kk, tc * 128:(tc + 1) * 128], wv[:, kk, :], kk == 0, kk == 7,
                   ["wv", ("hb", b, kk)], [("ps", bank)])
            copy_alt(k, vt[q][:, :, 0:64], ps[:, bank, :].rearrange("p (h d) -> p h d", h=8), [("ps", bank)], [("vt", q)])
            DMA(S, "sp", k.VT0[gtc], vt[q][:], f"vt{q}", reads=[("vt", q)])
            for half in range(2):
                bank = bk.next()
                for gg in range(2):
                    g = half * 2 + gg
                    MM(S, ps[:, bank, gg * 256:(gg + 1) * 256], uT[:, g, tc * 128:(tc + 1) * 128], csb[:, st, :],
                       True, True, [("uT", g), "csb"], [("ps", bank)])
                copy_alt(k, uc[q][:, half * 2:half * 2 + 2, :], ps[:, bank, :].rearrange("p (a n) -> p a n", a=2),
                         [("ps", bank)], [("uc", q, half)])
            DMA(S, "sp", k.UCS[gtc], uc[q][:], f"uc{q}", reads=[("uc", q, 0), ("uc", q, 1)])


def attention(k, st_, nq, qparts, kparts_fn, v_fn, dv, chunks, scale, bias_fn, dmode="pe", ndummy=0):
    S, ps = k.S, k.ps
    st_["n"] = st_.get("n", 0) + 1
    cnum = st_["n"]
    BO = 3 + cnum % 2
    BD = 5 + cnum % 2
    pt, p2, rec, on = st_["pt"], st_["p2"], st_["rec"], st_["on"]
    nch = len(chunks)
    LA = 2
    o = cnum % 2
    mo = 128 if dmode == "merged" else dv

    def qk(jj):
        bs = st_.setdefault("sb", 0) % 3
        st_["sb"] += 1
        kp = kparts_fn(chunks[jj])
        bias = bias_fn(jj) if bias_fn else None
        nparts = len(qparts) + (1 if bias is not None else 0)
        for pi, ((qa, qkey), (ka, kkey)) in enumerate(zip(qparts, kp)):
            MM(S, ps[:, bs, 0:nq], ka, qa, pi == 0, pi == nparts - 1, [qkey, kkey], [("ps", bs)])
        if bias is not None:
            bap, bkey = bias
            MM(S, ps[:, bs, 0:nq], k.identb[:], bap, False, True, ["identb", bkey], [("ps", bs)])
        return bs

    banks = [qk(jj) for jj in range(min(LA, nch))]
    dq = []
    npairs = (nch + 1) // 2
    dcount = [0]

    def flush_d():
        while dq:
            ap, key = dq.pop(0)
            MM(S, ps[0:dv, BD, 0:nq], k.ones[:, 0:dv], ap, dcount[0] == 0, dcount[0] == npairs - 1, ["ones", key], [("ps", BD)])
            dcount[0] += 1

    prev = None
    for jj in range(nch):
        bs = banks[jj]
        if jj + LA < nch:
            banks.append(qk(jj + LA))
        if dmode == "pair":
            flush_d()
        q = st_.setdefault("pi", 0) % 4
        st_["pi"] += 1
        ACTF(S, pt[q][:, 0:nq], ps[:, bs, 0:nq], AF.Exp, [("ps", bs)], [("pt", q)], scale=scale)
        va, vkey = v_fn(chunks[jj])
        for _ in range(ndummy):
            MM(S, ps[:, 7, :], k.ones[:], k.junk[:], True, True, ["ones", "junk"], [("ps", 7)])
        if dmode == "pe":
            MM(S, ps[0:dv, BD, 0:nq], k.ones[:, 0:dv], pt[q][:, 0:nq], jj == 0, jj == nch - 1, ["ones", ("pt", q)], [("ps", BD)])
        MM(S, ps[0:mo, BO, 0:nq], va, pt[q][:, 0:nq], jj == 0, jj == nch - 1, [vkey, ("pt", q)], [("ps", BO)])
        if dmode == "pair":
            if jj % 2 == 1:
                q2 = st_.setdefault("p2i", 0) % 2
                st_["p2i"] += 1
                TT(S, p2[q2][:, 0:nq], pt[prev][:, 0:nq], pt[q][:, 0:nq], ALU.add, [("pt", prev), ("pt", q)], [("p2", q2)])
                dq.append((p2[q2][:, 0:nq], ("p2", q2)))
            elif jj == nch - 1:
                dq.append((pt[q][:, 0:nq], ("pt", q)))
        prev = q
        if jj == 1 and st_.get("pending"):
            fn = st_.pop("pending")
            fn()
    if dmode in ("pair", "pe"):
        flush_d()
        RCP(S, rec[o][0:dv, 0:nq], ps[0:dv, BD, 0:nq], [("ps", BD)], [("rec", o)])
        TT(S, on[o][0:dv, 0:nq], ps[0:dv, BO, 0:nq], rec[o][0:dv, 0:nq], ALU.mult, [("ps", BO), ("rec", o)], [("on", o)])
    else:
        RCP(S, rec[o][64:128, 0:nq], ps[64:128, BO, 0:nq], [("ps", BO)], [("rec", o)])
        TT(S, on[o][0:dv, 0:nq], ps[0:dv, BO, 0:nq], rec[o][64:128, 0:nq], ALU.mult, [("ps", BO), ("rec", o)], [("on", o)])
    return on[o][0:dv, 0:nq], ("on", o)


def attn_bufs(A, nqmax):
    return dict(pt=[A.t(f"pt{i}", [128, nqmax], BF16) for i in range(4)],
                p2=[A.t(f"p2{i}", [128, nqmax], BF16) for i in range(2)],
                rec=[A.t(f"rec{i}", [128, nqmax], F32) for i in range(2)],
                on=[A.t(f"on{i}", [128, nqmax], BF16) for i in range(2)])


def na_phase(k, qtiles):
    S, A, ps = k.S, k.A, k.ps
    A.reset(k.static_end)
    kT = A.t("kT", [128, 4, NT], BF16)
    qT = A.t("qT", [128, 4, NT], BF16)
    vtk = A.t("vtk", [128, 34, 8, 128], BF16)
    bias = [A.t(f"bias{i}", [128, 8, 512], BF16) for i in range(2)]
    st_ = attn_bufs(A, 512)
    DMA(S, "sp", kT[:], k.KT0.rearrange("j p t -> p j t"), "kT", writes=["kT"])
    DMA(S, "sp", qT[:], k.QT0.rearrange("j p t -> p j t"), "qT", writes=["qT"])
    DMA(S, "sp", vtk[:], k.VT0.rearrange("c p h d -> p c h d"), "vtk", writes=["vtk"])
    nb = 0
    for h in range(8):
        hp, ho = h // 2, (h % 2) * 64
        for pat in (1, 0, 2, None):
            tiles = []
            for (q0, nq, kind) in qtiles:
                if kind == "lat":
                    i = q0 // 512
                    p_ = 0 if i == 0 else (2 if i == 7 else 1)
                    if p_ == pat:
                        c0 = int(np.clip(8 * i - 4, 0, 48)) // 2
                        tiles.append((q0, nq, list(range(c0, c0 + 8)) + [32, 33]))
                elif pat is None:
                    tiles.append((q0, nq, [32, 33]))
            if not tiles:
                continue
            bfn = None
            if pat is not None:
                bb = nb % 2
                nb += 1
                DMA(S, "sp", bias[bb][:], k.NAB[pat, h], f"bias{bb}", writes=[("bias", bb)])
                bfn = (lambda jj, bb=bb: (bias[bb][:, jj, :], ("bias", bb)) if jj < 8 else None)
            for (q0, nq, chunks) in tiles:
                on_ap, on_key = attention(
                    k, st_, nq, [(qT[ho:ho + 64, hp, q0:q0 + nq], "qT")],
                    lambda kc, hp=hp, ho=ho: [(kT[ho:ho + 64, hp, kc * 128:(kc + 1) * 128], "kT")],
                    lambda kc, h=h: (vtk[:, kc, h, :], "vtk"), 64, chunks, 1.0, bfn, dmode="merged", ndummy=0)
                DMA(S, "sp", k.CAT[4 + hp, ho:ho + 64, q0:q0 + nq], on_ap, f"on{on_key[1]}", reads=[on_key])


def fnet_phase(k, kts=range(8), do_ctx=True):
    S, A, ps = k.S, k.A, k.ps
    A.reset(k.static_end)
    ucs = A.t("ucs", [128, 34, 4, 256], BF16)
    kvec = A.t("kvec", [128, 4096], F32)
    tcol = A.t("tcol", [128, 32], F32)
    gc = A.t("gc", [128, 32, 512], BF16)
    gs = A.t("gs", [128, 32, 512], BF16)
    mS = [A.t(f"mS{i}", [128, 512], F32) for i in range(2)]
    mC = [A.t(f"mC{i}", [128, 512], F32) for i in range(2)]
    yv = [A.t(f"yv{i}", [128, 512], F32) for i in range(2)]
    nS = [A.t(f"nS{i}", [128, 512], F32) for i in range(2)]
    nC = [A.t(f"nC{i}", [128, 512], F32) for i in range(2)]
    tcol2 = A.t("tcol2", [128, 2], F32)
    at = [A.t(f"at{i}", [128, 512], BF16) for i in range(2)]
    DMA(S, "sp", ucs[:], k.UCS.rearrange("c p g n -> p c g n"), "ucs", writes=["ucs"])
    DMA(S, "sp", kvec[:], k.kvec, "kvec", writes=["kvec"])
    DMA(S, "sp", tcol[:], k.tcol, "tcol", writes=["tcol"])
    TS(S, tcol2[:], tcol[:, 0:2], 16.0, ALU.mult, ["tcol"], ["tcol"])
    bk = Banks([0, 1, 2, 3])
    ai = 0

    MAGIC = 12582912.0

    def tables(ntc, kslice, w, tcl):
        for tc0_ in range(0, ntc, 2):
            pr = [(tc0_ + i, i) for i in range(2) if tc0_ + i < ntc]
            for tc, q in pr:
                TS(S, yv[q][:, 0:w], kslice, tcl[:, tc:tc + 1], ALU.mult, ["kvec", "tcol"], [("yv", q)])
            for tc, q in pr:
                TS(S, nS[q][:, 0:w], yv[q][:, 0:w], MAGIC, ALU.add, [("yv", q)], [("nS", q)])
            for tc, q in pr:
                TS(S, nC[q][:, 0:w], yv[q][:, 0:w], 0.25, ALU.add, [("yv", q)], [("nC", q)], s2=MAGIC, op1=ALU.add)
            for tc, q in pr:
                STT(S, mS[q][:, 0:w], nS[q][:, 0:w], MAGIC, yv[q][:, 0:w], ALU.subtract, ALU.subtract, [("nS", q), ("yv", q)], [("mS", q)])
            for tc, q in pr:
                STT(S, mC[q][:, 0:w], nC[q][:, 0:w], MAGIC, yv[q][:, 0:w], ALU.subtract, ALU.subtract, [("nC", q), ("yv", q)], [("mC", q)])
            for tc, q in pr:
                ACTF(S, gs[:, tc, 0:w], mS[q][:, 0:w], AF.Sin, [("mS", q)], [("gs", tc)], scale=6.28318)
                ACTF(S, gc[:, tc, 0:w], mC[q][:, 0:w], AF.Sin, [("mC", q)], [("gc", tc)], scale=6.28318, bias=-1.570796)

    def mix(ntc, tc0, w, dst_fn):
        nonlocal ai
        for g in range(4):
            bank = bk.next()
            for tc in range(ntc):
                MM(S, ps[:, bank, 0:w], ucs[:, tc0 + tc, g, 0:128], gc[:, tc, 0:w], tc == 0, False, ["ucs", ("gc", tc)], [("ps", bank)])
                MM(S, ps[:, bank, 0:w], ucs[:, tc0 + tc, g, 128:256], gs[:, tc, 0:w], False, tc == ntc - 1, ["ucs", ("gs", tc)], [("ps", bank)])
            q = ai % 2
            ai += 1
            copy_alt(k, at[q][:, 0:w], ps[:, bank, 0:w], [("ps", bank)], [("at", q)])
            DMA(S, "sp", dst_fn(g), at[q][:, 0:w], f"at{q}", reads=[("at", q)])

    for kt in kts:
        tables(32, kvec[:, kt * 512:(kt + 1) * 512], 512, tcol)
        mix(32, 0, 512, lambda g, kt=kt: k.CAT[g, :, kt * 512:(kt + 1) * 512])
    if do_ctx:
        tables(2, kvec[:, 0:256], 256, tcol2)
        mix(2, 32, 256, lambda g: k.CAT[g, :, NL:NT])


def lin_res_phase(k, Xsrc, Xdst, ACTsrc, Wscr, l, groups):
    S, A, ps = k.S, k.A, k.ps
    A.reset(k.static_end)
    xb = [A.t(f"xb{i}", [128, 8, 1024], F32) for i in range(2)]
    cb = [A.t(f"cb{i}", [128, 8, 1024], BF16) for i in range(2)]
    wt = A.t("wt", [128, 8, 8, 128], BF16)
    DMA(S, "sp", wt[:], Wscr.rearrange("j p k m -> p j k m"), "wt", writes=["wt"])
    bk = Banks([0, 1, 2, 3, 4, 5])
    for gi, (t0, n, st) in enumerate(groups):
        b = gi % 2
        DMA(S, "sp", xb[b][:, :, 0:n], Xsrc[:, t0:t0 + n].rearrange("(j p) t -> p j t", p=128), f"xb{b}",
            writes=[("xb", b, j) for j in range(8)])
        DMA(S, "sp", cb[b][:, :, 0:n], ACTsrc[:, :, t0:t0 + n].rearrange("j p t -> p j t"), f"cb{b}", writes=[("cb", b)])
        nsub = (n + 511) // 512
        w = min(n, 512)
        for j in range(8):
            c = j * 2 + st
            for sub in range(nsub):
                bank = bk.next()
                for kk in range(8):
                    MM(S, ps[:, bank, 0:w], wt[:, j, kk, :], cb[b][:, kk, sub * 512:sub * 512 + w], kk == 0, kk == 7,
                       ["wt", ("cb", b)], [("ps", bank)])
                xs = xb[b][:, j, sub * 512:sub * 512 + w]
                STT(S, xs, ps[:, bank, 0:w], k.Gh[:, l, 1, c:c + 1], xs, ALU.mult, ALU.add, [("ps", bank), ("xb", b, j)], [("xb", b, j)])
        DMA(S, "sp", Xdst[:, t0:t0 + n].rearrange("(j p) t -> p j t", p=128), xb[b][:, :, 0:n], f"xs{b}",
            reads=[("xb", b, j) for j in range(8)])


def proj1_phase(k, groups):
    S, A, ps = k.S, k.A, k.ps
    A.reset(k.static_end)
    xb1 = A.t("xb", [128, 8, 1024], F32)
    xb = [xb1, xb1]
    hbs = [A.t(f"hb{i}", [128, 8, 1024], BF16) for i in range(2)]
    sq = [A.t(f"sq{i}", [128, 1024], BF16) for i in range(2)]
    rs = A.t("rs", [128, 1024], F32)
    tmp = [A.t(f"tmp{i}", [128, 1024], F32) for i in range(2)]
    win = A.t("win", [128, 5, 8, 128], BF16)
    wuq = A.t("wuq", [128, 16, 3, 128], BF16)
    wuk = A.t("wuk", [128, 8, 128], BF16)
    zq = A.t("zq", [128, 3, 1024], F32)
    zkv = A.t("zkv", [128, 1024], F32)
    kr = A.t("kr", [64, 2, 1024], F32)
    cqn = A.t("cqn", [128, 3, 1024], BF16)
    ckvn = A.t("ckvn", [128, 1024], F32)
    ckvb = A.t("ckvb", [128, 1024], BF16)
    krb = A.t("krb", [64, 1024], BF16)
    rq = A.t("rq", [128, 1024], F32)
    rc = A.t("rc", [64, 1024], F32)
    rsn = A.t("rsn", [64, 1024], F32)
    qn = [A.t(f"qn{i}", [128, 1024], BF16) for i in range(2)]
    qp = [A.t(f"qp{i}", [128, 1024], BF16) for i in range(2)]
    qr = [A.t(f"qr{i}", [64, 1024], BF16) for i in range(2)]
    ctok = [A.t(f"ctok{i}", [128, 128], BF16) for i in range(2)]
    onesQ = A.t("onesQ", [128, 128], BF16)
    onesK = A.t("onesK", [128, 128], BF16)
    gq = A.t("gq", [128, 3], F32)
    gkv = A.t("gkv", [128, 1], F32)
    t1 = [A.t(f"t1{i}", [64, 1024], F32) for i in range(2)]
    DMA(S, "sp", win[:], k.WIN1.rearrange("j p k m -> p j k m"), "win", writes=["win"])
    DMA(S, "sp", wuq[:], k.WUQ.rearrange("j p k m -> p j k m"), "wuq", writes=["wuq"])
    DMA(S, "sp", wuk[:], k.WUKT, "wuk", writes=["wuk"])
    DMA(S, "sp", gq[:], k.gq, "gq", writes=["gq"])
    DMA(S, "sp", gkv[:], k.gkv, "gkv", writes=["gkv"])
    MSET(S, onesQ[:], 1.0 / 384, ["onesQ"])
    MSET(S, onesK[:], 1.0 / 128, ["onesK"])
    bk = Banks([0, 1, 2, 3, 4, 5])
    ti = 0
    def ln(gi):
        t0, n, st = groups[gi]
        load_norm(k, k.X["X4"], t0, n, st, 1, 1, xb[gi % 2], sq, rs, tmp, hbs[gi % 2], gi % 2, xtag=0)

    ln(0)
    for gi, (t0, n, st) in enumerate(groups):
        b = gi % 2
        hb = hbs[b]
        if gi + 1 < len(groups):
            ln(gi + 1)
        nsub = (n + 511) // 512
        w = min(n, 512)
        if st == 0:
            DMA(S, "sp", rc[:, 0:n], k.ropec[:, t0:t0 + n], "rc", writes=["rc"])
            DMA(S, "sp", rsn[:, 0:n], k.ropes[:, t0:t0 + n], "rsn", writes=["rsn"])

        def lin5(j, m0, m1, dst, dkey):
            for sub in range(nsub):
                bank = bk.next()
                for kk in range(8):
                    MM(S, ps[0:m1 - m0, bank, 0:w], win[:, j, kk, m0:m1], hb[:, kk, sub * 512:sub * 512 + w], kk == 0, kk == 7,
                       ["win", ("hb", b, kk)], [("ps", bank)])
                copy_alt(k, dst[:, sub * 512:sub * 512 + w], ps[0:m1 - m0, bank, 0:w], [("ps", bank)], [dkey])

        for j in range(3):
            lin5(j, 0, 128, zq[:, j, :], ("zq", j))
        lin5(3, 0, 128, zkv[:, :], "zkv")
        lin5(4, 0, 64, kr[:, 0, :], ("kr", 0))
        lin5(4, 64, 128, kr[:, 1, :], ("kr", 1))
        for j in range(3):
            q = j % 2
            ACTF(S, sq[q][:, 0:n], zq[:, j, 0:n], AF.Square, [("zq", j)], [("sq", q)])
            for sub in range(nsub):
                MM(S, ps[:, 6 + sub, 0:w], onesQ[:], sq[q][:, sub * 512:sub * 512 + w], j == 0, j == 2, [("sq", q), "onesQ"], [("ps", 6 + sub)])
        ACTF(S, rq[:, 0:n], ps[:, 6:6 + nsub, 0:w].rearrange("p a n -> p (a n)"), AF.Sqrt, [("ps", 6 + i) for i in range(nsub)], ["rq"], bias=EPS)
        RCP(S, rq[:, 0:n], rq[:, 0:n], ["rq"], ["rq"])
        for j in range(3):
            STT(S, cqn[:, j, 0:n], zq[:, j, 0:n], gq[:, j:j + 1], rq[:, 0:n], ALU.mult, ALU.mult, [("zq", j), "gq", "rq"], [("cqn", j)])
        ACTF(S, sq[0][:, 0:n], zkv[:, 0:n], AF.Square, ["zkv"], [("sq", 0)])
        for sub in range(nsub):
            MM(S, ps[:, 6 + sub, 0:w], onesK[:], sq[0][:, sub * 512:sub * 512 + w], True, True, [("sq", 0), "onesK"], [("ps", 6 + sub)])
        ACTF(S, rq[:, 0:n], ps[:, 6:6 + nsub, 0:w].rearrange("p a n -> p (a n)"), AF.Sqrt, [("ps", 6 + i) for i in range(nsub)], ["rq"], bias=EPS)
        RCP(S, rq[:, 0:n], rq[:, 0:n], ["rq"], ["rq"])
        STT(S, ckvn[:, 0:n], zkv[:, 0:n], gkv[:, 0:1], rq[:, 0:n], ALU.mult, ALU.mult, ["zkv", "gkv", "rq"], ["ckvn"])
        CP(S, ckvb[:, 0:n], ckvn[:, 0:n], ["ckvn"], ["ckvb"])
        DMA(S, "sp", k.CKVT[:, t0:t0 + n], ckvb[:, 0:n], "ckvb", reads=["ckvb"])
        for tc in range(n // 128):
            bank = bk.next()
            q = tc % 2
            S.op("pe", (lambda tc=tc, bank=bank: (lambda e: e.transpose(ps[:, bank, 0:128], ckvn[:, tc * 128:(tc + 1) * 128], k.ident[:])))(),
                 ["ckvn", "ident"], [("ps", bank)])
            copy_alt(k, ctok[q][:], ps[:, bank, 0:128], [("ps", bank)], [("ctok", q)])
            DMA(S, "sp", k.CKVTOK[t0 // 128 + tc], ctok[q][:], f"ctok{q}", reads=[("ctok", q)])
        if st == 0:
            TT(S, t1[0][:, 0:n], kr[:, 0, 0:n], rc[:, 0:n], ALU.mult, [("kr", 0), "rc"], [("t1", 0)])
            TT(S, t1[1][:, 0:n], kr[:, 1, 0:n], rsn[:, 0:n], ALU.mult, [("kr", 1), "rsn"], [("t1", 1)])
            TT(S, krb[:, 0:n], t1[0][:, 0:n], t1[1][:, 0:n], ALU.add, [("t1", 0), ("t1", 1)], ["krb"])
        else:
            CP(S, krb[:, 0:n], kr[:, 0, 0:n], [("kr", 0)], ["krb"])
        DMA(S, "sp", k.KRT[0:64, t0:t0 + n], krb[:, 0:n], "krb", reads=["krb"])
        DMA(S, "sp", k.KRT[64:128, t0:t0 + n], krb[:, 0:n], "krb2", reads=["krb"])
        if st != 0:
            continue
        for h in range(8):
            q = h % 2
            for sub in range(nsub):
                bank = bk.next()
                for kk in range(3):
                    MM(S, ps[:, bank, 0:w], wuq[:, h, kk, :], cqn[:, kk, sub * 512:sub * 512 + w], kk == 0, kk == 2,
                       ["wuq", ("cqn", kk)], [("ps", bank)])
                copy_alt(k, qn[q][:, sub * 512:sub * 512 + w], ps[:, bank, 0:w], [("ps", bank)], [("qn", q)])
            for sub in range(nsub):
                bank = bk.next()
                MM(S, ps[:, bank, 0:w], wuk[:, h, :], qn[q][:, sub * 512:sub * 512 + w], True, True, ["wuk", ("qn", q)], [("ps", bank)])
                copy_alt(k, qp[q][:, sub * 512:sub * 512 + w], ps[:, bank, 0:w], [("ps", bank)], [("qp", q)])
            DMA(S, "sp", k.QP[h, :, t0:t0 + n], qp[q][:, 0:n], f"qp{q}", reads=[("qp", q)])
            hp, ho = h // 2, (h % 2) * 64
            for sub in range(nsub):
                ba, bb_ = bk.next(), bk.next()
                for part, bank in ((0, ba), (1, bb_)):
                    for kk in range(3):
                        MM(S, ps[0:64, bank, 0:w], wuq[:, 8 + 4 * part + hp, kk, ho:ho + 64], cqn[:, kk, sub * 512:sub * 512 + w],
                           kk == 0, kk == 2, ["wuq", ("cqn", kk)], [("ps", bank)])
                sl = slice(sub * 512, sub * 512 + w)
                TT(S, t1[0][:, sl], ps[0:64, ba, 0:w], rc[:, sl], ALU.mult, [("ps", ba), "rc"], [("t1", 0)])
                TT(S, t1[1][:, sl], ps[0:64, bb_, 0:w], rsn[:, sl], ALU.mult, [("ps", bb_), "rsn"], [("t1", 1)])
                TT(S, qr[q][:, sl], t1[0][:, sl], t1[1][:, sl], ALU.add, [("t1", 0), ("t1", 1)], [("qr", q)])
            DMA(S, "sp", k.QR[hp, ho:ho + 64, t0:t0 + n], qr[q][:, 0:n], f"qr{q}", reads=[("qr", q)])


def mla_phase(k, qtiles):
    S, A, ps = k.S, k.A, k.ps
    A.reset(k.static_end)
    ckvT = A.t("ckvT", [128, NT], BF16)
    krT = A.t("krT", [128, NT], BF16)
    ctok = A.t("ctok", [128, 34, 128], BF16)
    wuv = A.t("wuv", [128, 1024], BF16)
    wo = A.t("wo", [128, 8, 8, 128], BF16)
    qp = [A.t(f"qp{i}", [128, 8, 512], BF16) for i in range(2)]
    qr = [A.t(f"qr{i}", [128, 4, 512], BF16) for i in range(2)]
    xb = [A.t(f"xb{i}", [128, 8, 512], F32) for i in range(2)]
    oT = A.t("oT", [128, 8, 512], BF16)
    st_ = attn_bufs(A, 512)
    DMA(S, "sp", ckvT[:], k.CKVT, "ckvT", writes=["ckvT"])
    DMA(S, "sp", krT[:], k.KRT, "krT", writes=["krT"])
    DMA(S, "sp", ctok[:], k.CKVTOK.rearrange("c p r -> p c r"), "ctok", writes=["ctok"])
    DMA(S, "sp", wuv[:], k.WUV, "wuv", writes=["wuv"])
    DMA(S, "sp", wo[:], k.WO.rearrange("j p k m -> p j k m"), "wo", writes=["wo"])
    chunks = list(range(34))
    scale = float(192 ** -0.5)
    for qi, q0 in enumerate(qtiles):
        b = qi % 2
        DMA(S, "sp", qp[b][:], k.QP[:, :, q0:q0 + 512].rearrange("h p t -> p h t"), f"qp{b}", writes=[("qp", b)])
        DMA(S, "sp", qr[b][:], k.QR[:, :, q0:q0 + 512].rearrange("h p t -> p h t"), f"qr{b}", writes=[("qr", b)])
        DMA(S, "sp", xb[b][:], k.X["X4"][:, q0:q0 + 512].rearrange("(j p) t -> p j t", p=128), f"xb{b}",
            writes=[("xb", b, j) for j in range(8)])
        def tail(h, on_ap, on_key):
            MM(S, ps[:, 7, :], wuv[:, h * 128:(h + 1) * 128], on_ap, True, True, ["wuv", on_key], [("ps", 7)])
            copy_alt(k, oT[:, h, :], ps[:, 7, :], [("ps", 7)], [("oT", h)])

        for h in range(8):
            hp, ho = h // 2, (h % 2) * 64
            on_ap, on_key = attention(
                k, st_, 512, [(qp[b][:, h, :], ("qp", b)), (qr[b][ho:ho + 64, hp, :], ("qr", b))],
                lambda kc, ho=ho: [(ckvT[:, kc * 128:(kc + 1) * 128], "ckvT"), (krT[ho:ho + 64, kc * 128:(kc + 1) * 128], "krT")],
                lambda kc: (ctok[:, kc, :], "ctok"), 128, chunks, scale, None)
            if h < 7:
                st_["pending"] = (lambda h=h, a=on_ap, kk_=on_key: tail(h, a, kk_))
            else:
                tail(h, on_ap, on_key)
        for j in range(8):
            bank = 7
            for kk in range(8):
                MM(S, ps[:, bank, :], wo[:, j, kk, :], oT[:, kk, :], kk == 0, kk == 7, ["wo", ("oT", kk)], [("ps", bank)])
            STT(S, xb[b][:, j, :], ps[:, bank, :], k.Gh[:, 1, 1, 2 * j:2 * j + 1], xb[b][:, j, :], ALU.mult, ALU.add,
                [("ps", bank), ("xb", b, j)], [("xb", b, j)])
        DMA(S, "sp", k.X["X5"][:, q0:q0 + 512].rearrange("(j p) t -> p j t", p=128), xb[b][:], f"xs{b}",
            reads=[("xb", b, j) for j in range(8)])


def fm(v):
    v = np.asarray(v, np.float32)
    lead = v.shape[:-1]
    return np.ascontiguousarray(np.moveaxis(v.reshape(lead + (8, 128)), -1, 0))


_CONST = {}


def consts():
    if _CONST:
        return _CONST
    c = {}
    c["ident"] = np.eye(128, dtype=np.float32)
    c["kvec"] = np.ascontiguousarray(np.broadcast_to(np.arange(4096, dtype=np.float32) / np.float32(4096.0), (128, 4096)))
    c["tcol"] = (np.arange(32)[None, :] * 128 + np.arange(128)[:, None]).astype(np.float32)
    cc = np.arange(128, dtype=np.float64)
    ang = 2 * np.pi * np.outer(cc, cc) / 128.0
    cs = np.concatenate([-np.cos(ang), np.sin(ang)], 1)
    c["csl"] = (cs / np.sqrt(4096.0 * 128.0)).astype(np.float32)
    c["csc"] = (cs / np.sqrt(256.0 * 128.0)).astype(np.float32)
    pos = np.arange(NL)
    row = (pos // 64).astype(np.float32)
    col = (pos % 64).astype(np.float32)
    inv = (np.float32(10000.0) ** (-np.arange(0, 32, 2, dtype=np.float32) / np.float32(32))).astype(np.float32)
    ang_r = row[:, None] * inv[None]
    ang_c = col[:, None] * inv[None]
    ang = np.concatenate([ang_r, ang_r, ang_c, ang_c], -1)
    sign = np.ones(64, np.float32)
    sign[0:16] = -1
    sign[32:48] = -1
    c["ropec"] = np.ascontiguousarray(np.cos(ang).T.astype(np.float32))
    c["ropes"] = np.ascontiguousarray((np.sin(ang) * sign[None]).T.astype(np.float32))
    _CONST.update(c)
    return c


ROPE_PERM = np.concatenate([np.arange(16, 32), np.arange(0, 16), np.arange(48, 64), np.arange(32, 48)])


def na_bias(rpb):
    out = np.full((3, 8, 8, 128, 512), -30000.0, np.float32)
    p = np.arange(128)
    qq = np.arange(512)
    for pat, i in enumerate((0, 1, 7)):
        krow0 = int(np.clip(8 * i - 4, 0, 48))
        qrow = 8 * i + qq // 64
        qcol = qq % 64
        r0 = np.clip(qrow - 4, 0, 56)
        cs = np.clip(qcol - 8, 0, 48)
        for jj in range(8):
            krow = krow0 + 2 * jj + p // 64
            kcol = p % 64
            valid = ((krow[:, None] >= r0[None]) & (krow[:, None] < r0[None] + 8)
                     & (kcol[:, None] >= cs[None]) & (kcol[:, None] < cs[None] + 16))
            ri = np.clip(krow[:, None] - qrow[None] + 7, 0, 14)
            ci = np.clip(kcol[:, None] - qcol[None] + 15, 0, 30)
            for h in range(8):
                g = rpb[h][ri, ci]
                out[pat, h, jj] = np.where(valid, g, np.float32(-30000.0))
    return out


def prep_inputs(inp, b):
    x0 = np.concatenate([inp["x"][b].T, inp["ctx"][b].T], axis=1)
    sc = np.stack([inp["c"][b], inp["c_ctx"]], -1).reshape(8, 128, 2).transpose(1, 0, 2)
    adb = fm(inp["ada_b"].reshape(2, 9, D))
    adb = np.repeat(adb[..., None], 2, -1).reshape(128, 2, 9, 16)
    ng = fm(inp["norm_g"])
    ng = np.repeat(ng[..., None], 2, -1).reshape(128, 2, 3, 16)
    w_in = inp["mla_w_in"][0]
    w_in_ext = np.concatenate([w_in, w_in[:, 512:576][:, ROPE_PERM]], 1)
    wq = inp["mla_w_uq"][0].reshape(384, 8, 192)
    wq_ext = np.concatenate([wq[:, :, :128].reshape(384, 1024), wq[:, :, 128:].reshape(384, 512),
                             wq[:, :, 128:][:, :, ROPE_PERM].reshape(384, 512)], 1)
    wukT = inp["mla_w_uk"][0].reshape(128, 8, 128).transpose(1, 2, 0)
    m = dict(
        x0=np.ascontiguousarray(x0, np.float32), sc=np.ascontiguousarray(sc, np.float32),
        ada_w=inp["ada_w"], adb=np.ascontiguousarray(adb), ng=np.ascontiguousarray(ng),
        fg=fm(inp["final_g"]),
        ffn_w_gate=inp["ffn_w_gate"], ffn_w_up=inp["ffn_w_up"], ffn_w_down=inp["ffn_w_down"],
        ab_w_in=inp["ab_w_in"][0], ab_w_out=inp["ab_w_out"][0],
        mla_w_in=np.ascontiguousarray(w_in_ext), mla_w_uq=np.ascontiguousarray(wq_ext),
        mla_w_ukT=np.ascontiguousarray(wukT), mla_w_uv=inp["mla_w_uv"][0], mla_w_o=inp["mla_w_o"][0],
        gq=np.ascontiguousarray(inp["mla_g_q"][0].reshape(3, 128).T), gkv=np.ascontiguousarray(inp["mla_g_kv"][0].reshape(1, 128).T),
    )
    m.update(consts())
    return m


def kernel(**inputs):
    inp = {k_: np.asarray(v) for k_, v in inputs.items()}
    nc = build()
    nab = na_bias(inp["ab_rpb"][0])
    in_maps = [prep_inputs(inp, b) for b in range(8)]
    for m in in_maps:
        m["nabias"] = nab
    res = run_bass_kernel_spmd(nc, in_maps, core_ids=list(range(8)))
    out = np.stack([np.ascontiguousarray(r["outT"].T) for r in res.results], 0)
    return out.astype(np.float32)
```

```python
import contextlib
import numpy as np
import concourse.bass as bass
import concourse.mybir as mybir
from concourse.bass_utils import run_bass_kernel_spmd

F32 = mybir.dt.float32
BF16 = mybir.dt.bfloat16
AF = mybir.ActivationFunctionType
ALU = mybir.AluOpType

D = 1024
NL = 4096
NCX = 256
NT = NL + NCX
DFF = 2816
NF = DFF // 128
EPS = 1e-6
SEM_CHUNK = 30000


class Sched:
    ENG = ("pe", "act", "dve", "pool", "sp")

    def __init__(self, nc):
        self.nc = nc
        self.ops = {e: [] for e in self.ENG}
        self.res = {}
        self.dma_cnt = {}
        self.seen = {e: {} for e in self.ENG}

    def _collect(self, eng, reads, writes):
        need = {}

        def add(src, val, raw):
            if src[0] == "e" and src[1] == eng:
                if eng in ("pe", "sp"):
                    return
            if need.get(src, 0) < val:
                need[src] = val

        for k in reads:
            st = self.res.get(k)
            if st and st[0] is not None:
                add(st[0][0], st[0][1], True)
        for k in writes:
            st = self.res.get(k)
            if st:
                if st[0] is not None:
                    add(st[0][0], st[0][1], False)
                for src, val in st[1].items():
                    add(src, val, False)
        return self._filter(eng, need)

    def _filter(self, eng, need):
        waits = []
        seen = self.seen[eng]
        for src, val in need.items():
            if seen.get(src, 0) >= val:
                continue
            seen[src] = val
            waits.append((src, val))
            if src[0] == "e":
                self.ops[src[1]][val - 1]["signal"] = True
        return waits

    def _update(self, ref, reads, writes):
        src, val = ref
        for k in writes:
            self.res[k] = [ref, {}]
        for k in reads:
            st = self.res.setdefault(k, [None, {}])
            if st[1].get(src, 0) < val:
                st[1][src] = val

    def op(self, eng, fn, reads=(), writes=()):
        waits = self._collect(eng, reads, writes)
        lst = self.ops[eng]
        lst.append(dict(fn=fn, waits=waits, signal=False, dma=None))
        self._update((("e", eng), len(lst)), reads, writes)

    def dma(self, eng, fn, sem, reads=(), writes=()):
        waits = self._collect(eng, reads, writes)
        sem = f"{eng}_{sem}"
        cnt = self.dma_cnt.get(sem, 0) + 1
        assert cnt * 16 < SEM_CHUNK
        self.dma_cnt[sem] = cnt
        self.ops[eng].append(dict(fn=fn, waits=waits, signal=False, dma=sem))
        self._update((("d", sem), cnt), reads, writes)

    def barrier(self):
        last = {}
        for f in self.ENG:
            idx = 0
            for i, o in enumerate(self.ops[f]):
                if o["dma"] is None and o["fn"] is not None:
                    idx = i + 1
            last[f] = idx
        need = {("e", f): last[f] for f in self.ENG if f != "sp" and last[f] > 0}
        for k, c in self.dma_cnt.items():
            need[("d", k)] = c
        waits = self._filter("sp", need)
        self.ops["sp"].append(dict(fn=lambda e: e.nop(), waits=waits, signal=False, dma=None))
        idx = len(self.ops["sp"])
        for e in self.ENG:
            if e != "sp":
                w = self._filter(e, {("e", "sp"): idx})
                self.ops[e].append(dict(fn=None, waits=w, signal=False, dma=None))
        self.res = {}

    def emit(self):
        nc = self.nc
        with contextlib.ExitStack() as es:
            esem = {}
            sigpos = {}
            for e in self.ENG:
                c = 0
                pos = []
                for o in self.ops[e]:
                    if o["signal"]:
                        c += 1
                    pos.append(c)
                sigpos[e] = pos
                nsem = max(1, (c + SEM_CHUNK - 1) // SEM_CHUNK)
                esem[e] = [es.enter_context(nc.semaphore(f"s_{e}_{i}")) for i in range(nsem)]
            dsem = {k: es.enter_context(nc.semaphore(f"d_{k}")) for k in self.dma_cnt}
            block = es.enter_context(nc.Block())

            def run(e, engobj):
                cnt = 0
                for o in self.ops[e]:
                    for src, val in o["waits"]:
                        if src[0] == "e":
                            c = sigpos[src[1]][val - 1]
                            assert c > 0
                            engobj.wait_ge(esem[src[1]][(c - 1) // SEM_CHUNK], (c - 1) % SEM_CHUNK + 1)
                        else:
                            engobj.wait_ge(dsem[src[1]], 16 * val)
                    if o["fn"] is None:
                        continue
                    ins = o["fn"](engobj)
                    if o["dma"] is not None:
                        ins.then_inc(dsem[o["dma"]], 16)
                    elif o["signal"]:
                        cnt += 1
                        ins.then_inc(esem[e][(cnt - 1) // SEM_CHUNK], 1)

            @block.tensor
            def _(eng):
                run("pe", eng)

            @block.scalar
            def _(eng):
                run("act", eng)

            @block.vector
            def _(eng):
                run("dve", eng)

            @block.gpsimd
            def _(eng):
                run("pool", eng)

            @block.sync
            def _(eng):
                run("sp", eng)


class Arena:
    def __init__(self, nc, base, size):
        self.nc, self.base, self.size, self.off, self.n = nc, base, size, 0, 0

    def reset(self, off=0):
        self.off = off

    def t(self, name, shape, dt):
        nb = int(np.prod(shape[1:])) * (4 if dt == F32 else 2)
        nb = (nb + 63) // 64 * 64
        assert self.off + nb <= self.size, (name, self.off, nb, self.size)
        self.n += 1
        h = self.nc.alloc_sbuf_tensor_at(f"{name}_{self.n}", list(shape), dt, offset=self.base + self.off)
        self.off += nb
        return h


class K:
    pass


def build(only=None, dbg=(), small=False, xin=()):
    nc = bass.Bass("TRN2", target_bir_lowering=False)
    k = K()
    k.nc = nc

    def din(name, shape, dt=F32):
        return nc.dram_tensor(name, list(shape), dt, kind="ExternalInput").ap()

    def dscr(name, shape, dt=BF16):
        kind = "ExternalOutput" if name in dbg else "Internal"
        return nc.dram_tensor(name, list(shape), dt, kind=kind).ap()

    k.x0 = din("x0", [D, NT])
    k.sc = din("sc", [128, 8, 2])
    k.ada_w = din("ada_w", [2, D, 9 * D])
    k.adb = din("adb", [128, 2, 9, 16])
    k.ng = din("ng", [128, 2, 3, 16])
    k.fg = din("fg", [128, 8])
    k.w_gate = din("ffn_w_gate", [2, 2, D, DFF])
    k.w_up = din("ffn_w_up", [2, 2, D, DFF])
    k.w_down = din("ffn_w_down", [2, 2, DFF, D])
    k.out = nc.dram_tensor("outT", [D, NL], F32, kind="ExternalOutput").ap()
    k.ab_w_in = din("ab_w_in", [D, 2048])
    k.ab_w_out = din("ab_w_out", [D, D])
    k.mla_w_in = din("mla_w_in", [D, 640])
    k.mla_w_uq = din("mla_w_uq", [384, 2048])
    k.mla_w_ukT = din("mla_w_ukT", [8, 128, 128])
    k.mla_w_uv = din("mla_w_uv", [128, 1024])
    k.mla_w_o = din("mla_w_o", [D, D])
    k.gq = din("gq", [128, 3])
    k.gkv = din("gkv", [128, 1])
    k.ident_d = din("ident", [128, 128])
    k.kvec = din("kvec", [128, 4096])
    k.tcol = din("tcol", [128, 32])
    k.csl = din("csl", [128, 256])
    k.csc = din("csc", [128, 256])
    k.ropec = din("ropec", [64, NL])
    k.ropes = din("ropes", [64, NL])
    k.nabias = din("nabias", [3, 8, 8, 128, 512])

    k.X = {}
    for n_ in ("X1", "X2", "X3", "X4", "X5", "X6"):
        if n_ in xin:
            k.X[n_] = din(n_, [D, NT])
        else:
            k.X[n_] = dscr(n_, [D, NT], F32)
    k.wgu = {(l, s): dscr(f"wgu{l}{s}", [NF, 128, 2, 8, 128]) for l in range(2) for s in range(2)}
    k.wd = {(l, s): dscr(f"wd{l}{s}", [8, 128, NF, 128]) for l in range(2) for s in range(2)}
    k.WIN0 = dscr("WIN0", [16, 128, 8, 128])
    k.WV0 = dscr("WV0", [128, 8, 512])
    k.WOUT0 = dscr("WOUT0", [8, 128, 8, 128])
    k.QT0 = dscr("QT0", [4, 128, NT])
    k.KT0 = dscr("KT0", [4, 128, NT])
    k.VT0 = dscr("VT0", [34, 128, 8, 128])
    k.UCS = dscr("UCS", [34, 128, 4, 256])
    k.CAT = dscr("CAT", [8, 128, NT])
    k.NAB = dscr("NAB", [3, 8, 128, 8, 512])
    k.WIN1 = dscr("WIN1", [5, 128, 8, 128])
    k.WUQ = dscr("WUQ", [16, 128, 3, 128])
    k.WUKT = dscr("WUKT", [128, 8, 128])
    k.WUV = dscr("WUV", [128, 1024])
    k.WO = dscr("WO", [8, 128, 8, 128])
    k.QP = dscr("QP", [8, 128, NL])
    k.QR = dscr("QR", [4, 128, NL])
    k.CKVT = dscr("CKVT", [128, NT])
    k.KRT = dscr("KRT", [128, NT])
    k.CKVTOK = dscr("CKVTOK", [34, 128, 128])

    arena_h = nc.alloc_sbuf_tensor("arena", [128, 204800], mybir.dt.uint8)
    base = nc.lookup_mloc(arena_h).addr
    A = Arena(nc, base, 204800)
    k.A = A
    k.ps = nc.alloc_psum_tensor("ps", [128, 8, 512], F32)
    S = Sched(nc)
    k.S = S

    k.ident = A.t("ident", [128, 128], F32)
    k.onesD = A.t("onesD", [128, 128], BF16)
    k.ones = A.t("ones", [128, 128], BF16)
    k.identb = A.t("identb", [128, 128], BF16)
    k.junk = A.t("junk", [128, 512], BF16)
    k.mod = A.t("mod", [128, 2, 9, 16], F32)
    k.Am = A.t("Am", [128, 2, 3, 16], F32)
    k.Gh = A.t("Gh", [128, 2, 3, 16], F32)
    k.fgt = A.t("fgt", [128, 8], F32)
    k.ngt = A.t("ngt", [128, 2, 3, 16], F32)
    k.adbt = A.t("adbt", [128, 2, 9, 16], F32)
    k.static_end = 8192
    assert A.off <= k.static_end

    lat_groups = [(g * 1024, 1024, 0) for g in range(4)]
    na_q = [(i * 512, 512, "lat") for i in range(8)] + [(NL, NCX, "ctx")]
    kts = list(range(8))
    mla_q = [i * 512 for i in range(8)]
    if small:
        lat_groups = [(0, 1024, 0)]
        na_q = [(0, 512, "lat"), (NL, NCX, "ctx")]
        kts = [0]
        mla_q = [0]
    if small in (2, 3):
        lat_groups = [(g * 1024, 1024, 0) for g in range(4)]
    if small == 3:
        mla_q = [0, 512, 1024]
    all_groups = lat_groups + [(NL, NCX, 1)]

    def casts_mixer0():
        cast_lin(k, k.ab_w_in, k.WIN0, D, 2048)
        DMA(S, "pool", k.WV0, k.ab_w_in[:, 1536:2048].rearrange("(k p) n -> p k n", p=128), "cast")
        cast_lin(k, k.ab_w_out, k.WOUT0, D, D)
        for pat in range(3):
            for h in range(8):
                DMA(S, "pool", k.NAB[pat, h], k.nabias[pat, h].rearrange("j p q -> p j q"), "cast")

    def casts_l1():
        cast_lin(k, k.mla_w_in, k.WIN1, D, 640)
        cast_lin(k, k.mla_w_uq, k.WUQ, 384, 2048)
        DMA(S, "pool", k.WUKT, k.mla_w_ukT.rearrange("h d r -> d h r"), "cast")
        DMA(S, "pool", k.WUV, k.mla_w_uv, "cast")
        cast_lin(k, k.mla_w_o, k.WO, D, D)

    def all_casts():
        casts_mixer0()
        casts_l1()
        for l_, s_ in ((0, 1), (1, 0), (1, 1)):
            cast_ffn(k, l_, s_)

    full = only is None
    plan = [
        ("setup", lambda: phase_setup(k, casts_mixer0 if full else all_casts)),
        ("ffn00", lambda: ffn_phase(k, 0, 0, k.x0, k.X["X1"], all_groups,
                                    (lambda: (cast_ffn(k, 0, 1), cast_ffn(k, 1, 0))) if full else None)),
        ("proj0", lambda: proj0_phase(k, all_groups, (lambda: (casts_l1(), cast_ffn(k, 1, 1))) if full else None)),
        ("na", lambda: na_phase(k, na_q)),
        ("fnet", lambda: fnet_phase(k, kts)),
        ("wout", lambda: lin_res_phase(k, k.X["X1"], k.X["X2"], k.CAT, k.WOUT0, 0, all_groups)),
        ("ffn01", lambda: ffn_phase(k, 0, 1, k.X["X2"], k.X["X3"], all_groups, None)),
        ("ffn10", lambda: ffn_phase(k, 1, 0, k.X["X3"], k.X["X4"], all_groups, None)),
        ("proj1", lambda: proj1_phase(k, all_groups)),
        ("mla", lambda: mla_phase(k, mla_q)),
        ("ffn11", lambda: ffn_phase(k, 1, 1, k.X["X5"], k.X["X6"], lat_groups, None)),
        ("final", lambda: final_phase(k, k.X["X6"], lat_groups)),
    ]
    for name, fn in plan:
        if only is not None and name not in only:
            continue
        fn()
        S.barrier()
    global LAST_S
    LAST_S = S
    S.emit()
    return nc


def MM(S, out, lhsT, rhs, start, stop, reads, writes):
    S.op("pe", lambda e: e.matmul(out, lhsT=lhsT, rhs=rhs, start=start, stop=stop), reads, writes)


def ACTF(S, out, in_, func, reads, writes, scale=1.0, bias=0.0):
    S.op("act", lambda e: e.activation(out=out, in_=in_, func=func, scale=scale, bias=bias), reads, writes)


def TT(S, out, in0, in1, op, reads, writes, eng="dve"):
    S.op(eng, lambda e: e.tensor_tensor(out=out, in0=in0, in1=in1, op=op), reads, writes)


def STT(S, out, in0, scalar, in1, op0, op1, reads, writes, eng="dve"):
    S.op(eng, lambda e: e.scalar_tensor_tensor(out=out, in0=in0, scalar=scalar, in1=in1, op0=op0, op1=op1), reads, writes)


def TS(S, out, in0, s1, op0, reads, writes, s2=None, op1=None, eng="dve"):
    if op1 is None:
        S.op(eng, lambda e: e.tensor_scalar(out=out, in0=in0, scalar1=s1, scalar2=None, op0=op0), reads, writes)
    else:
        S.op(eng, lambda e: e.tensor_scalar(out=out, in0=in0, scalar1=s1, scalar2=s2, op0=op0, op1=op1), reads, writes)


def CP(S, out, in_, reads, writes, eng="dve"):
    S.op(eng, lambda e: e.tensor_copy(out=out, in_=in_), reads, writes)


def RCP(S, out, in_, reads, writes):
    S.op("dve", lambda e: e.reciprocal(out=out, in_=in_), reads, writes)


def MSET(S, ap, val, writes, eng="dve"):
    S.op(eng, lambda e: e.memset(ap, val), (), writes)


def DMA(S, eng, out, in_, sem, reads=(), writes=()):
    S.dma(eng, lambda e: e.dma_start(out=out, in_=in_), sem, reads, writes)


def phase_setup(k, extra_casts=None):
    nc, S, A, ps = k.nc, k.S, k.A, k.ps
    A.reset(k.static_end)
    sc_f = A.t("sc_f", [128, 8, 2], F32)
    sc_b = A.t("sc_b", [128, 8, 2], BF16)
    adw = [A.t(f"adw{i}", [128, 8, 1024], BF16) for i in range(2)]
    cast_ffn(k, 0, 0)
    if extra_casts:
        extra_casts()
    DMA(S, "sp", k.ident[:], k.ident_d, "m_id", writes=["ident"])
    DMA(S, "sp", sc_f[:], k.sc, "m_sc", writes=["sc_f"])
    DMA(S, "sp", k.adbt[:], k.adb, "m_adb", writes=["adbt"])
    DMA(S, "sp", k.ngt[:], k.ng, "m_ng", writes=["ngt"])
    DMA(S, "sp", k.fgt[:], k.fg, "m_fg", writes=["fgt"])
    MSET(S, k.onesD[:], 1.0 / D, ["onesD"])
    MSET(S, k.ones[:], 1.0, ["ones"])
    MSET(S, k.junk[:], 1.0, ["junk"])
    CP(S, k.identb[:], k.ident[:], ["ident"], ["identb"])
    ACTF(S, sc_b[:], sc_f[:], AF.Silu, ["sc_f"], ["sc_b"])
    n = 0
    for l in range(2):
        for i in range(9):
            b = n % 2
            DMA(S, "pool", adw[b][:], k.ada_w[l, :, i * 1024:(i + 1) * 1024].rearrange("(k p) n -> p k n", p=128),
                f"adw{b}", writes=[("adw", b)])
            bank = n % 2
            for j in range(8):
                for kk in range(8):
                    MM(S, ps[:, bank, j * 2:j * 2 + 2], adw[b][:, kk, j * 128:(j + 1) * 128], sc_b[:, kk, :],
                       kk == 0, kk == 7, [("adw", b), "sc_b"], [("ps", bank)])
            TT(S, k.mod[:, l, i, :], ps[:, bank, 0:16], k.adbt[:, l, i, :], ALU.add, [("ps", bank), "adbt"], ["mod"])
            n += 1
    for l in range(2):
        for i in range(3):
            STT(S, k.Am[:, l, i, :], k.mod[:, l, 3 * i + 1, :], 1.0, k.ngt[:, l, i, :], ALU.add, ALU.mult,
                ["mod", "ngt"], ["Am"])
            TS(S, k.Gh[:, l, i, :], k.mod[:, l, 3 * i + 2, :], (1.0 if i == 1 else 0.5), ALU.mult, ["mod"], ["Gh"])


def cast_ffn(k, l, s):
    S = k.S
    for f in range(NF):
        for gi, W in enumerate((k.w_gate, k.w_up)):
            DMA(S, "pool", k.wgu[(l, s)][f, :, gi],
                W[l, s, :, f * 128:(f + 1) * 128].rearrange("(k p) m -> p k m", p=128), "cast")
    for dj in range(8):
        DMA(S, "pool", k.wd[(l, s)][dj],
            k.w_down[l, s, :, dj * 128:(dj + 1) * 128].rearrange("(k p) m -> p k m", p=128), "cast")


def norm_stats(k, xb_j, n, sq, rs, tag):
    S, ps = k.S, k.ps
    nsub = (n + 511) // 512
    w = min(n, 512)
    for j in range(8):
        q = j % 2
        ap, key = xb_j(j)
        ACTF(S, sq[q][:, 0:n], ap, AF.Square, [key], [("sq", q)])
        for sub in range(nsub):
            MM(S, ps[:, 6 + sub, 0:w], k.onesD[:], sq[q][:, sub * 512:sub * 512 + w], j == 0, j == 7,
               [("sq", q), "onesD"], [("ps", 6 + sub)])
    ACTF(S, rs[:, 0:n], ps[:, 6:6 + nsub, 0:w].rearrange("p a n -> p (a n)"), AF.Sqrt,
         [("ps", 6 + i) for i in range(nsub)], [tag], bias=EPS)
    RCP(S, rs[:, 0:n], rs[:, 0:n], [tag], [tag])


def ffn_phase(k, l, s, Xsrc, Xdst, groups, pre_casts=None):
    nc, S, A, ps = k.nc, k.S, k.A, k.ps
    A.reset(k.static_end)
    xb = [A.t(f"xb{i}", [128, 8, 1024], F32) for i in range(2)]
    hb = A.t("hb", [128, 8, 1024], BF16)
    act = A.t("act", [128, NF, 1024], BF16)
    sq = [A.t(f"sq{i}", [128, 1024], BF16) for i in range(2)]
    rs = [A.t(f"rs{i}", [128, 1024], F32) for i in range(2)]
    tmp = [A.t(f"tmp{i}", [128, 1024], F32) for i in range(2)]
    sg = [A.t(f"sg{i}", [128, 512], F32) for i in range(2)]
    gu = [A.t(f"gu{i}", [128, 2, 8, 128], BF16) for i in range(3)]
    wdt = [A.t(f"wd{i}", [128, NF, 128], BF16) for i in range(3)]
    isub = 0 if s == 0 else 2
    wgu_scr, wd_scr = k.wgu[(l, s)], k.wd[(l, s)]
    ng = len(groups)

    events = []
    for gi in range(ng):
        events += [("gu", f) for f in range(NF)] + [("wd", dj) for dj in range(8)]
    cnt = {"gu": 0, "wd": 0}
    slot_of = []
    for kind, idx in events:
        slot_of.append(cnt[kind] % 3)
        cnt[kind] += 1
    issued = [0]

    def issue_upto(i):
        while issued[0] <= min(i, len(events) - 1):
            ev = issued[0]
            kind, idx = events[ev]
            sl = slot_of[ev]
            if kind == "gu":
                DMA(S, "sp", gu[sl][:], wgu_scr[idx], f"gu{sl}", writes=[("gu", sl)])
            else:
                DMA(S, "sp", wdt[sl][:], wd_scr[idx], f"wd{sl}", writes=[("wd", sl)])
            issued[0] += 1

    def load_x(gi):
        t0, n, st = groups[gi]
        b = gi % 2
        DMA(S, "pool", xb[b][:, :, 0:n], Xsrc[:, t0:t0 + n].rearrange("(j p) t -> p j t", p=128),
            f"xb{b}", writes=[("xb", b, j) for j in range(8)])

    def stats(gi):
        t0, n, st = groups[gi]
        b = gi % 2
        norm_stats(k, lambda j: (xb[b][:, j, 0:n], ("xb", b, j)), n, sq, rs[b], ("rs", b))

    def hpiece(gi, j):
        t0, n, st = groups[gi]
        b = gi % 2
        q = j % 2
        c = j * 2 + st
        TT(S, tmp[q][:, 0:n], xb[b][:, j, 0:n], rs[b][:, 0:n], ALU.mult, [("xb", b, j), ("rs", b)], [("tmp", q)])
        ACTF(S, hb[:, j, 0:n], tmp[q][:, 0:n], AF.Identity, [("tmp", q), "Am", "mod"], [("hb", j)],
             scale=k.Am[:, l, isub, c:c + 1], bias=k.mod[:, l, 3 * isub, c:c + 1])

    def do_group(gi):
        t0, n, st = groups[gi]
        b = gi % 2
        nsub = (n + 511) // 512
        w = min(n, 512)
        for f in range(NF):
            ev = gi * 30 + f
            issue_upto(ev + 2)
            sl = slot_of[ev]
            for sub in range(nsub):
                it = f * nsub + sub
                bg, bu = 2 * (it % 3), 2 * (it % 3) + 1
                for half, bank in ((0, bg), (1, bu)):
                    for kk in range(8):
                        MM(S, ps[:, bank, 0:w], gu[sl][:, half, kk, :], hb[:, kk, sub * 512:sub * 512 + w],
                           kk == 0, kk == 7, [("gu", sl), ("hb", kk)], [("ps", bank)])
                q = it % 2
                ACTF(S, sg[q][:, 0:w], ps[:, bg, 0:w], AF.Silu, [("ps", bg)], [("sg", q)])
                TT(S, act[:, f, sub * 512:sub * 512 + w], sg[q][:, 0:w], ps[:, bu, 0:w], ALU.mult,
                   [("sg", q), ("ps", bu)], [("act", f)])
        if gi + 1 < ng:
            stats(gi + 1)
        for dj in range(8):
            ev = gi * 30 + NF + dj
            issue_upto(ev + 2)
            sl = slot_of[ev]
            c = dj * 2 + st
            for sub in range(nsub):
                by = (dj * nsub + sub) % 6
                for kf in range(NF):
                    MM(S, ps[:, by, 0:w], wdt[sl][:, kf, :], act[:, kf, sub * 512:sub * 512 + w],
                       kf == 0, kf == NF - 1, [("wd", sl), ("act", kf)], [("ps", by)])
                xs = xb[b][:, dj, sub * 512:sub * 512 + w]
                STT(S, xs, ps[:, by, 0:w], k.Gh[:, l, isub, c:c + 1], xs, ALU.mult, ALU.add,
                    [("ps", by), ("xb", b, dj), "Gh"], [("xb", b, dj)])
            if gi + 1 < ng:
                hpiece(gi + 1, dj)
        DMA(S, "sp", Xdst[:, t0:t0 + n].rearrange("(j p) t -> p j t", p=128), xb[b][:, :, 0:n],
            f"xs{b}", reads=[("xb", b, j) for j in range(8)])
        if gi + 2 < ng:
            load_x(gi + 2)

    load_x(0)
    if ng > 1:
        load_x(1)
    if pre_casts:
        pre_casts()
    issue_upto(1)
    stats(0)
    for j in range(8):
        hpiece(0, j)
    for gi in range(ng):
        do_group(gi)


def final_phase(k, Xsrc, groups):
    S, A = k.S, k.A
    A.reset(k.static_end)
    xb = [A.t(f"xb{i}", [128, 8, 1024], F32) for i in range(2)]
    sq = [A.t(f"sq{i}", [128, 1024], BF16) for i in range(2)]
    rs = [A.t(f"rs{i}", [128, 1024], F32) for i in range(2)]

    def grp(gi):
        t0, n, st = groups[gi]
        b = gi % 2
        DMA(S, "sp", xb[b][:, :, 0:n], Xsrc[:, t0:t0 + n].rearrange("(j p) t -> p j t", p=128),
            f"xb{b}", writes=[("xb", b, j) for j in range(8)])
        norm_stats(k, lambda j: (xb[b][:, j, 0:n], ("xb", b, j)), n, sq, rs[b], ("rs", b))
        for j in range(8):
            STT(S, xb[b][:, j, 0:n], xb[b][:, j, 0:n], k.fgt[:, j:j + 1], rs[b][:, 0:n], ALU.mult, ALU.mult,
                [("xb", b, j), ("rs", b), "fgt"], [("xb", b, j)])
        DMA(S, "sp", k.out[:, t0:t0 + n].rearrange("(j p) t -> p j t", p=128), xb[b][:, :, 0:n],
            f"xs{b}", reads=[("xb", b, j) for j in range(8)])

    for gi in range(len(groups)):
        grp(gi)


def cast_lin(k, src, dst, K_, N_):
    for j in range(N_ // 128):
        DMA(k.S, "pool", dst[j], src[:, j * 128:(j + 1) * 128].rearrange("(k p) m -> p k m", p=128), "cast")


class Banks:
    def __init__(self, lst):
        self.lst, self.i = lst, 0

    def next(self):
        b = self.lst[self.i % len(self.lst)]
        self.i += 1
        return b


def load_norm(k, Xsrc, t0, n, st, l, isub, xb, sq, rs, tmp, hb, tag, xtag=None):
    S = k.S
    xt = tag if xtag is None else xtag
    DMA(S, "sp", xb[:, :, 0:n], Xsrc[:, t0:t0 + n].rearrange("(j p) t -> p j t", p=128), f"xb{xt}",
        writes=[("xb", xt, j) for j in range(8)])
    norm_stats(k, lambda j: (xb[:, j, 0:n], ("xb", xt, j)), n, sq, rs, ("rs", 0))
    for j in range(8):
        q = j % 2
        c = j * 2 + st
        TT(S, tmp[q][:, 0:n], xb[:, j, 0:n], rs[:, 0:n], ALU.mult, [("xb", xt, j), ("rs", 0)], [("tmp", q)])
        ACTF(S, hb[:, j, 0:n], tmp[q][:, 0:n], AF.Identity, [("tmp", q)], [("hb", tag, j)],
             scale=k.Am[:, l, isub, c:c + 1], bias=k.mod[:, l, 3 * isub, c:c + 1])


def copy_alt(k, out, in_, reads, writes):
    k.cpi = getattr(k, "cpi", 0) + 1
    if k.cpi % 2:
        ACTF(k.S, out, in_, AF.Identity, reads, writes)
    else:
        CP(k.S, out, in_, reads, writes)


def proj0_phase(k, groups, pre_casts=None):
    S, A, ps = k.S, k.A, k.ps
    A.reset(k.static_end)
    xb = [A.t(f"xb{i}", [128, 8, 1024], F32) for i in range(2)]
    hbs = [A.t(f"hb{i}", [128, 8, 1024], BF16) for i in range(2)]
    win = A.t("win", [128, 12, 8, 128], BF16)
    wv = A.t("wv", [128, 8, 512], BF16)
    sq = [A.t(f"sq{i}", [128, 1024], BF16) for i in range(2)]
    rs = A.t("rs", [128, 1024], F32)
    tmp = [A.t(f"tmp{i}", [128, 1024], F32) for i in range(2)]
    uT = A.t("uT", [128, 4, 1024], BF16)
    qk = [A.t(f"qk{i}", [128, 1024], BF16) for i in range(2)]
    vt = [A.t(f"vt{i}", [128, 8, 128], BF16) for i in range(2)]
    uc = [A.t(f"uc{i}", [128, 4, 256], BF16) for i in range(2)]
    csf = A.t("csf", [128, 2, 256], F32)
    csb = A.t("csb", [128, 2, 256], BF16)
    DMA(S, "sp", win[:], k.WIN0[0:12].rearrange("j p k m -> p j k m"), "win", writes=["win"])
    DMA(S, "sp", wv[:], k.WV0, "wv", writes=["wv"])
    DMA(S, "sp", csf[:, 0, :], k.csl, "csl", writes=["csf0"])
    DMA(S, "sp", csf[:, 1, :], k.csc, "csc", writes=["csf1"])
    CP(S, csb[:], csf[:], ["csf0", "csf1"], ["csb"])
    for i_ in range(2):
        MSET(S, vt[i_][:, :, 64:128], 1.0, [("vt", i_)])
    if pre_casts:
        pre_casts()
    bk = Banks([0, 1, 2, 3, 4, 5])
    n_i = 0
    def ln(gi):
        t0, n, st = groups[gi]
        load_norm(k, k.X["X1"], t0, n, st, 0, 1, xb[gi % 2], sq, rs, tmp, hbs[gi % 2], gi % 2)

    ln(0)
    for gi, (t0, n, st) in enumerate(groups):
        b = gi % 2
        hb = hbs[b]
        if gi + 1 < len(groups):
            ln(gi + 1)
        nsub = (n + 511) // 512
        w = min(n, 512)
        for j in range(12):
            q = n_i % 2
            n_i += 1
            for sub in range(nsub):
                bank = bk.next()
                for kk in range(8):
                    MM(S, ps[:, bank, 0:w], win[:, j, kk, :], hb[:, kk, sub * 512:sub * 512 + w], kk == 0, kk == 7,
                       ["win", ("hb", b, kk)], [("ps", bank)])
                if j < 4:
                    copy_alt(k, uT[:, j, sub * 512:sub * 512 + w], ps[:, bank, 0:w], [("ps", bank)], [("uT", j)])
                elif j < 8:
                    ACTF(S, qk[q][:, sub * 512:sub * 512 + w], ps[:, bank, 0:w], AF.Identity, [("ps", bank)], [("qk", q)], scale=0.125)
                else:
                    copy_alt(k, qk[q][:, sub * 512:sub * 512 + w], ps[:, bank, 0:w], [("ps", bank)], [("qk", q)])
            if j >= 4:
                dst = k.QT0[j - 4] if j < 8 else k.KT0[j - 8]
                DMA(S, "sp", dst[:, t0:t0 + n], qk[q][:, 0:n], f"qk{q}", reads=[("qk", q)])
        for tc in range(n // 128):
            gtc = t0 // 128 + tc
            q = tc % 2
            bank = bk.next()
            for kk in range(8):
                MM(S, ps[:, bank, :], hb[:, kk, tc * 128:(tc + 1) * 128], wv[:, kk, :], kk == 0, kk == 7,
                   ["wv", ("hb", b, kk)], [("ps", bank)])
            copy_alt(k, vt[q][:, :, 0:64], ps[:, bank, :].rearrange("p (h d) -> p h d", h=8), [("ps", bank)], [("vt", q)])
            DMA(S, "sp", k.VT0[gtc], vt[q][:], f"vt{q}", reads=[("vt", q)])
            for half in range(2):
                bank = bk.next()
                for gg in range(2):
                    g = half * 2 + gg
                    MM(S, ps[:, bank, gg * 256:(gg + 1) * 256], uT[:, g, tc * 128:(tc + 1) * 128], csb[:, st, :],
                       True, True, [("uT", g), "csb"], [("ps", bank)])
                copy_alt(k, uc[q][:, half * 2:half * 2 + 2, :], ps[:, bank, :].rearrange("p (a n) -> p a n", a=2),
                         [("ps", bank)], [("uc", q, half)])
            DMA(S, "sp", k.UCS[gtc], uc[q][:], f"uc{q}", reads=[("uc", q, 0), ("uc", q, 1)])


def attention(k, st_, nq, qparts, kparts_fn, v_fn, dv, chunks, scale, bias_fn, dmode="pe", ndummy=0):
    S, ps = k.S, k.ps
    st_["n"] = st_.get("n", 0) + 1
    cnum = st_["n"]
    BO = 3 + cnum % 2
    BD = 5 + cnum % 2
    pt, p2, rec, on = st_["pt"], st_["p2"], st_["rec"], st_["on"]
    nch = len(chunks)
    LA = 2
    o = cnum % 2
    mo = 128 if dmode == "merged" else dv

    def qk(jj):
        bs = st_.setdefault("sb", 0) % 3
        st_["sb"] += 1
        kp = kparts_fn(chunks[jj])
        bias = bias_fn(jj) if bias_fn else None
        nparts = len(qparts) + (1 if bias is not None else 0)
        for pi, ((qa, qkey), (ka, kkey)) in enumerate(zip(qparts, kp)):
            MM(S, ps[:, bs, 0:nq], ka, qa, pi == 0, pi == nparts - 1, [qkey, kkey], [("ps", bs)])
        if bias is not None:
            bap, bkey = bias
            MM(S, ps[:, bs, 0:nq], k.identb[:], bap, False, True, ["identb", bkey], [("ps", bs)])
        return bs

    banks = [qk(jj) for jj in range(min(LA, nch))]
    dq = []
    npairs = (nch + 1) // 2
    dcount = [0]

    def flush_d():
        while dq:
            ap, key = dq.pop(0)
            MM(S, ps[0:dv, BD, 0:nq], k.ones[:, 0:dv], ap, dcount[0] == 0, dcount[0] == npairs - 1, ["ones", key], [("ps", BD)])
            dcount[0] += 1

    prev = None
    for jj in range(nch):
        bs = banks[jj]
        if jj + LA < nch:
            banks.append(qk(jj + LA))
        if dmode == "pair":
            flush_d()
        q = st_.setdefault("pi", 0) % 4
        st_["pi"] += 1
        ACTF(S, pt[q][:, 0:nq], ps[:, bs, 0:nq], AF.Exp, [("ps", bs)], [("pt", q)], scale=scale)
        va, vkey = v_fn(chunks[jj])
        for _ in range(ndummy):
            MM(S, ps[:, 7, :], k.ones[:], k.junk[:], True, True, ["ones", "junk"], [("ps", 7)])
        if dmode == "pe":
            MM(S, ps[0:dv, BD, 0:nq], k.ones[:, 0:dv], pt[q][:, 0:nq], jj == 0, jj == nch - 1, ["ones", ("pt", q)], [("ps", BD)])
        MM(S, ps[0:mo, BO, 0:nq], va, pt[q][:, 0:nq], jj == 0, jj == nch - 1, [vkey, ("pt", q)], [("ps", BO)])
        if dmode == "pair":
            if jj % 2 == 1:
                q2 = st_.setdefault("p2i", 0) % 2
                st_["p2i"] += 1
                TT(S, p2[q2][:, 0:nq], pt[prev][:, 0:nq], pt[q][:, 0:nq], ALU.add, [("pt", prev), ("pt", q)], [("p2", q2)])
                dq.append((p2[q2][:, 0:nq], ("p2", q2)))
            elif jj == nch - 1:
                dq.append((pt[q][:, 0:nq], ("pt", q)))
        prev = q
        if jj >= 6 and jj % 2 == 0 and st_.get("pending"):
            st_["pending"].pop(0)()
    while st_.get("pending"):
        st_["pending"].pop(0)()
    if dmode in ("pair", "pe"):
        flush_d()
        RCP(S, rec[o][0:dv, 0:nq], ps[0:dv, BD, 0:nq], [("ps", BD)], [("rec", o)])
        TT(S, on[o][0:dv, 0:nq], ps[0:dv, BO, 0:nq], rec[o][0:dv, 0:nq], ALU.mult, [("ps", BO), ("rec", o)], [("on", o)])
    else:
        RCP(S, rec[o][64:128, 0:nq], ps[64:128, BO, 0:nq], [("ps", BO)], [("rec", o)])
        TT(S, on[o][0:dv, 0:nq], ps[0:dv, BO, 0:nq], rec[o][64:128, 0:nq], ALU.mult, [("ps", BO), ("rec", o)], [("on", o)])
    return on[o][0:dv, 0:nq], ("on", o)


def attn_bufs(A, nqmax):
    return dict(pt=[A.t(f"pt{i}", [128, nqmax], BF16) for i in range(4)],
                p2=[A.t(f"p2{i}", [128, nqmax], BF16) for i in range(2)],
                rec=[A.t(f"rec{i}", [128, nqmax], F32) for i in range(2)],
                on=[A.t(f"on{i}", [128, nqmax], BF16) for i in range(2)])


def na_phase(k, qtiles):
    S, A, ps = k.S, k.A, k.ps
    A.reset(k.static_end)
    kT = A.t("kT", [128, 4, NT], BF16)
    qT = A.t("qT", [128, 4, NT], BF16)
    vtk = A.t("vtk", [128, 34, 8, 128], BF16)
    bias = [A.t(f"bias{i}", [128, 8, 512], BF16) for i in range(2)]
    st_ = attn_bufs(A, 512)
    DMA(S, "sp", kT[:], k.KT0.rearrange("j p t -> p j t"), "kT", writes=["kT"])
    DMA(S, "sp", qT[:], k.QT0.rearrange("j p t -> p j t"), "qT", writes=["qT"])
    DMA(S, "sp", vtk[:], k.VT0.rearrange("c p h d -> p c h d"), "vtk", writes=["vtk"])
    nb = 0
    for h in range(8):
        hp, ho = h // 2, (h % 2) * 64
        for pat in (1, 0, 2, None):
            tiles = []
            for (q0, nq, kind) in qtiles:
                if kind == "lat":
                    i = q0 // 512
                    p_ = 0 if i == 0 else (2 if i == 7 else 1)
                    if p_ == pat:
                        c0 = int(np.clip(8 * i - 4, 0, 48)) // 2
                        tiles.append((q0, nq, list(range(c0, c0 + 8)) + [32, 33]))
                elif pat is None:
                    tiles.append((q0, nq, [32, 33]))
            if not tiles:
                continue
            bfn = None
            if pat is not None:
                bb = nb % 2
                nb += 1
                DMA(S, "sp", bias[bb][:], k.NAB[pat, h], f"bias{bb}", writes=[("bias", bb)])
                bfn = (lambda jj, bb=bb: (bias[bb][:, jj, :], ("bias", bb)) if jj < 8 else None)
            for (q0, nq, chunks) in tiles:
                on_ap, on_key = attention(
                    k, st_, nq, [(qT[ho:ho + 64, hp, q0:q0 + nq], "qT")],
                    lambda kc, hp=hp, ho=ho: [(kT[ho:ho + 64, hp, kc * 128:(kc + 1) * 128], "kT")],
                    lambda kc, h=h: (vtk[:, kc, h, :], "vtk"), 64, chunks, 1.0, bfn, dmode="merged", ndummy=1)
                DMA(S, "sp", k.CAT[4 + hp, ho:ho + 64, q0:q0 + nq], on_ap, f"on{on_key[1]}", reads=[on_key])


def fnet_phase(k, kts=range(8), do_ctx=True):
    S, A, ps = k.S, k.A, k.ps
    A.reset(k.static_end)
    ucs = A.t("ucs", [128, 34, 4, 256], BF16)
    kvec = A.t("kvec", [128, 4096], F32)
    tcol = A.t("tcol", [128, 32], F32)
    gc = A.t("gc", [128, 32, 512], BF16)
    gs = A.t("gs", [128, 32, 512], BF16)
    mS = [A.t(f"mS{i}", [128, 512], F32) for i in range(2)]
    mC = [A.t(f"mC{i}", [128, 512], F32) for i in range(2)]
    yv = [A.t(f"yv{i}", [128, 512], F32) for i in range(2)]
    nS = [A.t(f"nS{i}", [128, 512], F32) for i in range(2)]
    nC = [A.t(f"nC{i}", [128, 512], F32) for i in range(2)]
    tcol2 = A.t("tcol2", [128, 2], F32)
    at = [A.t(f"at{i}", [128, 512], BF16) for i in range(2)]
    DMA(S, "sp", ucs[:], k.UCS.rearrange("c p g n -> p c g n"), "ucs", writes=["ucs"])
    DMA(S, "sp", kvec[:], k.kvec, "kvec", writes=["kvec"])
    DMA(S, "sp", tcol[:], k.tcol, "tcol", writes=["tcol"])
    TS(S, tcol2[:], tcol[:, 0:2], 16.0, ALU.mult, ["tcol"], ["tcol"])
    bk = Banks([0, 1, 2, 3])
    ai = 0

    MAGIC = 12582912.0

    def tables(ntc, kslice, w, tcl):
        for tc0_ in range(0, ntc, 2):
            pr = [(tc0_ + i, i) for i in range(2) if tc0_ + i < ntc]
            for tc, q in pr:
                TS(S, yv[q][:, 0:w], kslice, tcl[:, tc:tc + 1], ALU.mult, ["kvec", "tcol"], [("yv", q)])
            for tc, q in pr:
                TS(S, nS[q][:, 0:w], yv[q][:, 0:w], MAGIC, ALU.add, [("yv", q)], [("nS", q)])
            for tc, q in pr:
                TS(S, nC[q][:, 0:w], yv[q][:, 0:w], 0.25, ALU.add, [("yv", q)], [("nC", q)], s2=MAGIC, op1=ALU.add)
            for tc, q in pr:
                STT(S, mS[q][:, 0:w], nS[q][:, 0:w], MAGIC, yv[q][:, 0:w], ALU.subtract, ALU.subtract, [("nS", q), ("yv", q)], [("mS", q)])
            for tc, q in pr:
                STT(S, mC[q][:, 0:w], nC[q][:, 0:w], MAGIC, yv[q][:, 0:w], ALU.subtract, ALU.subtract, [("nC", q), ("yv", q)], [("mC", q)])
            for tc, q in pr:
                ACTF(S, gs[:, tc, 0:w], mS[q][:, 0:w], AF.Sin, [("mS", q)], [("gs", tc)], scale=6.28318)
                ACTF(S, gc[:, tc, 0:w], mC[q][:, 0:w], AF.Sin, [("mC", q)], [("gc", tc)], scale=6.28318, bias=-1.570796)

    def mix(ntc, tc0, w, dst_fn):
        nonlocal ai
        for g in range(4):
            bank = bk.next()
            for tc in range(ntc):
                MM(S, ps[:, bank, 0:w], ucs[:, tc0 + tc, g, 0:128], gc[:, tc, 0:w], tc == 0, False, ["ucs", ("gc", tc)], [("ps", bank)])
                MM(S, ps[:, bank, 0:w], ucs[:, tc0 + tc, g, 128:256], gs[:, tc, 0:w], False, tc == ntc - 1, ["ucs", ("gs", tc)], [("ps", bank)])
            q = ai % 2
            ai += 1
            copy_alt(k, at[q][:, 0:w], ps[:, bank, 0:w], [("ps", bank)], [("at", q)])
            DMA(S, "sp", dst_fn(g), at[q][:, 0:w], f"at{q}", reads=[("at", q)])

    for kt in kts:
        tables(32, kvec[:, kt * 512:(kt + 1) * 512], 512, tcol)
        mix(32, 0, 512, lambda g, kt=kt: k.CAT[g, :, kt * 512:(kt + 1) * 512])
    if do_ctx:
        tables(2, kvec[:, 0:256], 256, tcol2)
        mix(2, 32, 256, lambda g: k.CAT[g, :, NL:NT])


def lin_res_phase(k, Xsrc, Xdst, ACTsrc, Wscr, l, groups):
    S, A, ps = k.S, k.A, k.ps
    A.reset(k.static_end)
    xb = [A.t(f"xb{i}", [128, 8, 1024], F32) for i in range(2)]
    cb = [A.t(f"cb{i}", [128, 8, 1024], BF16) for i in range(2)]
    wt = A.t("wt", [128, 8, 8, 128], BF16)
    DMA(S, "sp", wt[:], Wscr.rearrange("j p k m -> p j k m"), "wt", writes=["wt"])
    bk = Banks([0, 1, 2, 3, 4, 5])
    for gi, (t0, n, st) in enumerate(groups):
        b = gi % 2
        DMA(S, "sp", xb[b][:, :, 0:n], Xsrc[:, t0:t0 + n].rearrange("(j p) t -> p j t", p=128), f"xb{b}",
            writes=[("xb", b, j) for j in range(8)])
        DMA(S, "sp", cb[b][:, :, 0:n], ACTsrc[:, :, t0:t0 + n].rearrange("j p t -> p j t"), f"cb{b}", writes=[("cb", b)])
        nsub = (n + 511) // 512
        w = min(n, 512)
        for j in range(8):
            c = j * 2 + st
            for sub in range(nsub):
                bank = bk.next()
                for kk in range(8):
                    MM(S, ps[:, bank, 0:w], wt[:, j, kk, :], cb[b][:, kk, sub * 512:sub * 512 + w], kk == 0, kk == 7,
                       ["wt", ("cb", b)], [("ps", bank)])
                xs = xb[b][:, j, sub * 512:sub * 512 + w]
                STT(S, xs, ps[:, bank, 0:w], k.Gh[:, l, 1, c:c + 1], xs, ALU.mult, ALU.add, [("ps", bank), ("xb", b, j)], [("xb", b, j)])
        DMA(S, "sp", Xdst[:, t0:t0 + n].rearrange("(j p) t -> p j t", p=128), xb[b][:, :, 0:n], f"xs{b}",
            reads=[("xb", b, j) for j in range(8)])


def proj1_phase(k, groups):
    S, A, ps = k.S, k.A, k.ps
    A.reset(k.static_end)
    xb1 = A.t("xb", [128, 8, 1024], F32)
    xb = [xb1, xb1]
    hbs = [A.t(f"hb{i}", [128, 8, 1024], BF16) for i in range(2)]
    sq = [A.t(f"sq{i}", [128, 1024], BF16) for i in range(2)]
    rs = A.t("rs", [128, 1024], F32)
    tmp = [A.t(f"tmp{i}", [128, 1024], F32) for i in range(2)]
    win = A.t("win", [128, 5, 8, 128], BF16)
    wuq = A.t("wuq", [128, 16, 3, 128], BF16)
    wuk = A.t("wuk", [128, 8, 128], BF16)
    zq = A.t("zq", [128, 3, 1024], F32)
    zkv = A.t("zkv", [128, 1024], F32)
    kr = A.t("kr", [64, 2, 1024], F32)
    cqn = A.t("cqn", [128, 3, 1024], BF16)
    ckvn = A.t("ckvn", [128, 1024], F32)
    ckvb = A.t("ckvb", [128, 1024], BF16)
    krb = A.t("krb", [64, 1024], BF16)
    rq = A.t("rq", [128, 1024], F32)
    rc = A.t("rc", [64, 1024], F32)
    rsn = A.t("rsn", [64, 1024], F32)
    qn = [A.t(f"qn{i}", [128, 1024], BF16) for i in range(2)]
    qp = [A.t(f"qp{i}", [128, 1024], BF16) for i in range(2)]
    qr = [A.t(f"qr{i}", [64, 1024], BF16) for i in range(2)]
    ctok = [A.t(f"ctok{i}", [128, 128], BF16) for i in range(2)]
    onesQ = A.t("onesQ", [128, 128], BF16)
    onesK = A.t("onesK", [128, 128], BF16)
    gq = A.t("gq", [128, 3], F32)
    gkv = A.t("gkv", [128, 1], F32)
    t1 = [A.t(f"t1{i}", [64, 1024], F32) for i in range(2)]
    DMA(S, "sp", win[:], k.WIN1.rearrange("j p k m -> p j k m"), "win", writes=["win"])
    DMA(S, "sp", wuq[:], k.WUQ.rearrange("j p k m -> p j k m"), "wuq", writes=["wuq"])
    DMA(S, "sp", wuk[:], k.WUKT, "wuk", writes=["wuk"])
    DMA(S, "sp", gq[:], k.gq, "gq", writes=["gq"])
    DMA(S, "sp", gkv[:], k.gkv, "gkv", writes=["gkv"])
    MSET(S, onesQ[:], 1.0 / 384, ["onesQ"])
    MSET(S, onesK[:], 1.0 / 128, ["onesK"])
    bk = Banks([0, 1, 2, 3, 4, 5])
    ti = 0
    def ln(gi):
        t0, n, st = groups[gi]
        load_norm(k, k.X["X4"], t0, n, st, 1, 1, xb[gi % 2], sq, rs, tmp, hbs[gi % 2], gi % 2, xtag=0)

    ln(0)
    for gi, (t0, n, st) in enumerate(groups):
        b = gi % 2
        hb = hbs[b]
        if gi + 1 < len(groups):
            ln(gi + 1)
        nsub = (n + 511) // 512
        w = min(n, 512)
        if st == 0:
            DMA(S, "sp", rc[:, 0:n], k.ropec[:, t0:t0 + n], "rc", writes=["rc"])
            DMA(S, "sp", rsn[:, 0:n], k.ropes[:, t0:t0 + n], "rsn", writes=["rsn"])

        def lin5(j, m0, m1, dst, dkey):
            for sub in range(nsub):
                bank = bk.next()
                for kk in range(8):
                    MM(S, ps[0:m1 - m0, bank, 0:w], win[:, j, kk, m0:m1], hb[:, kk, sub * 512:sub * 512 + w], kk == 0, kk == 7,
                       ["win", ("hb", b, kk)], [("ps", bank)])
                copy_alt(k, dst[:, sub * 512:sub * 512 + w], ps[0:m1 - m0, bank, 0:w], [("ps", bank)], [dkey])

        for j in range(3):
            lin5(j, 0, 128, zq[:, j, :], ("zq", j))
        lin5(3, 0, 128, zkv[:, :], "zkv")
        lin5(4, 0, 64, kr[:, 0, :], ("kr", 0))
        lin5(4, 64, 128, kr[:, 1, :], ("kr", 1))
        for j in range(3):
            q = j % 2
            ACTF(S, sq[q][:, 0:n], zq[:, j, 0:n], AF.Square, [("zq", j)], [("sq", q)])
            for sub in range(nsub):
                MM(S, ps[:, 6 + sub, 0:w], onesQ[:], sq[q][:, sub * 512:sub * 512 + w], j == 0, j == 2, [("sq", q), "onesQ"], [("ps", 6 + sub)])
        ACTF(S, rq[:, 0:n], ps[:, 6:6 + nsub, 0:w].rearrange("p a n -> p (a n)"), AF.Sqrt, [("ps", 6 + i) for i in range(nsub)], ["rq"], bias=EPS)
        RCP(S, rq[:, 0:n], rq[:, 0:n], ["rq"], ["rq"])
        for j in range(3):
            STT(S, cqn[:, j, 0:n], zq[:, j, 0:n], gq[:, j:j + 1], rq[:, 0:n], ALU.mult, ALU.mult, [("zq", j), "gq", "rq"], [("cqn", j)])
        ACTF(S, sq[0][:, 0:n], zkv[:, 0:n], AF.Square, ["zkv"], [("sq", 0)])
        for sub in range(nsub):
            MM(S, ps[:, 6 + sub, 0:w], onesK[:], sq[0][:, sub * 512:sub * 512 + w], True, True, [("sq", 0), "onesK"], [("ps", 6 + sub)])
        ACTF(S, rq[:, 0:n], ps[:, 6:6 + nsub, 0:w].rearrange("p a n -> p (a n)"), AF.Sqrt, [("ps", 6 + i) for i in range(nsub)], ["rq"], bias=EPS)
        RCP(S, rq[:, 0:n], rq[:, 0:n], ["rq"], ["rq"])
        STT(S, ckvn[:, 0:n], zkv[:, 0:n], gkv[:, 0:1], rq[:, 0:n], ALU.mult, ALU.mult, ["zkv", "gkv", "rq"], ["ckvn"])
        CP(S, ckvb[:, 0:n], ckvn[:, 0:n], ["ckvn"], ["ckvb"])
        DMA(S, "sp", k.CKVT[:, t0:t0 + n], ckvb[:, 0:n], "ckvb", reads=["ckvb"])
        for tc in range(n // 128):
            bank = bk.next()
            q = tc % 2
            S.op("pe", (lambda tc=tc, bank=bank: (lambda e: e.transpose(ps[:, bank, 0:128], ckvn[:, tc * 128:(tc + 1) * 128], k.ident[:])))(),
                 ["ckvn", "ident"], [("ps", bank)])
            copy_alt(k, ctok[q][:], ps[:, bank, 0:128], [("ps", bank)], [("ctok", q)])
            DMA(S, "sp", k.CKVTOK[t0 // 128 + tc], ctok[q][:], f"ctok{q}", reads=[("ctok", q)])
        if st == 0:
            TT(S, t1[0][:, 0:n], kr[:, 0, 0:n], rc[:, 0:n], ALU.mult, [("kr", 0), "rc"], [("t1", 0)])
            TT(S, t1[1][:, 0:n], kr[:, 1, 0:n], rsn[:, 0:n], ALU.mult, [("kr", 1), "rsn"], [("t1", 1)])
            TT(S, krb[:, 0:n], t1[0][:, 0:n], t1[1][:, 0:n], ALU.add, [("t1", 0), ("t1", 1)], ["krb"])
        else:
            CP(S, krb[:, 0:n], kr[:, 0, 0:n], [("kr", 0)], ["krb"])
        DMA(S, "sp", k.KRT[0:64, t0:t0 + n], krb[:, 0:n], "krb", reads=["krb"])
        DMA(S, "sp", k.KRT[64:128, t0:t0 + n], krb[:, 0:n], "krb2", reads=["krb"])
        if st != 0:
            continue
        for h in range(8):
            q = h % 2
            for sub in range(nsub):
                bank = bk.next()
                for kk in range(3):
                    MM(S, ps[:, bank, 0:w], wuq[:, h, kk, :], cqn[:, kk, sub * 512:sub * 512 + w], kk == 0, kk == 2,
                       ["wuq", ("cqn", kk)], [("ps", bank)])
                copy_alt(k, qn[q][:, sub * 512:sub * 512 + w], ps[:, bank, 0:w], [("ps", bank)], [("qn", q)])
            for sub in range(nsub):
                bank = bk.next()
                MM(S, ps[:, bank, 0:w], wuk[:, h, :], qn[q][:, sub * 512:sub * 512 + w], True, True, ["wuk", ("qn", q)], [("ps", bank)])
                copy_alt(k, qp[q][:, sub * 512:sub * 512 + w], ps[:, bank, 0:w], [("ps", bank)], [("qp", q)])
            DMA(S, "sp", k.QP[h, :, t0:t0 + n], qp[q][:, 0:n], f"qp{q}", reads=[("qp", q)])
            hp, ho = h // 2, (h % 2) * 64
            for sub in range(nsub):
                ba, bb_ = bk.next(), bk.next()
                for part, bank in ((0, ba), (1, bb_)):
                    for kk in range(3):
                        MM(S, ps[0:64, bank, 0:w], wuq[:, 8 + 4 * part + hp, kk, ho:ho + 64], cqn[:, kk, sub * 512:sub * 512 + w],
                           kk == 0, kk == 2, ["wuq", ("cqn", kk)], [("ps", bank)])
                sl = slice(sub * 512, sub * 512 + w)
                TT(S, t1[0][:, sl], ps[0:64, ba, 0:w], rc[:, sl], ALU.mult, [("ps", ba), "rc"], [("t1", 0)])
                TT(S, t1[1][:, sl], ps[0:64, bb_, 0:w], rsn[:, sl], ALU.mult, [("ps", bb_), "rsn"], [("t1", 1)])
                TT(S, qr[q][:, sl], t1[0][:, sl], t1[1][:, sl], ALU.add, [("t1", 0), ("t1", 1)], [("qr", q)])
            DMA(S, "sp", k.QR[hp, ho:ho + 64, t0:t0 + n], qr[q][:, 0:n], f"qr{q}", reads=[("qr", q)])


def mla_phase(k, qtiles):
    S, A, ps = k.S, k.A, k.ps
    A.reset(k.static_end)
    ckvT = A.t("ckvT", [128, NT], BF16)
    krT = A.t("krT", [128, NT], BF16)
    ctok = A.t("ctok", [128, 34, 128], BF16)
    wuv = A.t("wuv", [128, 1024], BF16)
    wo = A.t("wo", [128, 8, 8, 128], BF16)
    qp = [A.t(f"qp{i}", [128, 8, 512], BF16) for i in range(2)]
    qr = [A.t(f"qr{i}", [128, 4, 512], BF16) for i in range(2)]
    xb = [A.t(f"xb{i}", [128, 8, 512], F32) for i in range(2)]
    oT = A.t("oT", [128, 8, 512], BF16)
    st_ = attn_bufs(A, 512)
    DMA(S, "sp", ckvT[:], k.CKVT, "ckvT", writes=["ckvT"])
    DMA(S, "sp", krT[:], k.KRT, "krT", writes=["krT"])
    DMA(S, "sp", ctok[:], k.CKVTOK.rearrange("c p r -> p c r"), "ctok", writes=["ctok"])
    DMA(S, "sp", wuv[:], k.WUV, "wuv", writes=["wuv"])
    DMA(S, "sp", wo[:], k.WO.rearrange("j p k m -> p j k m"), "wo", writes=["wo"])
    chunks = list(range(34))
    scale = float(192 ** -0.5)
    for qi, q0 in enumerate(qtiles):
        b = qi % 2
        DMA(S, "sp", qp[b][:], k.QP[:, :, q0:q0 + 512].rearrange("h p t -> p h t"), f"qp{b}", writes=[("qp", b)])
        DMA(S, "sp", qr[b][:], k.QR[:, :, q0:q0 + 512].rearrange("h p t -> p h t"), f"qr{b}", writes=[("qr", b)])
        DMA(S, "sp", xb[b][:], k.X["X4"][:, q0:q0 + 512].rearrange("(j p) t -> p j t", p=128), f"xb{b}",
            writes=[("xb", b, j) for j in range(8)])
        def tail(h, on_ap, on_key):
            MM(S, ps[:, 7, :], wuv[:, h * 128:(h + 1) * 128], on_ap, True, True, ["wuv", on_key], [("ps", 7)])
            copy_alt(k, oT[:, h, :], ps[:, 7, :], [("ps", 7)], [("oT", h)])

        def lin_j(j, b=b):
            for kk in range(8):
                MM(S, ps[:, 7, :], wo[:, j, kk, :], oT[:, kk, :], kk == 0, kk == 7, ["wo", ("oT", kk)], [("ps", 7)])
            STT(S, xb[b][:, j, :], ps[:, 7, :], k.Gh[:, 1, 1, 2 * j:2 * j + 1], xb[b][:, j, :], ALU.mult, ALU.add,
                [("ps", 7), ("xb", b, j)], [("xb", b, j)])

        def store(b=b, q0=q0):
            DMA(S, "sp", k.X["X5"][:, q0:q0 + 512].rearrange("(j p) t -> p j t", p=128), xb[b][:], f"xs{b}",
                reads=[("xb", b, j) for j in range(8)])

        pend = st_.setdefault("pending", [])
        for h in range(8):
            hp, ho = h // 2, (h % 2) * 64
            on_ap, on_key = attention(
                k, st_, 512, [(qp[b][:, h, :], ("qp", b)), (qr[b][ho:ho + 64, hp, :], ("qr", b))],
                lambda kc, ho=ho: [(ckvT[:, kc * 128:(kc + 1) * 128], "ckvT"), (krT[ho:ho + 64, kc * 128:(kc + 1) * 128], "krT")],
                lambda kc: (ctok[:, kc, :], "ctok"), 128, chunks, scale, None)
            pend.append(lambda h=h, a=on_ap, kk_=on_key, f=tail: f(h, a, kk_))
        for j in range(8):
            pend.append(lambda j=j, f=lin_j: f(j))
        pend.append(store)
    while st_.get("pending"):
        st_["pending"].pop(0)()


def fm(v):
    v = np.asarray(v, np.float32)
    lead = v.shape[:-1]
    return np.ascontiguousarray(np.moveaxis(v.reshape(lead + (8, 128)), -1, 0))


_CONST = {}


def consts():
    if _CONST:
        return _CONST
    c = {}
    c["ident"] = np.eye(128, dtype=np.float32)
    c["kvec"] = np.ascontiguousarray(np.broadcast_to(np.arange(4096, dtype=np.float32) / np.float32(4096.0), (128, 4096)))
    c["tcol"] = (np.arange(32)[None, :] * 128 + np.arange(128)[:, None]).astype(np.float32)
    cc = np.arange(128, dtype=np.float64)
    ang = 2 * np.pi * np.outer(cc, cc) / 128.0
    cs = np.concatenate([-np.cos(ang), np.sin(ang)], 1)
    c["csl"] = (cs / np.sqrt(4096.0 * 128.0)).astype(np.float32)
    c["csc"] = (cs / np.sqrt(256.0 * 128.0)).astype(np.float32)
    pos = np.arange(NL)
    row = (pos // 64).astype(np.float32)
    col = (pos % 64).astype(np.float32)
    inv = (np.float32(10000.0) ** (-np.arange(0, 32, 2, dtype=np.float32) / np.float32(32))).astype(np.float32)
    ang_r = row[:, None] * inv[None]
    ang_c = col[:, None] * inv[None]
    ang = np.concatenate([ang_r, ang_r, ang_c, ang_c], -1)
    sign = np.ones(64, np.float32)
    sign[0:16] = -1
    sign[32:48] = -1
    c["ropec"] = np.ascontiguousarray(np.cos(ang).T.astype(np.float32))
    c["ropes"] = np.ascontiguousarray((np.sin(ang) * sign[None]).T.astype(np.float32))
    _CONST.update(c)
    return c


ROPE_PERM = np.concatenate([np.arange(16, 32), np.arange(0, 16), np.arange(48, 64), np.arange(32, 48)])


def na_bias(rpb):
    out = np.full((3, 8, 8, 128, 512), -30000.0, np.float32)
    p = np.arange(128)
    qq = np.arange(512)
    for pat, i in enumerate((0, 1, 7)):
        krow0 = int(np.clip(8 * i - 4, 0, 48))
        qrow = 8 * i + qq // 64
        qcol = qq % 64
        r0 = np.clip(qrow - 4, 0, 56)
        cs = np.clip(qcol - 8, 0, 48)
        for jj in range(8):
            krow = krow0 + 2 * jj + p // 64
            kcol = p % 64
            valid = ((krow[:, None] >= r0[None]) & (krow[:, None] < r0[None] + 8)
                     & (kcol[:, None] >= cs[None]) & (kcol[:, None] < cs[None] + 16))
            ri = np.clip(krow[:, None] - qrow[None] + 7, 0, 14)
            ci = np.clip(kcol[:, None] - qcol[None] + 15, 0, 30)
            for h in range(8):
                g = rpb[h][ri, ci]
                out[pat, h, jj] = np.where(valid, g, np.float32(-30000.0))
    return out


def prep_inputs(inp, b):
    x0 = np.concatenate([inp["x"][b].T, inp["ctx"][b].T], axis=1)
    sc = np.stack([inp["c"][b], inp["c_ctx"]], -1).reshape(8, 128, 2).transpose(1, 0, 2)
    adb = fm(inp["ada_b"].reshape(2, 9, D))
    adb = np.repeat(adb[..., None], 2, -1).reshape(128, 2, 9, 16)
    ng = fm(inp["norm_g"])
    ng = np.repeat(ng[..., None], 2, -1).reshape(128, 2, 3, 16)
    w_in = inp["mla_w_in"][0]
    w_in_ext = np.concatenate([w_in, w_in[:, 512:576][:, ROPE_PERM]], 1)
    wq = inp["mla_w_uq"][0].reshape(384, 8, 192)
    wq_ext = np.concatenate([wq[:, :, :128].reshape(384, 1024), wq[:, :, 128:].reshape(384, 512),
                             wq[:, :, 128:][:, :, ROPE_PERM].reshape(384, 512)], 1)
    wukT = inp["mla_w_uk"][0].reshape(128, 8, 128).transpose(1, 2, 0)
    m = dict(
        x0=np.ascontiguousarray(x0, np.float32), sc=np.ascontiguousarray(sc, np.float32),
        ada_w=inp["ada_w"], adb=np.ascontiguousarray(adb), ng=np.ascontiguousarray(ng),
        fg=fm(inp["final_g"]),
        ffn_w_gate=inp["ffn_w_gate"], ffn_w_up=inp["ffn_w_up"], ffn_w_down=inp["ffn_w_down"],
        ab_w_in=inp["ab_w_in"][0], ab_w_out=inp["ab_w_out"][0],
        mla_w_in=np.ascontiguousarray(w_in_ext), mla_w_uq=np.ascontiguousarray(wq_ext),
        mla_w_ukT=np.ascontiguousarray(wukT), mla_w_uv=inp["mla_w_uv"][0], mla_w_o=inp["mla_w_o"][0],
        gq=np.ascontiguousarray(inp["mla_g_q"][0].reshape(3, 128).T), gkv=np.ascontiguousarray(inp["mla_g_kv"][0].reshape(1, 128).T),
    )
    m.update(consts())
    return m


def kernel(**inputs):
    inp = {k_: np.asarray(v) for k_, v in inputs.items()}
    nc = build()
    nab = na_bias(inp["ab_rpb"][0])
    in_maps = [prep_inputs(inp, b) for b in range(8)]
    for m in in_maps:
        m["nabias"] = nab
    res = run_bass_kernel_spmd(nc, in_maps, core_ids=list(range(8)))
    out = np.stack([np.ascontiguousarray(r["outT"].T) for r in res.results], 0)
    return out.astype(np.float32)
```

```python
import contextlib
import numpy as np
import concourse.bass as bass
import concourse.mybir as mybir
from concourse.bass_utils import run_bass_kernel_spmd

F32 = mybir.dt.float32
BF16 = mybir.dt.bfloat16
AF = mybir.ActivationFunctionType
ALU = mybir.AluOpType

D = 1024
NL = 4096
NCX = 256
NT = NL + NCX
DFF = 2816
NF = DFF // 128
EPS = 1e-6
SEM_CHUNK = 30000


class Sched:
    ENG = ("pe", "act", "dve", "pool", "sp")

    def __init__(self, nc):
        self.nc = nc
        self.ops = {e: [] for e in self.ENG}
        self.res = {}
        self.dma_cnt = {}
        self.seen = {e: {} for e in self.ENG}

    def _collect(self, eng, reads, writes):
        need = {}

        def add(src, val, raw):
            if src[0] == "e" and src[1] == eng:
                if eng in ("pe", "sp"):
                    return
            if need.get(src, 0) < val:
                need[src] = val

        for k in reads:
            st = self.res.get(k)
            if st and st[0] is not None:
                add(st[0][0], st[0][1], True)
        for k in writes:
            st = self.res.get(k)
            if st:
                if st[0] is not None:
                    add(st[0][0], st[0][1], False)
                for src, val in st[1].items():
                    add(src, val, False)
        return self._filter(eng, need)

    def _filter(self, eng, need):
        waits = []
        seen = self.seen[eng]
        for src, val in need.items():
            if seen.get(src, 0) >= val:
                continue
            seen[src] = val
            waits.append((src, val))
            if src[0] == "e":
                self.ops[src[1]][val - 1]["signal"] = True
        return waits

    def _update(self, ref, reads, writes):
        src, val = ref
        for k in writes:
            self.res[k] = [ref, {}]
        for k in reads:
            st = self.res.setdefault(k, [None, {}])
            if st[1].get(src, 0) < val:
                st[1][src] = val

    def op(self, eng, fn, reads=(), writes=()):
        waits = self._collect(eng, reads, writes)
        lst = self.ops[eng]
        lst.append(dict(fn=fn, waits=waits, signal=False, dma=None))
        self._update((("e", eng), len(lst)), reads, writes)

    def dma(self, eng, fn, sem, reads=(), writes=()):
        waits = self._collect(eng, reads, writes)
        sem = f"{eng}_{sem}"
        cnt = self.dma_cnt.get(sem, 0) + 1
        assert cnt * 16 < SEM_CHUNK
        self.dma_cnt[sem] = cnt
        self.ops[eng].append(dict(fn=fn, waits=waits, signal=False, dma=sem))
        self._update((("d", sem), cnt), reads, writes)

    def barrier(self):
        last = {}
        for f in self.ENG:
            idx = 0
            for i, o in enumerate(self.ops[f]):
                if o["dma"] is None and o["fn"] is not None:
                    idx = i + 1
            last[f] = idx
        need = {("e", f): last[f] for f in self.ENG if f != "sp" and last[f] > 0}
        for k, c in self.dma_cnt.items():
            need[("d", k)] = c
        waits = self._filter("sp", need)
        self.ops["sp"].append(dict(fn=lambda e: e.nop(), waits=waits, signal=False, dma=None))
        idx = len(self.ops["sp"])
        for e in self.ENG:
            if e != "sp":
                w = self._filter(e, {("e", "sp"): idx})
                self.ops[e].append(dict(fn=None, waits=w, signal=False, dma=None))
        self.res = {}

    def emit(self):
        nc = self.nc
        with contextlib.ExitStack() as es:
            esem = {}
            sigpos = {}
            for e in self.ENG:
                c = 0
                pos = []
                for o in self.ops[e]:
                    if o["signal"]:
                        c += 1
                    pos.append(c)
                sigpos[e] = pos
                nsem = max(1, (c + SEM_CHUNK - 1) // SEM_CHUNK)
                esem[e] = [es.enter_context(nc.semaphore(f"s_{e}_{i}")) for i in range(nsem)]
            dsem = {k: es.enter_context(nc.semaphore(f"d_{k}")) for k in self.dma_cnt}
            block = es.enter_context(nc.Block())

            def run(e, engobj):
                cnt = 0
                for o in self.ops[e]:
                    for src, val in o["waits"]:
                        if src[0] == "e":
                            c = sigpos[src[1]][val - 1]
                            assert c > 0
                            engobj.wait_ge(esem[src[1]][(c - 1) // SEM_CHUNK], (c - 1) % SEM_CHUNK + 1)
                        else:
                            engobj.wait_ge(dsem[src[1]], 16 * val)
                    if o["fn"] is None:
                        continue
                    ins = o["fn"](engobj)
                    if o["dma"] is not None:
                        ins.then_inc(dsem[o["dma"]], 16)
                    elif o["signal"]:
                        cnt += 1
                        ins.then_inc(esem[e][(cnt - 1) // SEM_CHUNK], 1)

            @block.tensor
            def _(eng):
                run("pe", eng)

            @block.scalar
            def _(eng):
                run("act", eng)

            @block.vector
            def _(eng):
                run("dve", eng)

            @block.gpsimd
            def _(eng):
                run("pool", eng)

            @block.sync
            def _(eng):
                run("sp", eng)


class Arena:
    def __init__(self, nc, base, size):
        self.nc, self.base, self.size, self.off, self.n = nc, base, size, 0, 0

    def reset(self, off=0):
        self.off = off

    def t(self, name, shape, dt):
        nb = int(np.prod(shape[1:])) * (4 if dt == F32 else 2)
        nb = (nb + 63) // 64 * 64
        assert self.off + nb <= self.size, (name, self.off, nb, self.size)
        self.n += 1
        h = self.nc.alloc_sbuf_tensor_at(f"{name}_{self.n}", list(shape), dt, offset=self.base + self.off)
        self.off += nb
        return h


class K:
    pass


def build(only=None, dbg=(), small=False, xin=()):
    nc = bass.Bass("TRN2", target_bir_lowering=False)
    k = K()
    k.nc = nc

    def din(name, shape, dt=F32):
        return nc.dram_tensor(name, list(shape), dt, kind="ExternalInput").ap()

    def dscr(name, shape, dt=BF16):
        kind = "ExternalOutput" if name in dbg else "Internal"
        return nc.dram_tensor(name, list(shape), dt, kind=kind).ap()

    k.x0 = din("x0", [D, NT])
    k.sc = din("sc", [128, 8, 2])
    k.ada_w = din("ada_w", [2, D, 9 * D])
    k.adb = din("adb", [128, 2, 9, 16])
    k.ng = din("ng", [128, 2, 3, 16])
    k.fg = din("fg", [128, 8])
    k.w_gate = din("ffn_w_gate", [2, 2, D, DFF])
    k.w_up = din("ffn_w_up", [2, 2, D, DFF])
    k.w_down = din("ffn_w_down", [2, 2, DFF, D])
    k.out = nc.dram_tensor("outT", [D, NL], F32, kind="ExternalOutput").ap()
    k.ab_w_in = din("ab_w_in", [D, 2048])
    k.ab_w_out = din("ab_w_out", [D, D])
    k.mla_w_in = din("mla_w_in", [D, 640])
    k.mla_w_uq = din("mla_w_uq", [384, 2048])
    k.mla_w_ukT = din("mla_w_ukT", [8, 128, 128])
    k.mla_w_uv = din("mla_w_uv", [128, 1024])
    k.mla_w_o = din("mla_w_o", [D, D])
    k.gq = din("gq", [128, 3])
    k.gkv = din("gkv", [128, 1])
    k.ident_d = din("ident", [128, 128])
    k.kvec = din("kvec", [128, 4096])
    k.tcol = din("tcol", [128, 32])
    k.csl = din("csl", [128, 256])
    k.csc = din("csc", [128, 256])
    k.ropec = din("ropec", [64, NL])
    k.ropes = din("ropes", [64, NL])
    k.nabias = din("nabias", [3, 8, 8, 128, 512])

    k.X = {}
    for n_ in ("X1", "X2", "X3", "X4", "X5", "X6"):
        if n_ in xin:
            k.X[n_] = din(n_, [D, NT])
        else:
            k.X[n_] = dscr(n_, [D, NT], F32)
    k.wgu = {(l, s): dscr(f"wgu{l}{s}", [NF, 128, 2, 8, 128]) for l in range(2) for s in range(2)}
    k.wd = {(l, s): dscr(f"wd{l}{s}", [8, 128, NF, 128]) for l in range(2) for s in range(2)}
    k.WIN0 = dscr("WIN0", [16, 128, 8, 128])
    k.WV0 = dscr("WV0", [128, 8, 512])
    k.WOUT0 = dscr("WOUT0", [8, 128, 8, 128])
    k.QT0 = dscr("QT0", [4, 128, NT])
    k.KT0 = dscr("KT0", [4, 128, NT])
    k.VT0 = dscr("VT0", [34, 128, 8, 128])
    k.UCS = dscr("UCS", [34, 128, 4, 256])
    k.CAT = dscr("CAT", [8, 128, NT])
    k.NAB = dscr("NAB", [3, 8, 128, 8, 512])
    k.WIN1 = dscr("WIN1", [5, 128, 8, 128])
    k.WUQ = dscr("WUQ", [16, 128, 3, 128])
    k.WUKT = dscr("WUKT", [128, 8, 128])
    k.WUV = dscr("WUV", [128, 1024])
    k.WO = dscr("WO", [8, 128, 8, 128])
    k.QP = dscr("QP", [8, 128, NL])
    k.QR = dscr("QR", [4, 128, NL])
    k.CKVT = dscr("CKVT", [128, NT])
    k.KRT = dscr("KRT", [128, NT])
    k.CKVTOK = dscr("CKVTOK", [34, 128, 128])

    arena_h = nc.alloc_sbuf_tensor("arena", [128, 204800], mybir.dt.uint8)
    base = nc.lookup_mloc(arena_h).addr
    A = Arena(nc, base, 204800)
    k.A = A
    k.ps = nc.alloc_psum_tensor("ps", [128, 8, 512], F32)
    S = Sched(nc)
    k.S = S

    k.ident = A.t("ident", [128, 128], F32)
    k.onesD = A.t("onesD", [128, 128], BF16)
    k.ones = A.t("ones", [128, 128], BF16)
    k.identb = A.t("identb", [128, 128], BF16)
    k.junk = A.t("junk", [128, 512], BF16)
    k.mod = A.t("mod", [128, 2, 9, 16], F32)
    k.Am = A.t("Am", [128, 2, 3, 16], F32)
    k.Gh = A.t("Gh", [128, 2, 3, 16], F32)
    k.fgt = A.t("fgt", [128, 8], F32)
    k.ngt = A.t("ngt", [128, 2, 3, 16], F32)
    k.adbt = A.t("adbt", [128, 2, 9, 16], F32)
    k.static_end = 8192
    assert A.off <= k.static_end

    lat_groups = [(g * 1024, 1024, 0) for g in range(4)]
    na_q = [(i * 512, 512, "lat") for i in range(8)] + [(NL, NCX, "ctx")]
    kts = list(range(8))
    mla_q = [i * 512 for i in range(8)]
    if small:
        lat_groups = [(0, 1024, 0)]
        na_q = [(0, 512, "lat"), (NL, NCX, "ctx")]
        kts = [0]
        mla_q = [0]
    if small in (2, 3):
        lat_groups = [(g * 1024, 1024, 0) for g in range(4)]
    if small == 3:
        mla_q = [0, 512, 1024]
    all_groups = lat_groups + [(NL, NCX, 1)]

    def casts_mixer0():
        cast_lin(k, k.ab_w_in, k.WIN0, D, 2048)
        DMA(S, "pool", k.WV0, k.ab_w_in[:, 1536:2048].rearrange("(k p) n -> p k n", p=128), "cast")
        cast_lin(k, k.ab_w_out, k.WOUT0, D, D)
        for pat in range(3):
            for h in range(8):
                DMA(S, "pool", k.NAB[pat, h], k.nabias[pat, h].rearrange("j p q -> p j q"), "cast")

    def casts_l1():
        cast_lin(k, k.mla_w_in, k.WIN1, D, 640)
        cast_lin(k, k.mla_w_uq, k.WUQ, 384, 2048)
        DMA(S, "pool", k.WUKT, k.mla_w_ukT.rearrange("h d r -> d h r"), "cast")
        DMA(S, "pool", k.WUV, k.mla_w_uv, "cast")
        cast_lin(k, k.mla_w_o, k.WO, D, D)

    def all_casts():
        casts_mixer0()
        casts_l1()
        for l_, s_ in ((0, 1), (1, 0), (1, 1)):
            cast_ffn(k, l_, s_)

    full = only is None
    plan = [
        ("setup", lambda: phase_setup(k, casts_mixer0 if full else all_casts)),
        ("ffn00", lambda: ffn_phase(k, 0, 0, k.x0, k.X["X1"], all_groups,
                                    (lambda: (cast_ffn(k, 0, 1), cast_ffn(k, 1, 0))) if full else None)),
        ("proj0", lambda: proj0_phase(k, all_groups, (lambda: (casts_l1(), cast_ffn(k, 1, 1))) if full else None)),
        ("na", lambda: na_phase(k, na_q)),
        ("fnet", lambda: fnet_phase(k, kts)),
        ("wout", lambda: lin_res_phase(k, k.X["X1"], k.X["X2"], k.CAT, k.WOUT0, 0, all_groups)),
        ("ffn01", lambda: ffn_phase(k, 0, 1, k.X["X2"], k.X["X3"], all_groups, None)),
        ("ffn10", lambda: ffn_phase(k, 1, 0, k.X["X3"], k.X["X4"], all_groups, None)),
        ("proj1", lambda: proj1_phase(k, all_groups)),
        ("mla", lambda: mla_phase(k, mla_q)),
        ("ffn11", lambda: ffn_phase(k, 1, 1, k.X["X5"], k.X["X6"], lat_groups, None)),
        ("final", lambda: final_phase(k, k.X["X6"], lat_groups)),
    ]
    for name, fn in plan:
        if only is not None and name not in only:
            continue
        fn()
        S.barrier()
    global LAST_S
    LAST_S = S
    S.emit()
    return nc


def MM(S, out, lhsT, rhs, start, stop, reads, writes):
    S.op("pe", lambda e: e.matmul(out, lhsT=lhsT, rhs=rhs, start=start, stop=stop), reads, writes)


def ACTF(S, out, in_, func, reads, writes, scale=1.0, bias=0.0):
    S.op("act", lambda e: e.activation(out=out, in_=in_, func=func, scale=scale, bias=bias), reads, writes)


def TT(S, out, in0, in1, op, reads, writes, eng="dve"):
    S.op(eng, lambda e: e.tensor_tensor(out=out, in0=in0, in1=in1, op=op), reads, writes)


def STT(S, out, in0, scalar, in1, op0, op1, reads, writes, eng="dve"):
    S.op(eng, lambda e: e.scalar_tensor_tensor(out=out, in0=in0, scalar=scalar, in1=in1, op0=op0, op1=op1), reads, writes)


def TS(S, out, in0, s1, op0, reads, writes, s2=None, op1=None, eng="dve"):
    if op1 is None:
        S.op(eng, lambda e: e.tensor_scalar(out=out, in0=in0, scalar1=s1, scalar2=None, op0=op0), reads, writes)
    else:
        S.op(eng, lambda e: e.tensor_scalar(out=out, in0=in0, scalar1=s1, scalar2=s2, op0=op0, op1=op1), reads, writes)


def CP(S, out, in_, reads, writes, eng="dve"):
    S.op(eng, lambda e: e.tensor_copy(out=out, in_=in_), reads, writes)


def RCP(S, out, in_, reads, writes):
    S.op("dve", lambda e: e.reciprocal(out=out, in_=in_), reads, writes)


def MSET(S, ap, val, writes, eng="dve"):
    S.op(eng, lambda e: e.memset(ap, val), (), writes)


def DMA(S, eng, out, in_, sem, reads=(), writes=()):
    S.dma(eng, lambda e: e.dma_start(out=out, in_=in_), sem, reads, writes)


def phase_setup(k, extra_casts=None):
    nc, S, A, ps = k.nc, k.S, k.A, k.ps
    A.reset(k.static_end)
    sc_f = A.t("sc_f", [128, 8, 2], F32)
    sc_b = A.t("sc_b", [128, 8, 2], BF16)
    adw = [A.t(f"adw{i}", [128, 8, 1024], BF16) for i in range(2)]
    cast_ffn(k, 0, 0)
    if extra_casts:
        extra_casts()
    DMA(S, "sp", k.ident[:], k.ident_d, "m_id", writes=["ident"])
    DMA(S, "sp", sc_f[:], k.sc, "m_sc", writes=["sc_f"])
    DMA(S, "sp", k.adbt[:], k.adb, "m_adb", writes=["adbt"])
    DMA(S, "sp", k.ngt[:], k.ng, "m_ng", writes=["ngt"])
    DMA(S, "sp", k.fgt[:], k.fg, "m_fg", writes=["fgt"])
    MSET(S, k.onesD[:], 1.0 / D, ["onesD"])
    MSET(S, k.ones[:], 1.0, ["ones"])
    MSET(S, k.junk[:], 1.0, ["junk"])
    CP(S, k.identb[:], k.ident[:], ["ident"], ["identb"])
    ACTF(S, sc_b[:], sc_f[:], AF.Silu, ["sc_f"], ["sc_b"])
    n = 0
    for l in range(2):
        for i in range(9):
            b = n % 2
            DMA(S, "pool", adw[b][:], k.ada_w[l, :, i * 1024:(i + 1) * 1024].rearrange("(k p) n -> p k n", p=128),
                f"adw{b}", writes=[("adw", b)])
            bank = n % 2
            for j in range(8):
                for kk in range(8):
                    MM(S, ps[:, bank, j * 2:j * 2 + 2], adw[b][:, kk, j * 128:(j + 1) * 128], sc_b[:, kk, :],
                       kk == 0, kk == 7, [("adw", b), "sc_b"], [("ps", bank)])
            TT(S, k.mod[:, l, i, :], ps[:, bank, 0:16], k.adbt[:, l, i, :], ALU.add, [("ps", bank), "adbt"], ["mod"])
            n += 1
    for l in range(2):
        for i in range(3):
            STT(S, k.Am[:, l, i, :], k.mod[:, l, 3 * i + 1, :], 1.0, k.ngt[:, l, i, :], ALU.add, ALU.mult,
                ["mod", "ngt"], ["Am"])
            TS(S, k.Gh[:, l, i, :], k.mod[:, l, 3 * i + 2, :], (1.0 if i == 1 else 0.5), ALU.mult, ["mod"], ["Gh"])


def cast_ffn(k, l, s):
    S = k.S
    for f in range(NF):
        for gi, W in enumerate((k.w_gate, k.w_up)):
            DMA(S, "pool", k.wgu[(l, s)][f, :, gi],
                W[l, s, :, f * 128:(f + 1) * 128].rearrange("(k p) m -> p k m", p=128), "cast")
    for dj in range(8):
        DMA(S, "pool", k.wd[(l, s)][dj],
            k.w_down[l, s, :, dj * 128:(dj + 1) * 128].rearrange("(k p) m -> p k m", p=128), "cast")


def norm_stats(k, xb_j, n, sq, rs, tag):
    S, ps = k.S, k.ps
    nsub = (n + 511) // 512
    w = min(n, 512)
    for j in range(8):
        q = j % 2
        ap, key = xb_j(j)
        ACTF(S, sq[q][:, 0:n], ap, AF.Square, [key], [("sq", q)])
        for sub in range(nsub):
            MM(S, ps[:, 6 + sub, 0:w], k.onesD[:], sq[q][:, sub * 512:sub * 512 + w], j == 0, j == 7,
               [("sq", q), "onesD"], [("ps", 6 + sub)])
    ACTF(S, rs[:, 0:n], ps[:, 6:6 + nsub, 0:w].rearrange("p a n -> p (a n)"), AF.Sqrt,
         [("ps", 6 + i) for i in range(nsub)], [tag], bias=EPS)
    RCP(S, rs[:, 0:n], rs[:, 0:n], [tag], [tag])


def ffn_phase(k, l, s, Xsrc, Xdst, groups, pre_casts=None):
    nc, S, A, ps = k.nc, k.S, k.A, k.ps
    A.reset(k.static_end)
    xb = [A.t(f"xb{i}", [128, 8, 1024], F32) for i in range(2)]
    hb = A.t("hb", [128, 8, 1024], BF16)
    act = A.t("act", [128, NF, 1024], BF16)
    sq = [A.t(f"sq{i}", [128, 1024], BF16) for i in range(2)]
    rs = [A.t(f"rs{i}", [128, 1024], F32) for i in range(2)]
    tmp = [A.t(f"tmp{i}", [128, 1024], F32) for i in range(2)]
    sg = [A.t(f"sg{i}", [128, 512], F32) for i in range(2)]
    gu = [A.t(f"gu{i}", [128, 2, 8, 128], BF16) for i in range(3)]
    wdt = [A.t(f"wd{i}", [128, NF, 128], BF16) for i in range(3)]
    isub = 0 if s == 0 else 2
    wgu_scr, wd_scr = k.wgu[(l, s)], k.wd[(l, s)]
    ng = len(groups)

    events = []
    for gi in range(ng):
        events += [("gu", f) for f in range(NF)] + [("wd", dj) for dj in range(8)]
    cnt = {"gu": 0, "wd": 0}
    slot_of = []
    for kind, idx in events:
        slot_of.append(cnt[kind] % 3)
        cnt[kind] += 1
    issued = [0]

    def issue_upto(i):
        while issued[0] <= min(i, len(events) - 1):
            ev = issued[0]
            kind, idx = events[ev]
            sl = slot_of[ev]
            if kind == "gu":
                DMA(S, "sp", gu[sl][:], wgu_scr[idx], f"gu{sl}", writes=[("gu", sl)])
            else:
                DMA(S, "sp", wdt[sl][:], wd_scr[idx], f"wd{sl}", writes=[("wd", sl)])
            issued[0] += 1

    def load_x(gi):
        t0, n, st = groups[gi]
        b = gi % 2
        DMA(S, "pool", xb[b][:, :, 0:n], Xsrc[:, t0:t0 + n].rearrange("(j p) t -> p j t", p=128),
            f"xb{b}", writes=[("xb", b, j) for j in range(8)])

    def stats(gi):
        t0, n, st = groups[gi]
        b = gi % 2
        norm_stats(k, lambda j: (xb[b][:, j, 0:n], ("xb", b, j)), n, sq, rs[b], ("rs", b))

    def hpiece(gi, j):
        t0, n, st = groups[gi]
        b = gi % 2
        q = j % 2
        c = j * 2 + st
        TT(S, tmp[q][:, 0:n], xb[b][:, j, 0:n], rs[b][:, 0:n], ALU.mult, [("xb", b, j), ("rs", b)], [("tmp", q)])
        ACTF(S, hb[:, j, 0:n], tmp[q][:, 0:n], AF.Identity, [("tmp", q), "Am", "mod"], [("hb", j)],
             scale=k.Am[:, l, isub, c:c + 1], bias=k.mod[:, l, 3 * isub, c:c + 1])

    def do_group(gi):
        t0, n, st = groups[gi]
        b = gi % 2
        nsub = (n + 511) // 512
        w = min(n, 512)
        for f in range(NF):
            ev = gi * 30 + f
            issue_upto(ev + 2)
            sl = slot_of[ev]
            for sub in range(nsub):
                it = f * nsub + sub
                bg, bu = 2 * (it % 3), 2 * (it % 3) + 1
                for half, bank in ((0, bg), (1, bu)):
                    for kk in range(8):
                        MM(S, ps[:, bank, 0:w], gu[sl][:, half, kk, :], hb[:, kk, sub * 512:sub * 512 + w],
                           kk == 0, kk == 7, [("gu", sl), ("hb", kk)], [("ps", bank)])
                q = it % 2
                ACTF(S, sg[q][:, 0:w], ps[:, bg, 0:w], AF.Silu, [("ps", bg)], [("sg", q)])
                TT(S, act[:, f, sub * 512:sub * 512 + w], sg[q][:, 0:w], ps[:, bu, 0:w], ALU.mult,
                   [("sg", q), ("ps", bu)], [("act", f)])
        if gi + 1 < ng:
            stats(gi + 1)
        for dj in range(8):
            ev = gi * 30 + NF + dj
            issue_upto(ev + 2)
            sl = slot_of[ev]
            c = dj * 2 + st
            for sub in range(nsub):
                by = (dj * nsub + sub) % 6
                for kf in range(NF):
                    MM(S, ps[:, by, 0:w], wdt[sl][:, kf, :], act[:, kf, sub * 512:sub * 512 + w],
                       kf == 0, kf == NF - 1, [("wd", sl), ("act", kf)], [("ps", by)])
                xs = xb[b][:, dj, sub * 512:sub * 512 + w]
                STT(S, xs, ps[:, by, 0:w], k.Gh[:, l, isub, c:c + 1], xs, ALU.mult, ALU.add,
                    [("ps", by), ("xb", b, dj), "Gh"], [("xb", b, dj)])
            if gi + 1 < ng:
                hpiece(gi + 1, dj)
        DMA(S, "sp", Xdst[:, t0:t0 + n].rearrange("(j p) t -> p j t", p=128), xb[b][:, :, 0:n],
            f"xs{b}", reads=[("xb", b, j) for j in range(8)])
        if gi + 2 < ng:
            load_x(gi + 2)

    load_x(0)
    if ng > 1:
        load_x(1)
    if pre_casts:
        pre_casts()
    issue_upto(1)
    stats(0)
    for j in range(8):
        hpiece(0, j)
    for gi in range(ng):
        do_group(gi)


def final_phase(k, Xsrc, groups):
    S, A = k.S, k.A
    A.reset(k.static_end)
    xb = [A.t(f"xb{i}", [128, 8, 1024], F32) for i in range(2)]
    sq = [A.t(f"sq{i}", [128, 1024], BF16) for i in range(2)]
    rs = [A.t(f"rs{i}", [128, 1024], F32) for i in range(2)]

    def load(gi):
        t0, n, st = groups[gi]
        b = gi % 2
        DMA(S, "sp", xb[b][:, :, 0:n], Xsrc[:, t0:t0 + n].rearrange("(j p) t -> p j t", p=128),
            f"xb{b}", writes=[("xb", b, j) for j in range(8)])

    def grp(gi):
        t0, n, st = groups[gi]
        b = gi % 2
        norm_stats(k, lambda j: (xb[b][:, j, 0:n], ("xb", b, j)), n, sq, rs[b], ("rs", b))
        for j in range(8):
            STT(S, xb[b][:, j, 0:n], xb[b][:, j, 0:n], k.fgt[:, j:j + 1], rs[b][:, 0:n], ALU.mult, ALU.mult,
                [("xb", b, j), ("rs", b), "fgt"], [("xb", b, j)])
        DMA(S, "pool", k.out[:, t0:t0 + n].rearrange("(j p) t -> p j t", p=128), xb[b][:, :, 0:n],
            f"xs{b}", reads=[("xb", b, j) for j in range(8)])

    ng = len(groups)
    load(0)
    if ng > 1:
        load(1)
    for gi in range(ng):
        grp(gi)
        if gi + 2 < ng:
            load(gi + 2)


def cast_lin(k, src, dst, K_, N_):
    for j in range(N_ // 128):
        DMA(k.S, "pool", dst[j], src[:, j * 128:(j + 1) * 128].rearrange("(k p) m -> p k m", p=128), "cast")


class Banks:
    def __init__(self, lst):
        self.lst, self.i = lst, 0

    def next(self):
        b = self.lst[self.i % len(self.lst)]
        self.i += 1
        return b


def load_norm(k, Xsrc, t0, n, st, l, isub, xb, sq, rs, tmp, hb, tag, xtag=None):
    S = k.S
    xt = tag if xtag is None else xtag
    DMA(S, "sp", xb[:, :, 0:n], Xsrc[:, t0:t0 + n].rearrange("(j p) t -> p j t", p=128), f"xb{xt}",
        writes=[("xb", xt, j) for j in range(8)])
    norm_stats(k, lambda j: (xb[:, j, 0:n], ("xb", xt, j)), n, sq, rs, ("rs", 0))
    for j in range(8):
        q = j % 2
        c = j * 2 + st
        TT(S, tmp[q][:, 0:n], xb[:, j, 0:n], rs[:, 0:n], ALU.mult, [("xb", xt, j), ("rs", 0)], [("tmp", q)])
        ACTF(S, hb[:, j, 0:n], tmp[q][:, 0:n], AF.Identity, [("tmp", q)], [("hb", tag, j)],
             scale=k.Am[:, l, isub, c:c + 1], bias=k.mod[:, l, 3 * isub, c:c + 1])


def copy_alt(k, out, in_, reads, writes):
    k.cpi = getattr(k, "cpi", 0) + 1
    if k.cpi % 2:
        ACTF(k.S, out, in_, AF.Identity, reads, writes)
    else:
        CP(k.S, out, in_, reads, writes)


def proj0_phase(k, groups, pre_casts=None):
    S, A, ps = k.S, k.A, k.ps
    A.reset(k.static_end)
    xb = [A.t(f"xb{i}", [128, 8, 1024], F32) for i in range(2)]
    hbs = [A.t(f"hb{i}", [128, 8, 1024], BF16) for i in range(2)]
    win = A.t("win", [128, 12, 8, 128], BF16)
    wv = A.t("wv", [128, 8, 512], BF16)
    sq = [A.t(f"sq{i}", [128, 1024], BF16) for i in range(2)]
    rs = A.t("rs", [128, 1024], F32)
    tmp = [A.t(f"tmp{i}", [128, 1024], F32) for i in range(2)]
    uT = A.t("uT", [128, 4, 1024], BF16)
    qk = [A.t(f"qk{i}", [128, 1024], BF16) for i in range(2)]
    vt = [A.t(f"vt{i}", [128, 8, 128], BF16) for i in range(2)]
    uc = [A.t(f"uc{i}", [128, 4, 256], BF16) for i in range(2)]
    csf = A.t("csf", [128, 2, 256], F32)
    csb = A.t("csb", [128, 2, 256], BF16)
    DMA(S, "sp", win[:], k.WIN0[0:12].rearrange("j p k m -> p j k m"), "win", writes=["win"])
    DMA(S, "sp", wv[:], k.WV0, "wv", writes=["wv"])
    DMA(S, "sp", csf[:, 0, :], k.csl, "csl", writes=["csf0"])
    DMA(S, "sp", csf[:, 1, :], k.csc, "csc", writes=["csf1"])
    CP(S, csb[:], csf[:], ["csf0", "csf1"], ["csb"])
    for i_ in range(2):
        MSET(S, vt[i_][:, :, 64:128], 1.0, [("vt", i_)])
    if pre_casts:
        pre_casts()
    bk = Banks([0, 1, 2, 3, 4, 5])
    n_i = 0
    def ln(gi):
        t0, n, st = groups[gi]
        load_norm(k, k.X["X1"], t0, n, st, 0, 1, xb[gi % 2], sq, rs, tmp, hbs[gi % 2], gi % 2)

    ln(0)
    for gi, (t0, n, st) in enumerate(groups):
        b = gi % 2
        hb = hbs[b]
        if gi + 1 < len(groups):
            ln(gi + 1)
        nsub = (n + 511) // 512
        w = min(n, 512)
        for j in range(12):
            q = n_i % 2
            n_i += 1
            for sub in range(nsub):
                bank = bk.next()
                for kk in range(8):
                    MM(S, ps[:, bank, 0:w], win[:, j, kk, :], hb[:, kk, sub * 512:sub * 512 + w], kk == 0, kk == 7,
                       ["win", ("hb", b, kk)], [("ps", bank)])
                if j < 4:
                    copy_alt(k, uT[:, j, sub * 512:sub * 512 + w], ps[:, bank, 0:w], [("ps", bank)], [("uT", j)])
                elif j < 8:
                    ACTF(S, qk[q][:, sub * 512:sub * 512 + w], ps[:, bank, 0:w], AF.Identity, [("ps", bank)], [("qk", q)], scale=0.125)
                else:
                    copy_alt(k, qk[q][:, sub * 512:sub * 512 + w], ps[:, bank, 0:w], [("ps", bank)], [("qk", q)])
            if j >= 4:
                dst = k.QT0[j - 4] if j < 8 else k.KT0[j - 8]
                DMA(S, "sp", dst[:, t0:t0 + n], qk[q][:, 0:n], f"qk{q}", reads=[("qk", q)])
        for tc in range(n // 128):
            gtc = t0 // 128 + tc
            q = tc % 2
            bank = bk.next()
            for kk in range(8):
                MM(S, ps[:, bank, :], hb[:, kk, tc * 128:(tc + 1) * 128], wv[:, kk, :], kk == 0, kk == 7,
                   ["wv", ("hb", b, kk)], [("ps", bank)])
            copy_alt(k, vt[q][:, :, 0:64], ps[:, bank, :].rearrange("p (h d) -> p h d", h=8), [("ps", bank)], [("vt", q)])
            DMA(S, "sp", k.VT0[gtc], vt[q][:], f"vt{q}", reads=[("vt", q)])
            for half in range(2):
                bank = bk.next()
                for gg in range(2):
                    g = half * 2 + gg
                    MM(S, ps[:, bank, gg * 256:(gg + 1) * 256], uT[:, g, tc * 128:(tc + 1) * 128], csb[:, st, :],
                       True, True, [("uT", g), "csb"], [("ps", bank)])
                copy_alt(k, uc[q][:, half * 2:half * 2 + 2, :], ps[:, bank, :].rearrange("p (a n) -> p a n", a=2),
                         [("ps", bank)], [("uc", q, half)])
            DMA(S, "sp", k.UCS[gtc], uc[q][:], f"uc{q}", reads=[("uc", q, 0), ("uc", q, 1)])


def attention(k, st_, nq, qparts, kparts_fn, v_fn, dv, chunks, scale, bias_fn, dmode="pe", ndummy=0):
    S, ps = k.S, k.ps
    st_["n"] = st_.get("n", 0) + 1
    cnum = st_["n"]
    BO = 3 + cnum % 2
    BD = 5 + cnum % 2
    pt, p2, rec, on = st_["pt"], st_["p2"], st_["rec"], st_["on"]
    nch = len(chunks)
    LA = 2
    o = cnum % 2
    mo = 128 if dmode == "merged" else dv

    def qk(jj):
        bs = st_.setdefault("sb", 0) % 3
        st_["sb"] += 1
        kp = kparts_fn(chunks[jj])
        bias = bias_fn(jj) if bias_fn else None
        nparts = len(qparts) + (1 if bias is not None else 0)
        for pi, ((qa, qkey), (ka, kkey)) in enumerate(zip(qparts, kp)):
            MM(S, ps[:, bs, 0:nq], ka, qa, pi == 0, pi == nparts - 1, [qkey, kkey], [("ps", bs)])
        if bias is not None:
            bap, bkey = bias
            MM(S, ps[:, bs, 0:nq], k.identb[:], bap, False, True, ["identb", bkey], [("ps", bs)])
        return bs

    banks = [qk(jj) for jj in range(min(LA, nch))]
    dq = []
    npairs = (nch + 1) // 2
    dcount = [0]

    def flush_d():
        while dq:
            ap, key = dq.pop(0)
            MM(S, ps[0:dv, BD, 0:nq], k.ones[:, 0:dv], ap, dcount[0] == 0, dcount[0] == npairs - 1, ["ones", key], [("ps", BD)])
            dcount[0] += 1

    prev = None
    for jj in range(nch):
        bs = banks[jj]
        if jj + LA < nch:
            banks.append(qk(jj + LA))
        if dmode == "pair":
            flush_d()
        q = st_.setdefault("pi", 0) % 4
        st_["pi"] += 1
        ACTF(S, pt[q][:, 0:nq], ps[:, bs, 0:nq], AF.Exp, [("ps", bs)], [("pt", q)], scale=scale)
        va, vkey = v_fn(chunks[jj])
        for _ in range(ndummy):
            MM(S, ps[:, 7, :], k.ones[:], k.junk[:], True, True, ["ones", "junk"], [("ps", 7)])
        if dmode == "pe":
            MM(S, ps[0:dv, BD, 0:nq], k.ones[:, 0:dv], pt[q][:, 0:nq], jj == 0, jj == nch - 1, ["ones", ("pt", q)], [("ps", BD)])
        MM(S, ps[0:mo, BO, 0:nq], va, pt[q][:, 0:nq], jj == 0, jj == nch - 1, [vkey, ("pt", q)], [("ps", BO)])
        if dmode == "pair":
            if jj % 2 == 1:
                q2 = st_.setdefault("p2i", 0) % 2
                st_["p2i"] += 1
                TT(S, p2[q2][:, 0:nq], pt[prev][:, 0:nq], pt[q][:, 0:nq], ALU.add, [("pt", prev), ("pt", q)], [("p2", q2)])
                dq.append((p2[q2][:, 0:nq], ("p2", q2)))
            elif jj == nch - 1:
                dq.append((pt[q][:, 0:nq], ("pt", q)))
        prev = q
        if jj >= 6 and jj % 2 == 0 and st_.get("pending"):
            st_["pending"].pop(0)()
    while st_.get("pending"):
        st_["pending"].pop(0)()
    if dmode in ("pair", "pe"):
        flush_d()
        RCP(S, rec[o][0:dv, 0:nq], ps[0:dv, BD, 0:nq], [("ps", BD)], [("rec", o)])
        TT(S, on[o][0:dv, 0:nq], ps[0:dv, BO, 0:nq], rec[o][0:dv, 0:nq], ALU.mult, [("ps", BO), ("rec", o)], [("on", o)])
    else:
        RCP(S, rec[o][64:128, 0:nq], ps[64:128, BO, 0:nq], [("ps", BO)], [("rec", o)])
        TT(S, on[o][0:dv, 0:nq], ps[0:dv, BO, 0:nq], rec[o][64:128, 0:nq], ALU.mult, [("ps", BO), ("rec", o)], [("on", o)])
    return on[o][0:dv, 0:nq], ("on", o)


def attn_bufs(A, nqmax):
    return dict(pt=[A.t(f"pt{i}", [128, nqmax], BF16) for i in range(4)],
                p2=[A.t(f"p2{i}", [128, nqmax], BF16) for i in range(2)],
                rec=[A.t(f"rec{i}", [128, nqmax], F32) for i in range(2)],
                on=[A.t(f"on{i}", [128, nqmax], BF16) for i in range(2)])


def na_phase(k, qtiles):
    S, A, ps = k.S, k.A, k.ps
    A.reset(k.static_end)
    kT = A.t("kT", [128, 4, NT], BF16)
    qT = A.t("qT", [128, 4, NT], BF16)
    vtk = A.t("vtk", [128, 34, 8, 128], BF16)
    bias = [A.t(f"bias{i}", [128, 8, 512], BF16) for i in range(2)]
    st_ = attn_bufs(A, 512)
    DMA(S, "sp", kT[:], k.KT0.rearrange("j p t -> p j t"), "kT", writes=["kT"])
    DMA(S, "sp", qT[:], k.QT0.rearrange("j p t -> p j t"), "qT", writes=["qT"])
    DMA(S, "sp", vtk[:], k.VT0.rearrange("c p h d -> p c h d"), "vtk", writes=["vtk"])
    nb = 0
    for h in range(8):
        hp, ho = h // 2, (h % 2) * 64
        for pat in (1, 0, 2, None):
            tiles = []
            for (q0, nq, kind) in qtiles:
                if kind == "lat":
                    i = q0 // 512
                    p_ = 0 if i == 0 else (2 if i == 7 else 1)
                    if p_ == pat:
                        c0 = int(np.clip(8 * i - 4, 0, 48)) // 2
                        tiles.append((q0, nq, list(range(c0, c0 + 8)) + [32, 33]))
                elif pat is None:
                    tiles.append((q0, nq, [32, 33]))
            if not tiles:
                continue
            bfn = None
            if pat is not None:
                bb = nb % 2
                nb += 1
                DMA(S, "sp", bias[bb][:], k.NAB[pat, h], f"bias{bb}", writes=[("bias", bb)])
                bfn = (lambda jj, bb=bb: (bias[bb][:, jj, :], ("bias", bb)) if jj < 8 else None)
            for (q0, nq, chunks) in tiles:
                on_ap, on_key = attention(
                    k, st_, nq, [(qT[ho:ho + 64, hp, q0:q0 + nq], "qT")],
                    lambda kc, hp=hp, ho=ho: [(kT[ho:ho + 64, hp, kc * 128:(kc + 1) * 128], "kT")],
                    lambda kc, h=h: (vtk[:, kc, h, :], "vtk"), 64, chunks, 1.0, bfn, dmode="merged", ndummy=1)
                DMA(S, "sp", k.CAT[4 + hp, ho:ho + 64, q0:q0 + nq], on_ap, f"on{on_key[1]}", reads=[on_key])


def fnet_phase(k, kts=range(8), do_ctx=True):
    S, A, ps = k.S, k.A, k.ps
    A.reset(k.static_end)
    ucs = A.t("ucs", [128, 34, 4, 256], BF16)
    kvec = A.t("kvec", [128, 4096], F32)
    tcol = A.t("tcol", [128, 32], F32)
    gc = A.t("gc", [128, 32, 512], BF16)
    gs = A.t("gs", [128, 32, 512], BF16)
    mS = [A.t(f"mS{i}", [128, 512], F32) for i in range(2)]
    mC = [A.t(f"mC{i}", [128, 512], F32) for i in range(2)]
    yv = [A.t(f"yv{i}", [128, 512], F32) for i in range(2)]
    nS = [A.t(f"nS{i}", [128, 512], F32) for i in range(2)]
    nC = [A.t(f"nC{i}", [128, 512], F32) for i in range(2)]
    tcol2 = A.t("tcol2", [128, 2], F32)
    at = [A.t(f"at{i}", [128, 512], BF16) for i in range(2)]
    DMA(S, "sp", ucs[:], k.UCS.rearrange("c p g n -> p c g n"), "ucs", writes=["ucs"])
    DMA(S, "sp", kvec[:], k.kvec, "kvec", writes=["kvec"])
    DMA(S, "sp", tcol[:], k.tcol, "tcol", writes=["tcol"])
    TS(S, tcol2[:], tcol[:, 0:2], 16.0, ALU.mult, ["tcol"], ["tcol"])
    bk = Banks([0, 1, 2, 3])
    ai = 0

    MAGIC = 12582912.0

    def tables(ntc, kslice, w, tcl):
        for tc0_ in range(0, ntc, 2):
            pr = [(tc0_ + i, i) for i in range(2) if tc0_ + i < ntc]
            for tc, q in pr:
                TS(S, yv[q][:, 0:w], kslice, tcl[:, tc:tc + 1], ALU.mult, ["kvec", "tcol"], [("yv", q)])
            for tc, q in pr:
                TS(S, nS[q][:, 0:w], yv[q][:, 0:w], MAGIC, ALU.add, [("yv", q)], [("nS", q)])
            for tc, q in pr:
                TS(S, nC[q][:, 0:w], yv[q][:, 0:w], 0.25, ALU.add, [("yv", q)], [("nC", q)], s2=MAGIC, op1=ALU.add)
            for tc, q in pr:
                STT(S, mS[q][:, 0:w], nS[q][:, 0:w], MAGIC, yv[q][:, 0:w], ALU.subtract, ALU.subtract, [("nS", q), ("yv", q)], [("mS", q)])
            for tc, q in pr:
                STT(S, mC[q][:, 0:w], nC[q][:, 0:w], MAGIC, yv[q][:, 0:w], ALU.subtract, ALU.subtract, [("nC", q), ("yv", q)], [("mC", q)])
            for tc, q in pr:
                ACTF(S, gs[:, tc, 0:w], mS[q][:, 0:w], AF.Sin, [("mS", q)], [("gs", tc)], scale=6.28318)
                ACTF(S, gc[:, tc, 0:w], mC[q][:, 0:w], AF.Sin, [("mC", q)], [("gc", tc)], scale=6.28318, bias=-1.570796)

    def mix(ntc, tc0, w, dst_fn):
        nonlocal ai
        for g in range(4):
            bank = bk.next()
            for tc in range(ntc):
                MM(S, ps[:, bank, 0:w], ucs[:, tc0 + tc, g, 0:128], gc[:, tc, 0:w], tc == 0, False, ["ucs", ("gc", tc)], [("ps", bank)])
                MM(S, ps[:, bank, 0:w], ucs[:, tc0 + tc, g, 128:256], gs[:, tc, 0:w], False, tc == ntc - 1, ["ucs", ("gs", tc)], [("ps", bank)])
            q = ai % 2
            ai += 1
            copy_alt(k, at[q][:, 0:w], ps[:, bank, 0:w], [("ps", bank)], [("at", q)])
            DMA(S, "sp", dst_fn(g), at[q][:, 0:w], f"at{q}", reads=[("at", q)])

    for kt in kts:
        tables(32, kvec[:, kt * 512:(kt + 1) * 512], 512, tcol)
        mix(32, 0, 512, lambda g, kt=kt: k.CAT[g, :, kt * 512:(kt + 1) * 512])
    if do_ctx:
        tables(2, kvec[:, 0:256], 256, tcol2)
        mix(2, 32, 256, lambda g: k.CAT[g, :, NL:NT])


def lin_res_phase(k, Xsrc, Xdst, ACTsrc, Wscr, l, groups):
    S, A, ps = k.S, k.A, k.ps
    A.reset(k.static_end)
    xb = [A.t(f"xb{i}", [128, 8, 1024], F32) for i in range(2)]
    cb = [A.t(f"cb{i}", [128, 8, 1024], BF16) for i in range(2)]
    wt = A.t("wt", [128, 8, 8, 128], BF16)
    DMA(S, "sp", wt[:], Wscr.rearrange("j p k m -> p j k m"), "wt", writes=["wt"])
    bk = Banks([0, 1, 2, 3, 4, 5])

    def load(gi):
        t0, n, st = groups[gi]
        b = gi % 2
        DMA(S, "sp", cb[b][:, :, 0:n], ACTsrc[:, :, t0:t0 + n].rearrange("j p t -> p j t"), f"cb{b}", writes=[("cb", b)])
        DMA(S, "sp", xb[b][:, :, 0:n], Xsrc[:, t0:t0 + n].rearrange("(j p) t -> p j t", p=128), f"xb{b}",
            writes=[("xb", b, j) for j in range(8)])

    load(0)
    for gi, (t0, n, st) in enumerate(groups):
        b = gi % 2
        if gi + 1 < len(groups):
            load(gi + 1)
        nsub = (n + 511) // 512
        w = min(n, 512)
        for j in range(8):
            c = j * 2 + st
            for sub in range(nsub):
                bank = bk.next()
                for kk in range(8):
                    MM(S, ps[:, bank, 0:w], wt[:, j, kk, :], cb[b][:, kk, sub * 512:sub * 512 + w], kk == 0, kk == 7,
                       ["wt", ("cb", b)], [("ps", bank)])
                xs = xb[b][:, j, sub * 512:sub * 512 + w]
                STT(S, xs, ps[:, bank, 0:w], k.Gh[:, l, 1, c:c + 1], xs, ALU.mult, ALU.add, [("ps", bank), ("xb", b, j)], [("xb", b, j)])
        DMA(S, "pool", Xdst[:, t0:t0 + n].rearrange("(j p) t -> p j t", p=128), xb[b][:, :, 0:n], f"xs{b}",
            reads=[("xb", b, j) for j in range(8)])


def proj1_phase(k, groups):
    S, A, ps = k.S, k.A, k.ps
    A.reset(k.static_end)
    xb1 = A.t("xb", [128, 8, 1024], F32)
    xb = [xb1, xb1]
    hbs = [A.t(f"hb{i}", [128, 8, 1024], BF16) for i in range(2)]
    sq = [A.t(f"sq{i}", [128, 1024], BF16) for i in range(2)]
    rs = A.t("rs", [128, 1024], F32)
    tmp = [A.t(f"tmp{i}", [128, 1024], F32) for i in range(2)]
    win = A.t("win", [128, 5, 8, 128], BF16)
    wuq = A.t("wuq", [128, 16, 3, 128], BF16)
    wuk = A.t("wuk", [128, 8, 128], BF16)
    zq = A.t("zq", [128, 3, 1024], F32)
    zkv = A.t("zkv", [128, 1024], F32)
    kr = A.t("kr", [64, 2, 1024], F32)
    cqn = A.t("cqn", [128, 3, 1024], BF16)
    ckvn = A.t("ckvn", [128, 1024], F32)
    ckvb = A.t("ckvb", [128, 1024], BF16)
    krb = A.t("krb", [64, 1024], BF16)
    rq = A.t("rq", [128, 1024], F32)
    rc = A.t("rc", [64, 1024], F32)
    rsn = A.t("rsn", [64, 1024], F32)
    qn = [A.t(f"qn{i}", [128, 1024], BF16) for i in range(2)]
    qp = [A.t(f"qp{i}", [128, 1024], BF16) for i in range(2)]
    qr = [A.t(f"qr{i}", [64, 1024], BF16) for i in range(2)]
    ctok = [A.t(f"ctok{i}", [128, 128], BF16) for i in range(2)]
    onesQ = A.t("onesQ", [128, 128], BF16)
    onesK = A.t("onesK", [128, 128], BF16)
    gq = A.t("gq", [128, 3], F32)
    gkv = A.t("gkv", [128, 1], F32)
    t1 = [A.t(f"t1{i}", [64, 1024], F32) for i in range(2)]
    DMA(S, "sp", win[:], k.WIN1.rearrange("j p k m -> p j k m"), "win", writes=["win"])
    DMA(S, "sp", wuq[:], k.WUQ.rearrange("j p k m -> p j k m"), "wuq", writes=["wuq"])
    DMA(S, "sp", wuk[:], k.WUKT, "wuk", writes=["wuk"])
    DMA(S, "sp", gq[:], k.gq, "gq", writes=["gq"])
    DMA(S, "sp", gkv[:], k.gkv, "gkv", writes=["gkv"])
    MSET(S, onesQ[:], 1.0 / 384, ["onesQ"])
    MSET(S, onesK[:], 1.0 / 128, ["onesK"])
    bk = Banks([0, 1, 2, 3, 4, 5])
    ti = 0
    def ln(gi):
        t0, n, st = groups[gi]
        load_norm(k, k.X["X4"], t0, n, st, 1, 1, xb[gi % 2], sq, rs, tmp, hbs[gi % 2], gi % 2, xtag=0)

    ln(0)
    for gi, (t0, n, st) in enumerate(groups):
        b = gi % 2
        hb = hbs[b]
        if gi + 1 < len(groups):
            ln(gi + 1)
        nsub = (n + 511) // 512
        w = min(n, 512)
        if st == 0:
            DMA(S, "sp", rc[:, 0:n], k.ropec[:, t0:t0 + n], "rc", writes=["rc"])
            DMA(S, "sp", rsn[:, 0:n], k.ropes[:, t0:t0 + n], "rsn", writes=["rsn"])

        def lin5(j, m0, m1, dst, dkey):
            for sub in range(nsub):
                bank = bk.next()
                for kk in range(8):
                    MM(S, ps[0:m1 - m0, bank, 0:w], win[:, j, kk, m0:m1], hb[:, kk, sub * 512:sub * 512 + w], kk == 0, kk == 7,
                       ["win", ("hb", b, kk)], [("ps", bank)])
                copy_alt(k, dst[:, sub * 512:sub * 512 + w], ps[0:m1 - m0, bank, 0:w], [("ps", bank)], [dkey])

        for j in range(3):
            lin5(j, 0, 128, zq[:, j, :], ("zq", j))
        lin5(3, 0, 128, zkv[:, :], "zkv")
        lin5(4, 0, 64, kr[:, 0, :], ("kr", 0))
        lin5(4, 64, 128, kr[:, 1, :], ("kr", 1))
        for j in range(3):
            q = j % 2
            ACTF(S, sq[q][:, 0:n], zq[:, j, 0:n], AF.Square, [("zq", j)], [("sq", q)])
            for sub in range(nsub):
                MM(S, ps[:, 6 + sub, 0:w], onesQ[:], sq[q][:, sub * 512:sub * 512 + w], j == 0, j == 2, [("sq", q), "onesQ"], [("ps", 6 + sub)])
        ACTF(S, rq[:, 0:n], ps[:, 6:6 + nsub, 0:w].rearrange("p a n -> p (a n)"), AF.Sqrt, [("ps", 6 + i) for i in range(nsub)], ["rq"], bias=EPS)
        RCP(S, rq[:, 0:n], rq[:, 0:n], ["rq"], ["rq"])
        for j in range(3):
            STT(S, cqn[:, j, 0:n], zq[:, j, 0:n], gq[:, j:j + 1], rq[:, 0:n], ALU.mult, ALU.mult, [("zq", j), "gq", "rq"], [("cqn", j)])
        ACTF(S, sq[0][:, 0:n], zkv[:, 0:n], AF.Square, ["zkv"], [("sq", 0)])
        for sub in range(nsub):
            MM(S, ps[:, 6 + sub, 0:w], onesK[:], sq[0][:, sub * 512:sub * 512 + w], True, True, [("sq", 0), "onesK"], [("ps", 6 + sub)])
        ACTF(S, rq[:, 0:n], ps[:, 6:6 + nsub, 0:w].rearrange("p a n -> p (a n)"), AF.Sqrt, [("ps", 6 + i) for i in range(nsub)], ["rq"], bias=EPS)
        RCP(S, rq[:, 0:n], rq[:, 0:n], ["rq"], ["rq"])
        STT(S, ckvn[:, 0:n], zkv[:, 0:n], gkv[:, 0:1], rq[:, 0:n], ALU.mult, ALU.mult, ["zkv", "gkv", "rq"], ["ckvn"])
        CP(S, ckvb[:, 0:n], ckvn[:, 0:n], ["ckvn"], ["ckvb"])
        DMA(S, "sp", k.CKVT[:, t0:t0 + n], ckvb[:, 0:n], "ckvb", reads=["ckvb"])
        for tc in range(n // 128):
            bank = bk.next()
            q = tc % 2
            S.op("pe", (lambda tc=tc, bank=bank: (lambda e: e.transpose(ps[:, bank, 0:128], ckvn[:, tc * 128:(tc + 1) * 128], k.ident[:])))(),
                 ["ckvn", "ident"], [("ps", bank)])
            copy_alt(k, ctok[q][:], ps[:, bank, 0:128], [("ps", bank)], [("ctok", q)])
            DMA(S, "sp", k.CKVTOK[t0 // 128 + tc], ctok[q][:], f"ctok{q}", reads=[("ctok", q)])
        if st == 0:
            TT(S, t1[0][:, 0:n], kr[:, 0, 0:n], rc[:, 0:n], ALU.mult, [("kr", 0), "rc"], [("t1", 0)])
            TT(S, t1[1][:, 0:n], kr[:, 1, 0:n], rsn[:, 0:n], ALU.mult, [("kr", 1), "rsn"], [("t1", 1)])
            TT(S, krb[:, 0:n], t1[0][:, 0:n], t1[1][:, 0:n], ALU.add, [("t1", 0), ("t1", 1)], ["krb"])
        else:
            CP(S, krb[:, 0:n], kr[:, 0, 0:n], [("kr", 0)], ["krb"])
        DMA(S, "sp", k.KRT[0:64, t0:t0 + n], krb[:, 0:n], "krb", reads=["krb"])
        DMA(S, "sp", k.KRT[64:128, t0:t0 + n], krb[:, 0:n], "krb2", reads=["krb"])
        if st != 0:
            continue
        for h in range(8):
            q = h % 2
            for sub in range(nsub):
                bank = bk.next()
                for kk in range(3):
                    MM(S, ps[:, bank, 0:w], wuq[:, h, kk, :], cqn[:, kk, sub * 512:sub * 512 + w], kk == 0, kk == 2,
                       ["wuq", ("cqn", kk)], [("ps", bank)])
                copy_alt(k, qn[q][:, sub * 512:sub * 512 + w], ps[:, bank, 0:w], [("ps", bank)], [("qn", q)])
            for sub in range(nsub):
                bank = bk.next()
                MM(S, ps[:, bank, 0:w], wuk[:, h, :], qn[q][:, sub * 512:sub * 512 + w], True, True, ["wuk", ("qn", q)], [("ps", bank)])
                copy_alt(k, qp[q][:, sub * 512:sub * 512 + w], ps[:, bank, 0:w], [("ps", bank)], [("qp", q)])
            DMA(S, "sp", k.QP[h, :, t0:t0 + n], qp[q][:, 0:n], f"qp{q}", reads=[("qp", q)])
            hp, ho = h // 2, (h % 2) * 64
            for sub in range(nsub):
                ba, bb_ = bk.next(), bk.next()
                for part, bank in ((0, ba), (1, bb_)):
                    for kk in range(3):
                        MM(S, ps[0:64, bank, 0:w], wuq[:, 8 + 4 * part + hp, kk, ho:ho + 64], cqn[:, kk, sub * 512:sub * 512 + w],
                           kk == 0, kk == 2, ["wuq", ("cqn", kk)], [("ps", bank)])
                sl = slice(sub * 512, sub * 512 + w)
                TT(S, t1[0][:, sl], ps[0:64, ba, 0:w], rc[:, sl], ALU.mult, [("ps", ba), "rc"], [("t1", 0)])
                TT(S, t1[1][:, sl], ps[0:64, bb_, 0:w], rsn[:, sl], ALU.mult, [("ps", bb_), "rsn"], [("t1", 1)])
                TT(S, qr[q][:, sl], t1[0][:, sl], t1[1][:, sl], ALU.add, [("t1", 0), ("t1", 1)], [("qr", q)])
            DMA(S, "sp", k.QR[hp, ho:ho + 64, t0:t0 + n], qr[q][:, 0:n], f"qr{q}", reads=[("qr", q)])


def mla_phase(k, qtiles):
    S, A, ps = k.S, k.A, k.ps
    A.reset(k.static_end)
    ckvT = A.t("ckvT", [128, NT], BF16)
    krT = A.t("krT", [128, NT], BF16)
    ctok = A.t("ctok", [128, 34, 128], BF16)
    wuv = A.t("wuv", [128, 1024], BF16)
    wo = A.t("wo", [128, 8, 8, 128], BF16)
    qp = [A.t(f"qp{i}", [128, 8, 512], BF16) for i in range(2)]
    qr = [A.t(f"qr{i}", [128, 4, 512], BF16) for i in range(2)]
    xb = [A.t(f"xb{i}", [128, 8, 512], F32) for i in range(2)]
    oT = A.t("oT", [128, 8, 512], BF16)
    st_ = attn_bufs(A, 512)
    DMA(S, "sp", ckvT[:], k.CKVT, "ckvT", writes=["ckvT"])
    DMA(S, "sp", krT[:], k.KRT, "krT", writes=["krT"])
    DMA(S, "sp", ctok[:], k.CKVTOK.rearrange("c p r -> p c r"), "ctok", writes=["ctok"])
    DMA(S, "sp", wuv[:], k.WUV, "wuv", writes=["wuv"])
    DMA(S, "sp", wo[:], k.WO.rearrange("j p k m -> p j k m"), "wo", writes=["wo"])
    chunks = list(range(34))
    scale = float(192 ** -0.5)
    for qi, q0 in enumerate(qtiles):
        b = qi % 2
        DMA(S, "sp", qp[b][:], k.QP[:, :, q0:q0 + 512].rearrange("h p t -> p h t"), f"qp{b}", writes=[("qp", b)])
        DMA(S, "sp", qr[b][:], k.QR[:, :, q0:q0 + 512].rearrange("h p t -> p h t"), f"qr{b}", writes=[("qr", b)])
        DMA(S, "sp", xb[b][:], k.X["X4"][:, q0:q0 + 512].rearrange("(j p) t -> p j t", p=128), f"xb{b}",
            writes=[("xb", b, j) for j in range(8)])
        def tail(h, on_ap, on_key):
            MM(S, ps[:, 7, :], wuv[:, h * 128:(h + 1) * 128], on_ap, True, True, ["wuv", on_key], [("ps", 7)])
            copy_alt(k, oT[:, h, :], ps[:, 7, :], [("ps", 7)], [("oT", h)])

        def lin_j(j, b=b):
            for kk in range(8):
                MM(S, ps[:, 7, :], wo[:, j, kk, :], oT[:, kk, :], kk == 0, kk == 7, ["wo", ("oT", kk)], [("ps", 7)])
            STT(S, xb[b][:, j, :], ps[:, 7, :], k.Gh[:, 1, 1, 2 * j:2 * j + 1], xb[b][:, j, :], ALU.mult, ALU.add,
                [("ps", 7), ("xb", b, j)], [("xb", b, j)])

        def store(b=b, q0=q0):
            DMA(S, "sp", k.X["X5"][:, q0:q0 + 512].rearrange("(j p) t -> p j t", p=128), xb[b][:], f"xs{b}",
                reads=[("xb", b, j) for j in range(8)])

        pend = st_.setdefault("pending", [])
        for h in range(8):
            hp, ho = h // 2, (h % 2) * 64
            on_ap, on_key = attention(
                k, st_, 512, [(qp[b][:, h, :], ("qp", b)), (qr[b][ho:ho + 64, hp, :], ("qr", b))],
                lambda kc, ho=ho: [(ckvT[:, kc * 128:(kc + 1) * 128], "ckvT"), (krT[ho:ho + 64, kc * 128:(kc + 1) * 128], "krT")],
                lambda kc: (ctok[:, kc, :], "ctok"), 128, chunks, scale, None)
            pend.append(lambda h=h, a=on_ap, kk_=on_key, f=tail: f(h, a, kk_))
        for j in range(8):
            pend.append(lambda j=j, f=lin_j: f(j))
        pend.append(store)
    while st_.get("pending"):
        st_["pending"].pop(0)()


def fm(v):
    v = np.asarray(v, np.float32)
    lead = v.shape[:-1]
    return np.ascontiguousarray(np.moveaxis(v.reshape(lead + (8, 128)), -1, 0))


_CONST = {}


def consts():
    if _CONST:
        return _CONST
    c = {}
    c["ident"] = np.eye(128, dtype=np.float32)
    c["kvec"] = np.ascontiguousarray(np.broadcast_to(np.arange(4096, dtype=np.float32) / np.float32(4096.0), (128, 4096)))
    c["tcol"] = (np.arange(32)[None, :] * 128 + np.arange(128)[:, None]).astype(np.float32)
    cc = np.arange(128, dtype=np.float64)
    ang = 2 * np.pi * np.outer(cc, cc) / 128.0
    cs = np.concatenate([-np.cos(ang), np.sin(ang)], 1)
    c["csl"] = (cs / np.sqrt(4096.0 * 128.0)).astype(np.float32)
    c["csc"] = (cs / np.sqrt(256.0 * 128.0)).astype(np.float32)
    pos = np.arange(NL)
    row = (pos // 64).astype(np.float32)
    col = (pos % 64).astype(np.float32)
    inv = (np.float32(10000.0) ** (-np.arange(0, 32, 2, dtype=np.float32) / np.float32(32))).astype(np.float32)
    ang_r = row[:, None] * inv[None]
    ang_c = col[:, None] * inv[None]
    ang = np.concatenate([ang_r, ang_r, ang_c, ang_c], -1)
    sign = np.ones(64, np.float32)
    sign[0:16] = -1
    sign[32:48] = -1
    c["ropec"] = np.ascontiguousarray(np.cos(ang).T.astype(np.float32))
    c["ropes"] = np.ascontiguousarray((np.sin(ang) * sign[None]).T.astype(np.float32))
    _CONST.update(c)
    return c


ROPE_PERM = np.concatenate([np.arange(16, 32), np.arange(0, 16), np.arange(48, 64), np.arange(32, 48)])


def na_bias(rpb):
    out = np.full((3, 8, 8, 128, 512), -30000.0, np.float32)
    p = np.arange(128)
    qq = np.arange(512)
    for pat, i in enumerate((0, 1, 7)):
        krow0 = int(np.clip(8 * i - 4, 0, 48))
        qrow = 8 * i + qq // 64
        qcol = qq % 64
        r0 = np.clip(qrow - 4, 0, 56)
        cs = np.clip(qcol - 8, 0, 48)
        for jj in range(8):
            krow = krow0 + 2 * jj + p // 64
            kcol = p % 64
            valid = ((krow[:, None] >= r0[None]) & (krow[:, None] < r0[None] + 8)
                     & (kcol[:, None] >= cs[None]) & (kcol[:, None] < cs[None] + 16))
            ri = np.clip(krow[:, None] - qrow[None] + 7, 0, 14)
            ci = np.clip(kcol[:, None] - qcol[None] + 15, 0, 30)
            for h in range(8):
                g = rpb[h][ri, ci]
                out[pat, h, jj] = np.where(valid, g, np.float32(-30000.0))
    return out


def prep_inputs(inp, b):
    x0 = np.concatenate([inp["x"][b].T, inp["ctx"][b].T], axis=1)
    sc = np.stack([inp["c"][b], inp["c_ctx"]], -1).reshape(8, 128, 2).transpose(1, 0, 2)
    adb = fm(inp["ada_b"].reshape(2, 9, D))
    adb = np.repeat(adb[..., None], 2, -1).reshape(128, 2, 9, 16)
    ng = fm(inp["norm_g"])
    ng = np.repeat(ng[..., None], 2, -1).reshape(128, 2, 3, 16)
    w_in = inp["mla_w_in"][0]
    w_in_ext = np.concatenate([w_in, w_in[:, 512:576][:, ROPE_PERM]], 1)
    wq = inp["mla_w_uq"][0].reshape(384, 8, 192)
    wq_ext = np.concatenate([wq[:, :, :128].reshape(384, 1024), wq[:, :, 128:].reshape(384, 512),
                             wq[:, :, 128:][:, :, ROPE_PERM].reshape(384, 512)], 1)
    wukT = inp["mla_w_uk"][0].reshape(128, 8, 128).transpose(1, 2, 0)
    m = dict(
        x0=np.ascontiguousarray(x0, np.float32), sc=np.ascontiguousarray(sc, np.float32),
        ada_w=inp["ada_w"], adb=np.ascontiguousarray(adb), ng=np.ascontiguousarray(ng),
        fg=fm(inp["final_g"]),
        ffn_w_gate=inp["ffn_w_gate"], ffn_w_up=inp["ffn_w_up"], ffn_w_down=inp["ffn_w_down"],
        ab_w_in=inp["ab_w_in"][0], ab_w_out=inp["ab_w_out"][0],
        mla_w_in=np.ascontiguousarray(w_in_ext), mla_w_uq=np.ascontiguousarray(wq_ext),
        mla_w_ukT=np.ascontiguousarray(wukT), mla_w_uv=inp["mla_w_uv"][0], mla_w_o=inp["mla_w_o"][0],
        gq=np.ascontiguousarray(inp["mla_g_q"][0].reshape(3, 128).T), gkv=np.ascontiguousarray(inp["mla_g_kv"][0].reshape(1, 128).T),
    )
    m.update(consts())
    return m


def kernel(**inputs):
    inp = {k_: np.asarray(v) for k_, v in inputs.items()}
    nc = build()
    nab = na_bias(inp["ab_rpb"][0])
    in_maps = [prep_inputs(inp, b) for b in range(8)]
    for m in in_maps:
        m["nabias"] = nab
    res = run_bass_kernel_spmd(nc, in_maps, core_ids=list(range(8)))
    out = np.stack([np.ascontiguousarray(r["outT"].T) for r in res.results], 0)
    return out.astype(np.float32)
```
